# Optimizing a Trainium2 kernel written in Bass

```python
import jax, jax.numpy as jnp
from jax import lax
import numpy as np

D_MODEL = 1024
BATCH = 8
SEQ = 4096
DEPTH = 4

D_FF = 2816
FFN_RESID = 0.5
EPS = 1e-6
GM_WIDTH = 512
GM_GROUPS = 4
GM_GROUP_DIM = GM_WIDTH // GM_GROUPS
GM_CHUNK = 128
MLA_HEADS = 8
MLA_Q_RANK = 384
MLA_KV_RANK = 256
MLA_NOPE = 64
MLA_ROPE = 32
MLA_QK_DIM = MLA_NOPE + MLA_ROPE
MLA_V = 64
MLA_WIDTH = MLA_HEADS * MLA_V
ROPE_THETA = 10000.0
Q_BLOCK = 128
SSD_HEADS = 8
SSD_HEAD_DIM = 64
SSD_INNER = SSD_HEADS * SSD_HEAD_DIM
SSD_GROUPS = 2
SSD_STATE = 128
SSD_CONV = 4
SSD_CHUNK = 128
SSD_CONV_DIM = SSD_INNER + 2 * SSD_GROUPS * SSD_STATE
N_BRANCH = 3
BRANCH_WIDTH = 512
IN_WIDTHS = (2 * GM_WIDTH, MLA_Q_RANK, MLA_KV_RANK, MLA_ROPE, SSD_INNER, SSD_CONV_DIM, SSD_HEADS, N_BRANCH * D_MODEL)
IN_COLS = sum(IN_WIDTHS)
IN_OFFSETS = tuple(int(v) for v in np.cumsum(IN_WIDTHS)[:-1])

kernel_name = 'hybrid_gmlp_mla_ssd_macaron'


def rms_norm(x, gain):
    xf = x.astype(jnp.float32)
    y = xf * lax.rsqrt(jnp.mean(xf * xf, axis=-1, keepdims=True) + EPS)
    return (y * gain.astype(jnp.float32)).astype(x.dtype)


def swiglu_ffn(h, w_in, w_out):
    gate, up = jnp.split(h @ w_in, 2, axis=-1)
    return (jax.nn.silu(gate) * up) @ w_out


def apply_rope(x, cos, sin):
    x1, x2 = jnp.split(x, 2, axis=-1)
    return jnp.concatenate([x1 * cos - x2 * sin, x2 * cos + x1 * sin], axis=-1)


def gmlp_mixer(uv, v_gain, w_s, b_s):
    bsz, s, _ = uv.shape
    u, v = jnp.split(jax.nn.gelu(uv, approximate=False), 2, axis=-1)
    v = rms_norm(v, v_gain).reshape(bsz, s // GM_CHUNK, GM_CHUNK, GM_GROUPS, GM_GROUP_DIM)
    causal = jnp.tril(jnp.ones((GM_CHUNK, GM_CHUNK), dtype=bool))
    w = jnp.where(causal[None], w_s, 0.0).astype(v.dtype)
    sp = jnp.einsum('gts,bcsgd->bctgd', w, v) + b_s.T[:, :, None].astype(v.dtype)
    return u * sp.reshape(bsz, s, GM_WIDTH)


def blocked_causal_attention(q, k, v):
    bsz, s, nh, dqk = q.shape
    dv = v.shape[-1]
    nb = s // Q_BLOCK
    scale = dqk ** -0.5
    qb = q.reshape(bsz, nb, Q_BLOCK, nh, dqk).transpose(1, 0, 3, 2, 4)
    kt = k.transpose(0, 2, 1, 3)
    vt = v.transpose(0, 2, 1, 3)
    kpos = jnp.arange(s)

    def one_block(args):
        qi, i = args
        sc = jnp.einsum('bhqd,bhkd->bhqk', qi, kt, preferred_element_type=jnp.float32) * scale
        qpos = i * Q_BLOCK + jnp.arange(Q_BLOCK)
        sc = jnp.where(kpos[None, :] <= qpos[:, None], sc, -jnp.inf)
        p = jax.nn.softmax(sc, axis=-1).astype(vt.dtype)
        return jnp.einsum('bhqk,bhkd->bhqd', p, vt)

    out = lax.map(one_block, (qb, jnp.arange(nb)))
    return out.transpose(1, 0, 3, 2, 4).reshape(bsz, s, nh * dv)


def mla_mixer(c_q, c_kv, k_rope, cos, sin, q_norm, kv_norm, w_uq, w_ukv, q_gain, k_gain):
    bsz, s, _ = c_q.shape
    q = (rms_norm(c_q, q_norm) @ w_uq).reshape(bsz, s, MLA_HEADS, MLA_QK_DIM)
    kv = (rms_norm(c_kv, kv_norm) @ w_ukv).reshape(bsz, s, MLA_HEADS, MLA_NOPE + MLA_V)
    k_nope, v = jnp.split(kv, [MLA_NOPE], axis=-1)
    k_pe = jnp.broadcast_to(k_rope[:, :, None, :], (bsz, s, MLA_HEADS, MLA_ROPE))
    k = jnp.concatenate([k_nope, k_pe], axis=-1)
    q = rms_norm(q, q_gain)
    k = rms_norm(k, k_gain)
    q = jnp.concatenate([q[..., :MLA_NOPE], apply_rope(q[..., MLA_NOPE:], cos, sin)], axis=-1)
    k = jnp.concatenate([k[..., :MLA_NOPE], apply_rope(k[..., MLA_NOPE:], cos, sin)], axis=-1)
    return blocked_causal_attention(q, k, v)


def segsum(a):
    t = a.shape[-1]
    idx = jnp.arange(t)
    ax = jnp.where(idx[:, None] > idx[None, :], a[..., :, None], 0.0)
    ss = jnp.cumsum(ax, axis=-2)
    return jnp.where(idx[:, None] >= idx[None, :], ss, -jnp.inf)


def ssd_scan(xs, dt, a_log, b_in, c_in):
    bsz, s, nh, hp = xs.shape
    ng, ns = b_in.shape[2], b_in.shape[3]
    nr = nh // ng
    nc = s // SSD_CHUNK
    a = -jnp.exp(a_log.astype(jnp.float32))
    da = (dt * a).reshape(bsz, nc, SSD_CHUNK, ng, nr).transpose(0, 3, 4, 1, 2)
    xdt = (xs * dt[..., None].astype(xs.dtype)).reshape(bsz, nc, SSD_CHUNK, ng, nr, hp)
    bc = b_in.reshape(bsz, nc, SSD_CHUNK, ng, ns)
    cc = c_in.reshape(bsz, nc, SSD_CHUNK, ng, ns)
    cs = jnp.cumsum(da, axis=-1)
    dt_ = xs.dtype
    lmat = jnp.exp(segsum(da)).astype(dt_)
    cb = jnp.einsum('bclgn,bcsgn->bgcls', cc, bc)
    y_diag = jnp.einsum('bgrcls,bcsgrp->bclgrp', cb[:, :, None] * lmat, xdt)
    decay_states = jnp.exp(cs[..., -1:] - cs).astype(dt_)
    states = jnp.einsum('bclgn,bgrcl,bclgrp->bcgrpn', bc, decay_states, xdt)
    chunk_tot = jnp.pad(cs[..., -1], ((0, 0), (0, 0), (0, 0), (1, 0)))
    decay_chunk = jnp.exp(segsum(chunk_tot)).astype(dt_)
    states0 = jnp.concatenate([jnp.zeros_like(states[:, :1]), states], axis=1)
    new_states = jnp.einsum('bgrzc,bcgrpn->bzgrpn', decay_chunk, states0)
    states_in = new_states[:, :-1]
    y_off = jnp.einsum('bclgn,bcgrpn,bgrcl->bclgrp', cc, states_in, jnp.exp(cs).astype(dt_))
    return (y_diag + y_off).reshape(bsz, s, nh, hp)


def ssd_mixer(z, xbc, dt_raw, conv_w, conv_b, dt_bias, a_log, d_skip, norm_gain):
    bsz, s, _ = xbc.shape
    xbc = lax.conv_general_dilated(xbc, conv_w[:, None, :].astype(xbc.dtype), (1,), [(SSD_CONV - 1, 0)],
                                   dimension_numbers=('NWC', 'WIO', 'NWC'), feature_group_count=SSD_CONV_DIM)
    xbc = jax.nn.silu(xbc + conv_b.astype(xbc.dtype))
    xs, b_in, c_in = jnp.split(xbc, [SSD_INNER, SSD_INNER + SSD_GROUPS * SSD_STATE], axis=-1)
    xs = xs.reshape(bsz, s, SSD_HEADS, SSD_HEAD_DIM)
    b_in = b_in.reshape(bsz, s, SSD_GROUPS, SSD_STATE)
    c_in = c_in.reshape(bsz, s, SSD_GROUPS, SSD_STATE)
    dt = jax.nn.softplus(dt_raw.astype(jnp.float32) + dt_bias.astype(jnp.float32))
    y = ssd_scan(xs, dt, a_log, b_in, c_in) + xs * d_skip[:, None].astype(xs.dtype)
    y = y.reshape(bsz, s, SSD_INNER) * jax.nn.silu(z)
    y = rms_norm(y.reshape(bsz, s, SSD_GROUPS, SSD_INNER // SSD_GROUPS), norm_gain.reshape(SSD_GROUPS, -1))
    return y.reshape(bsz, s, SSD_INNER)


def setup_inputs(seed: int = 0) -> dict:
    key = jax.random.key(seed)
    ks = jax.random.split(key, 32)
    f32 = jnp.float32

    def nrm(k, shape, scale):
        return jax.random.normal(k, shape, f32) * scale

    def gain(k, shape):
        return 1.0 + 0.05 * jax.random.normal(k, shape, f32)

    L = DEPTH
    x = jax.random.normal(ks[0], (BATCH, SEQ, D_MODEL), f32)
    offsets = jax.random.randint(ks[1], (BATCH, 1), 0, SEQ, dtype=jnp.int32)
    positions = offsets + jnp.arange(SEQ, dtype=jnp.int32)[None, :]
    dt0 = jnp.exp(jax.random.uniform(ks[2], (L, SSD_HEADS), f32, np.log(1e-3), np.log(1e-1)))
    return {
        'x': x,
        'positions': positions,
        'ffn1_norm': gain(ks[3], (L, D_MODEL)),
        'ffn1_w_in': nrm(ks[4], (L, D_MODEL, 2 * D_FF), D_MODEL ** -0.5),
        'ffn1_w_out': nrm(ks[5], (L, D_FF, D_MODEL), D_FF ** -0.5),
        'mix_norm': gain(ks[6], (L, D_MODEL)),
        'w_in': nrm(ks[7], (L, D_MODEL, IN_COLS), D_MODEL ** -0.5),
        'gm_v_norm': gain(ks[8], (L, GM_WIDTH)),
        'gm_w_s': nrm(ks[9], (L, GM_GROUPS, GM_CHUNK, GM_CHUNK), 0.5 * GM_CHUNK ** -0.5),
        'gm_b_s': 1.0 + 0.1 * jax.random.normal(ks[10], (L, GM_GROUPS, GM_CHUNK), f32),
        'mla_q_norm': gain(ks[11], (L, MLA_Q_RANK)),
        'mla_kv_norm': gain(ks[12], (L, MLA_KV_RANK)),
        'mla_w_uq': nrm(ks[13], (L, MLA_Q_RANK, MLA_HEADS * MLA_QK_DIM), MLA_Q_RANK ** -0.5),
        'mla_w_ukv': nrm(ks[14], (L, MLA_KV_RANK, MLA_HEADS * (MLA_NOPE + MLA_V)), MLA_KV_RANK ** -0.5),
        'mla_q_gain': gain(ks[15], (L, MLA_QK_DIM)),
        'mla_k_gain': gain(ks[16], (L, MLA_QK_DIM)),
        'ssd_conv_w': nrm(ks[17], (L, SSD_CONV, SSD_CONV_DIM), SSD_CONV ** -0.5),
        'ssd_conv_b': nrm(ks[18], (L, SSD_CONV_DIM), 0.02),
        'ssd_dt_bias': dt0 + jnp.log(-jnp.expm1(-dt0)),
        'ssd_a_log': jnp.log(jax.random.uniform(ks[19], (L, SSD_HEADS), f32, 1.0, 16.0)),
        'ssd_d': 1.0 + 0.1 * jax.random.normal(ks[20], (L, SSD_HEADS), f32),
        'ssd_norm': gain(ks[21], (L, SSD_INNER)),
        'w_branch': nrm(ks[22], (L, N_BRANCH, BRANCH_WIDTH, D_MODEL), BRANCH_WIDTH ** -0.5),
        'w_out': nrm(ks[23], (L, D_MODEL, D_MODEL), D_MODEL ** -0.5),
        'ffn2_norm': gain(ks[24], (L, D_MODEL)),
        'ffn2_w_in': nrm(ks[25], (L, D_MODEL, 2 * D_FF), D_MODEL ** -0.5),
        'ffn2_w_out': nrm(ks[26], (L, D_FF, D_MODEL), D_FF ** -0.5),
    }


def reference(x, positions, ffn1_norm, ffn1_w_in, ffn1_w_out, mix_norm, w_in, gm_v_norm, gm_w_s, gm_b_s,
              mla_q_norm, mla_kv_norm, mla_w_uq, mla_w_ukv, mla_q_gain, mla_k_gain,
              ssd_conv_w, ssd_conv_b, ssd_dt_bias, ssd_a_log, ssd_d, ssd_norm,
              w_branch, w_out, ffn2_norm, ffn2_w_in, ffn2_w_out):
    bsz, s, _ = x.shape
    inv_freq = 1.0 / (ROPE_THETA ** (jnp.arange(0, MLA_ROPE, 2, dtype=jnp.float32) / MLA_ROPE))
    ang = positions.astype(jnp.float32)[..., None] * inv_freq
    cos = jnp.cos(ang)[:, :, None, :].astype(x.dtype)
    sin = jnp.sin(ang)[:, :, None, :].astype(x.dtype)
    for l in range(DEPTH):
        x = x + FFN_RESID * swiglu_ffn(rms_norm(x, ffn1_norm[l]), ffn1_w_in[l], ffn1_w_out[l])
        h = rms_norm(x, mix_norm[l])
        uv, c_q, c_kv, k_rope, z, xbc, dt_raw, gates = jnp.split(h @ w_in[l], IN_OFFSETS, axis=-1)
        y_a = gmlp_mixer(uv, gm_v_norm[l], gm_w_s[l], gm_b_s[l])
        y_b = mla_mixer(c_q, c_kv, k_rope, cos, sin, mla_q_norm[l], mla_kv_norm[l], mla_w_uq[l], mla_w_ukv[l],
                        mla_q_gain[l], mla_k_gain[l])
        y_c = ssd_mixer(z, xbc, dt_raw, ssd_conv_w[l], ssd_conv_b[l], ssd_dt_bias[l], ssd_a_log[l], ssd_d[l],
                        ssd_norm[l])
        g = jax.nn.sigmoid(gates).reshape(bsz, s, N_BRANCH, D_MODEL)
        merged = (g[:, :, 0] * (y_a @ w_branch[l, 0])
                  + g[:, :, 1] * (y_b @ w_branch[l, 1])
                  + g[:, :, 2] * (y_c @ w_branch[l, 2]))
        x = x + merged @ w_out[l]
        x = x + FFN_RESID * swiglu_ffn(rms_norm(x, ffn2_norm[l]), ffn2_w_in[l], ffn2_w_out[l])
    return x
```

```python
import contextlib
import numpy as np
import concourse.bass as bass
import concourse.mybir as mybir
from concourse.bass_utils import run_bass_kernel_spmd

F32 = mybir.dt.float32
BF16 = mybir.dt.bfloat16
I32 = mybir.dt.int32
AF = mybir.ActivationFunctionType
ALU = mybir.AluOpType
AX = mybir.AxisListType

D = 1024
DFF = 2816
EPS = 1e-6
TS = 256
NC = TS // 128
NK = 8
NJ = DFF // 128
NJJ = NJ // 2
IN_COLS = 6312

ENGS = ("pe", "act", "dve", "pool", "sp")
NDMA_SEM = 12


class Op:
    __slots__ = ("eng", "fn", "dma", "idx", "deps", "signal", "sigval", "dma_i")

    def __init__(self, eng, fn, dma):
        self.eng = eng
        self.fn = fn
        self.dma = dma
        self.deps = None
        self.signal = False
        self.sigval = 0
        self.dma_i = -1


class Prog:
    def __init__(self, nc):
        self.nc = nc
        self.ops = {e: [] for e in ENGS}
        self.last_w = {}
        self.readers = {}
        self.dma_count = {e: 0 for e in ENGS}
        self.dma_ops = {e: [] for e in ENGS}

    @staticmethod
    def _stream(op):
        return op.eng + ":dma" if op.dma else op.eng

    def add(self, eng, fn, reads=(), writes=(), dma=False):
        op = Op(eng, fn, dma)
        op.idx = len(self.ops[eng])
        deps = {}
        if eng != "pe" and not dma:
            pk = tuple(k for k in reads if isinstance(k, str) and k[:2] == "ps" and k[2:].isdigit())
            if pk:
                writes = tuple(writes) + pk

        def dep(o, raw):
            if o is None:
                return
            if (not o.dma) and o.eng == eng and not dma:
                if eng == "pe" or not raw:
                    return
            s = self._stream(o)
            cur = deps.get(s)
            if cur is None or cur.idx < o.idx:
                deps[s] = o

        for k in reads:
            dep(self.last_w.get(k), True)
        for k in writes:
            dep(self.last_w.get(k), False)
            rd = self.readers.get(k)
            if rd:
                for o in rd.values():
                    dep(o, False)
        if dma:
            i = self.dma_count[eng]
            op.dma_i = i
            self.dma_count[eng] = i + 1
            self.dma_ops[eng].append(op)
            if i >= NDMA_SEM:
                o = self.dma_ops[eng][i - NDMA_SEM]
                s = self._stream(o)
                cur = deps.get(s)
                if cur is None or cur.idx < o.idx:
                    deps[s] = o
        op.deps = list(deps.values())
        for o in op.deps:
            o.signal = True
        st = self._stream(op)
        for k in reads:
            self.readers.setdefault(k, {})[st] = op
        for k in writes:
            self.last_w[k] = op
            self.readers[k] = {}
        self.ops[eng].append(op)
        return op

    def emit(self, final_ops):
        nc = self.nc
        for o in final_ops:
            o.signal = True
        with contextlib.ExitStack() as es:
            csem = {}
            for e in ("pe", "act", "dve", "pool"):
                csem[e] = es.enter_context(nc.semaphore("c_" + e))
            dsem = {}
            for e in ENGS:
                if self.dma_count[e]:
                    dsem[e] = [es.enter_context(nc.semaphore("d_%s_%d" % (e, i))) for i in range(NDMA_SEM)]
            for e in ENGS:
                c = 0
                for op in self.ops[e]:
                    if op.dma:
                        op.sigval = 16 * (op.dma_i // NDMA_SEM + 1)
                    elif op.signal:
                        c += 1
                        op.sigval = c
            block = es.enter_context(nc.Block())
            engobj = {"pe": block.tensor, "act": block.scalar, "dve": block.vector, "pool": block.gpsimd,
                      "sp": block.sync}

            def make(e):
                def body(eng):
                    waited = {}
                    for op in self.ops[e]:
                        for d in op.deps:
                            if d.dma:
                                sem = dsem[d.eng][d.dma_i % NDMA_SEM]
                                key = (d.eng, d.dma_i % NDMA_SEM)
                            else:
                                sem = csem[d.eng]
                                key = d.eng
                            if waited.get(key, 0) >= d.sigval:
                                continue
                            waited[key] = d.sigval
                            eng.wait_ge(sem, d.sigval)
                        ins = op.fn(eng)
                        if op.dma:
                            ins.then_inc(dsem[e][op.dma_i % NDMA_SEM], 16)
                        elif op.signal:
                            ins.then_inc(csem[e], 1)
                    if e == "sp":
                        for o in final_ops:
                            if o.dma:
                                eng.wait_ge(dsem[o.eng][o.dma_i % NDMA_SEM], o.sigval)
                            else:
                                eng.wait_ge(csem[o.eng], o.sigval)
                return body

            for e in ENGS:
                if self.ops[e] or e == "sp":
                    engobj[e](make(e))


NWI = 3
INV_FREQ = (1.0 / (np.float32(10000.0) ** (np.arange(0, 32, 2, dtype=np.float32) / np.float32(32)))).astype(np.float32)
PI = float(np.pi)


def psk(b):
    return "ps%d" % b


class Builder:
    def __init__(self, S, L, do_mix=True, do_ffn=True, mix_parts=(1, 1, 1)):
        self.S = S
        self.L = L
        self.NT = S // TS
        self.NCH = S // 128
        self.do_mix = do_mix
        self.do_ffn = do_ffn
        self.mix_parts = mix_parts
        self.nc = bass.Bass("TRN2", target_bir_lowering=False)
        self.P = Prog(self.nc)
        self.es = contextlib.ExitStack()
        self.bank_i = 0
        self.reserved = set()
        self.pending_conv = []
        self.stage = 8
        self.brmask = 7

    def sb(self, name, shape, dt):
        return self.es.enter_context(self.nc.sbuf_tensor(name, shape, dt))

    def dram_in(self, name, shape, dt=F32):
        return self.nc.dram_tensor(name, list(shape), dt, kind="ExternalInput")

    def bank(self):
        while self.bank_i in self.reserved:
            self.bank_i = (self.bank_i + 1) % 8
        b = self.bank_i
        self.bank_i = (self.bank_i + 1) % 8
        return b

    def pt(self, b):
        return self.ps[b][:, 0:TS]

    def psb(self, b):
        return self.ps[b][:].bitcast(BF16)

    def mm(self, b, out, lhsT, rhs, start, stop, reads, skip=False):
        if skip:
            self.P.add("pe", lambda e: e.matmul(out, lhsT, rhs, start=start, stop=stop, skip_group_check=True), reads,
                       (psk(b),))
        else:
            self.P.add("pe", lambda e: e.matmul(out, lhsT, rhs, start=start, stop=stop), reads, (psk(b),))

    def tr(self, b, out, in_, ident, reads):
        self.P.add("pe", lambda e: e.transpose(out, in_, ident), reads, (psk(b),))

    def act(self, out, in_, func, reads, writes, bias=None, scale=None):
        kw = {}
        if bias is not None:
            kw["bias"] = bias
        if scale is not None:
            kw["scale"] = scale
        self.P.add("act", lambda e: e.activation(out=out, in_=in_, func=func, **kw), reads, writes)

    def ts(self, eng, out, in0, s1, s2, op0, op1, reads, writes):
        if op1 is None:
            self.P.add(eng, lambda e: e.tensor_scalar(out=out, in0=in0, scalar1=s1, scalar2=None, op0=op0), reads, writes)
        else:
            self.P.add(eng, lambda e: e.tensor_scalar(out=out, in0=in0, scalar1=s1, scalar2=s2, op0=op0, op1=op1),
                       reads, writes)

    def rsqrt(self, out, in_, scale, reads, writes):
        self.act(out, in_, AF.Sqrt, tuple(reads) + ("eps_t",), writes, bias=self.eps_t[:out.shape[0], :], scale=scale)
        self.P.add("dve", lambda e: e.reciprocal(out=out, in_=out), writes, writes)

    def stt(self, eng, out, in0, scalar, in1, op0, op1, reads, writes):
        self.P.add(eng, lambda e: e.scalar_tensor_tensor(out=out, in0=in0, scalar=scalar, in1=in1, op0=op0, op1=op1),
                   reads, writes)

    def tt(self, eng, out, in0, in1, op, reads, writes):
        self.P.add(eng, lambda e: e.tensor_tensor(out=out, in0=in0, in1=in1, op=op), reads, writes)

    def red(self, out, in_, reads, writes):
        self.P.add("dve", lambda e: e.tensor_reduce(out=out, in_=in_, axis=AX.X, op=ALU.add), reads, writes)

    def cp(self, eng, out, in_, reads, writes):
        if eng == "act":
            self.P.add("act", lambda e: e.copy(out=out, in_=in_), reads, writes)
        else:
            self.P.add(eng, lambda e: e.tensor_copy(out=out, in_=in_), reads, writes)

    def dma(self, q, out, in_, reads, writes, slow=False):
        if slow:
            return self.P.add(q, lambda e: e.dma_start(out=out, in_=in_, allow_slow_non_contiguous=True), reads, writes,
                              dma=True)
        return self.P.add(q, lambda e: e.dma_start(out=out, in_=in_), reads, writes, dma=True)

    def declare(self):
        nc, S, L = self.nc, self.S, self.L
        self.x_in = self.dram_in("x", (S, D))
        self.pos_in = self.dram_in("positions", (S,), I32)
        names = [
            ("ffn1_norm", (L, D)), ("ffn1_w_in", (L, D, 2 * DFF)), ("ffn1_w_out", (L, DFF, D)),
            ("mix_norm", (L, D)), ("w_in", (L, D, IN_COLS)), ("gm_v_norm", (L, 512)),
            ("gm_w_s", (L, 4, 128, 128)), ("gm_b_s", (L, 4, 128)), ("mla_q_norm", (L, 384)),
            ("mla_kv_norm", (L, 256)), ("mla_w_uq", (L, 384, 768)), ("mla_w_ukv", (L, 256, 1024)),
            ("mla_q_gain", (L, 96)), ("mla_k_gain", (L, 96)), ("ssd_conv_w", (L, 4, 1024)),
            ("ssd_conv_b", (L, 1024)), ("ssd_dt_bias", (L, 8)), ("ssd_a_log", (L, 8)), ("ssd_d", (L, 8)),
            ("ssd_norm", (L, 512)), ("w_branch", (L, 3, 512, D)), ("w_out", (L, D, D)),
            ("ffn2_norm", (L, D)), ("ffn2_w_in", (L, D, 2 * DFF)), ("ffn2_w_out", (L, DFF, D)),
        ]
        self.W = {}
        for n, shp in names:
            self.W[n] = self.dram_in(n, shp)
        self.out = nc.dram_tensor("out", [S, D], F32, kind="ExternalOutput")
        self.xscr = nc.dram_tensor("xscr", [128, NK, S], F32, kind="Internal")
        self.kscr = nc.dram_tensor("kscr", [8, 96, S], BF16, kind="Internal")
        self.vscr = nc.dram_tensor("vscr", [8, 128, S // 128, 65], BF16, kind="Internal")
        self.scr_wi, self.scr_wo, self.scr_in, self.scr_br, self.scr_o, self.scr_uq, self.scr_ukv = {}, {}, {}, {}, {}, {}, {}
        for l in range(L):
            for f in (1, 2):
                self.scr_wi[(l, f)] = nc.dram_tensor("swi_%d_%d" % (l, f), [NJJ, 128, NK, 512], BF16, kind="Internal")
                self.scr_wo[(l, f)] = nc.dram_tensor("swo_%d_%d" % (l, f), [2, NJJ, 128, 2, 512], BF16, kind="Internal")
            self.scr_in[l] = nc.dram_tensor("sin_%d" % l, [13, 128, NK, 512], BF16, kind="Internal")
            self.scr_br[l] = nc.dram_tensor("sbr_%d" % l, [3, 128, 4, D], BF16, kind="Internal")
            self.scr_o[l] = nc.dram_tensor("so_%d" % l, [2, 128, NK, 512], BF16, kind="Internal")
            self.scr_uq[l] = nc.dram_tensor("suq_%d" % l, [128, 3, 768], BF16, kind="Internal")
            self.scr_ukv[l] = nc.dram_tensor("sukv_%d" % l, [128, 2, 1024], BF16, kind="Internal")

    def alloc(self):
        nc = self.nc
        sb = self.sb
        NCH = self.NCH
        self.ident_f = sb("ident_f", [128, 128], F32)
        self.ident_b = sb("ident_b", [128, 128], BF16)
        self.ones_b = sb("ones_b", [128, 128], BF16)
        self.ones_f = sb("ones_f", [128, 128], F32)
        self.U_f = sb("U_f", [128, 128], F32)
        self.U_b = sb("U_b", [128, 128], BF16)
        self.Lm_f = sb("Lm_f", [128, 128], F32)
        self.eps_t = sb("eps_t", [128, 1], F32)
        self.one_t = sb("one_t", [128, 1], F32)
        self.negpi_t = sb("negpi_t", [128, 1], F32)
        self.xTs = [sb("xT%d" % i, [128, NK, TS], F32) for i in range(2)]
        self.hT = sb("hT", [128, NK, TS], BF16)
        self.hTf = sb("hTf", [128, NK, TS], BF16)
        self.sq = [sb("sq%d" % i, [128, TS], BF16) for i in range(2)]
        self.sqf = [sb("sqf%d" % i, [128, TS], BF16) for i in range(2)]
        self.rstd = sb("rstd", [128, TS], F32)
        self.rstdf = sb("rstdf", [128, TS], F32)
        self.gains = sb("gains", [128, 2, 3, NK], F32)
        self.Rarg = sb("Rarg", [128, 16, 128], F32)
        self.xtm = self.Rarg[:].rearrange("p a b -> p (a b)").rearrange("p (c d) -> p c d", c=NC)
        self.wif = [sb("wif%d" % i, [128, NK, 512], BF16) for i in range(2)]
        self.wif_i = 0
        self.actT = sb("actT", [128, NJ, TS], BF16)
        self.sg = [sb("sg%d" % i, [128, TS], F32) for i in range(2)]
        self.wi = [sb("wi%d" % i, [128, NK, 512], BF16) for i in range(NWI)]
        self.wo = [sb("wo%d" % i, [128, 2, 512], BF16) for i in range(3)]
        self.ps = [self.es.enter_context(nc.psum_tensor("ps%d" % i, [128, 512], F32)) for i in range(8)]
        self.wi_i = self.wo_i = self.sq_i = self.sqf_i = self.sg_i = self.sig_i = self.pt_i = 0
        if not self.do_mix:
            return
        self.pos_i = sb("pos_i", [128, NCH], I32)
        self.pos_f = sb("pos_f", [128, NCH], F32)
        self.invf = sb("invf", [128, 16], F32)
        av = self.actT[:].rearrange("p j t -> p (j t)").bitcast(F32)
        n16 = NCH * 16
        self.ang = av[:, 0:n16].rearrange("p (c j) -> p c j", j=16)
        self.angf = av[:, n16:2 * n16].rearrange("p (c j) -> p c j", j=16)
        self.angi = av[:, 2 * n16:3 * n16].bitcast(I32).rearrange("p (c j) -> p c j", j=16)
        self.cosT = sb("cosT", [128, NCH, 16], F32)
        self.sinT = sb("sinT", [128, NCH, 16], F32)
        self.gvb = sb("gvb", [128, 512], F32)
        self.bsb = sb("bsb", [128, 4, 128], F32)
        self.Wsraw = sb("Wsraw", [128, 4, 128], F32)
        self.Wsm = sb("Wsm", [128, 4, 128], BF16)
        self.WsT = sb("WsT", [128, 4, 128], BF16)
        self.gqn = sb("gqn", [128, 3], F32)
        self.gkvn = sb("gkvn", [128, 2], F32)
        self.gqb = sb("gqb", [128, 96], F32)
        self.gkb = sb("gkb", [128, 96], F32)
        self.gmx = sb("gmx", [128, 2], F32)
        self.negC = sb("negC", [128, 1], F32)
        self.cw = sb("cw", [128, NK, 4], F32)
        self.cb = sb("cb", [128, NK], F32)
        self.dtb = sb("dtb", [128, 8], F32)
        self.aneg = sb("aneg", [128, 8], F32)
        self.Dsk = sb("Dsk", [128, 8], F32)
        self.gsn = sb("gsn", [128, 512], F32)
        self.uT = sb("uT", [128, 4, TS], BF16)
        self.vg = sb("vg", [128, 512], F32)
        self.junk = sb("junk", [128, 768], F32)
        self.s1 = sb("s1", [128, 8], F32)
        self.s2 = sb("s2", [128, 8], F32)
        self.vtm = sb("vtm", [128, NC, 512], BF16)
        self.zs = sb("zs", [128, NC, 512], BF16)
        self.cqnT = sb("cqnT", [128, 3, TS], BF16)
        self.ckvnT = sb("ckvnT", [128, 2, TS], BF16)
        self.sqq = sb("sqq", [128, 5, TS], BF16)
        self.krdt = sb("krdt", [128, NC, 40], F32)
        self.xbcp = [sb("xbcp%d" % i, [128, TS + 3], F32) for i in range(2)]
        self.hist = sb("hist", [128, NK, 3], F32)
        self.cacc = self.vg[:, 0:TS]
        self.xbcT = sb("xbcT", [128, NK, TS], BF16)
        self.spb = sb("spb", [128, 4, 128], F32)
        self.yaT = sb("yaT", [128, 4, TS], BF16)
        self.dt8 = sb("dt8", [128, 8], F32)
        self.da8 = sb("da8", [128, 8], F32)
        self.t8a = sb("t8a", [128, 8], F32)
        self.t8b = sb("t8b", [128, 8], F32)
        self.cs8 = sb("cs8", [128, 8], F32)
        self.tot8 = sb("tot8", [128, 8], F32)
        self.dout8 = sb("dout8", [128, 8], F32)
        self.etot8 = sb("etot8", [128, 8], F32)
        self.R = self.Rarg[:, 0:8, :]
        self.arg = self.Rarg[:, 8:16, :]
        self.ecs = sb("ecs", [128, 8, 128], F32)
        self.cbm = sb("cbm", [128, 2, 128], F32)
        self.MT = sb("MT", [128, 8, 128], BF16)
        self.Cs = sb("Cs", [128, 8, 128], BF16)
        self.xdt = sb("xdt", [128, 8, 64], BF16)
        self.xdd = sb("xdd", [128, 8, 64], BF16)
        self.xsd = sb("xsd", [128, 8, 64], F32)
        self.Btm = sb("Btm", [128, 2, 128], BF16)
        self.Sst = sb("Sst", [128, 8, 64], F32)
        self.Sbf = sb("Sbf", [128, 8, 64], BF16)
        self.y1 = self.vg
        self.yctm = sb("yctm", [128, 512], BF16)
        self.ycT = sb("ycT", [128, 4, TS], BF16)
        self.rq2 = sb("rq2", [128, 2], F32)
        self.qtm = sb("qtm", [128, 8, 96], F32)
        self.hss = sb("hss", [128, 8], F32)
        self.r1 = sb("r1", [128, 8, 16], F32)
        self.r2 = sb("r2", [128, 8, 16], F32)
        self.qr = sb("qr", [128, 8, 96], BF16)
        self.qT = sb("qT", [128, 8, TS], BF16)
        self.kst = sb("kst", [128, 8, 128], BF16)
        self.vst = sb("vst", [128, 8, 65], BF16)
        self.kh = [sb("kh%d" % i, [128, self.S], BF16) for i in range(2)]
        self.vh = [sb("vh%d" % i, [128, NCH, 65], BF16) for i in range(1)]
        self.kv_i = 0
        self.pT = [sb("pT%d" % i, [128, TS], BF16) for i in range(2)]
        self.rden = sb("rden", [128, 4], F32)
        self.ybtm = sb("ybtm", [128, NC, 512], BF16)
        self.ybT = sb("ybT", [128, 4, TS], BF16)
        self.sig = [sb("sig%d" % i, [128, TS], F32) for i in range(2)]
        self.mgacc = sb("mgacc", [128, 4, TS], F32)
        self.mgT = sb("mgT", [128, NK, TS], BF16)

    def consts(self):
        P = self.P
        idf, idb, ones_b, ones_f, U_f, U_b, Lm_f = self.ident_f, self.ident_b, self.ones_b, self.ones_f, self.U_f, self.U_b, self.Lm_f
        P.add("pool", lambda e: e.memset(idf[:], 0.0), (), ("ident_f",))
        P.add("pool", lambda e: e.affine_select(out=idf[:], in_=idf[:], pattern=[[-1, 128]], compare_op=ALU.not_equal,
                                                fill=1.0, base=0, channel_multiplier=1), ("ident_f",), ("ident_f",))
        P.add("pool", lambda e: e.tensor_copy(out=idb[:], in_=idf[:]), ("ident_f",), ("ident_b",))
        P.add("pool", lambda e: e.memset(ones_b[:], 1.0), (), ("ones_b",))
        P.add("pool", lambda e: e.memset(ones_f[:], 1.0), (), ("ones_f",))
        P.add("pool", lambda e: e.memset(U_f[:], 1.0), (), ("U_f",))
        P.add("pool", lambda e: e.affine_select(out=U_f[:], in_=U_f[:], pattern=[[1, 128]], compare_op=ALU.is_ge,
                                                fill=0.0, base=0, channel_multiplier=-1), ("U_f",), ("U_f",))
        P.add("pool", lambda e: e.tensor_copy(out=U_b[:], in_=U_f[:]), ("U_f",), ("U_b",))
        P.add("pool", lambda e: e.memset(Lm_f[:], 1.0), (), ("Lm_f",))
        P.add("pool", lambda e: e.affine_select(out=Lm_f[:], in_=Lm_f[:], pattern=[[-1, 128]], compare_op=ALU.is_ge,
                                                fill=0.0, base=0, channel_multiplier=1), ("Lm_f",), ("Lm_f",))
        eps_t, one_t, negpi_t = self.eps_t, self.one_t, self.negpi_t
        P.add("pool", lambda e: e.memset(eps_t[:], EPS), (), ("eps_t",))
        P.add("pool", lambda e: e.memset(one_t[:], 1.0), (), ("one_t",))
        P.add("pool", lambda e: e.memset(negpi_t[:], -PI), (), ("negpi_t",))
        if not self.do_mix:
            return
        invf = self.invf
        for j in range(16):
            P.add("pool", lambda e, j=j: e.memset(invf[:, j:j + 1], float(INV_FREQ[j])), (), ("invf",))
        NCH = self.NCH
        self.dma("sp", self.pos_i[:], self.pos_in.ap().rearrange("(c p) -> p c", p=128), (), ("pos_i",), slow=True)
        self.cp("dve", self.pos_f[:], self.pos_i[:], ("pos_i",), ("pos_f",))
        self.tt("dve", self.ang, self.pos_f[:, :, None].broadcast_to([128, NCH, 16]),
                self.invf[:, None, :].broadcast_to([128, NCH, 16]), ALU.mult, ("pos_f", "invf"), ("ang",))
        for dst, dk, shift in ((self.sinT, "sinT", 0.0), (self.cosT, "cosT", 0.5 * PI)):
            self.ts("dve", dst[:], self.ang, shift, 1.0 / (2 * PI), ALU.add, ALU.mult, ("ang",), (dk,))
            self.cp("dve", self.angi, dst[:], (dk,), ("angi",))
            self.cp("dve", self.angf, self.angi, ("angi",), ("angf",))
            self.ts("dve", dst[:], self.ang, shift, None, ALU.add, None, ("ang",), (dk,))
            self.stt("dve", dst[:], self.angf, -2 * PI, dst[:], ALU.mult, ALU.add, ("angf", dk), (dk,))
            self.ts("dve", self.angf, dst[:], PI, 2 * PI, ALU.is_gt, ALU.mult, (dk,), ("angf",))
            self.tt("dve", dst[:], dst[:], self.angf, ALU.subtract, (dk, "angf"), (dk,))
            self.ts("dve", self.angf, dst[:], -PI, 2 * PI, ALU.is_lt, ALU.mult, (dk,), ("angf",))
            self.tt("dve", dst[:], dst[:], self.angf, ALU.add, (dk, "angf"), (dk,))
            self.act(dst[:], dst[:], AF.Sin, (dk,), (dk,))

    def conv_list(self, l):
        lst = []

        def add(out, in_, key):
            lst.append(lambda: self.dma("pool", out, in_, (), (key,)))

        if self.do_ffn:
            for f in (1, 2):
                w_in = self.W["ffn%d_w_in" % f][l].rearrange("(k p) c -> p k c", p=128)
                w_out = self.W["ffn%d_w_out" % f][l].rearrange("(jj j2 p) c -> jj p j2 c", j2=2, p=128)
                swi, swo = self.scr_wi[(l, f)], self.scr_wo[(l, f)]
                for jj in range(NJJ):
                    for h in range(2):
                        add(swi[jj][:, :, h * 256:(h + 1) * 256],
                            w_in[:, :, h * DFF + jj * 256: h * DFF + (jj + 1) * 256], ("swi", l, f, jj, h))
                for mg in range(2):
                    for jj in range(NJJ):
                        add(swo[mg, jj], w_out[jj][:, :, mg * 512:(mg + 1) * 512], ("swo", l, f, mg, jj))
        if self.do_mix:
            w = self.W["w_in"][l].rearrange("(k p) c -> p k c", p=128)
            sin = self.scr_in[l]
            srcs = {0: (0, 512), 1: (512, 1024), 2: (1024, 1536), 4: (1696, 2208), 5: (2208, 2720), 6: (2720, 3232)}
            for i in range(6):
                srcs[7 + i] = (3240 + 512 * i, 3240 + 512 * (i + 1))
            for bi, (a, b_) in srcs.items():
                add(sin[bi], w[:, :, a:b_], ("sin", l, bi))
            add(sin[3][:, :, 0:160], w[:, :, 1536:1696], ("sin", l, 3, 0))
            add(sin[3][:, :, 160:168], w[:, :, 3232:3240], ("sin", l, 3, 1))
            for i in range(3):
                add(self.scr_br[l][i], self.W["w_branch"][l, i].rearrange("(kk p) c -> p kk c", p=128), ("sbr", l, i))
            wo_ = self.W["w_out"][l].rearrange("(k p) c -> p k c", p=128)
            for hf in range(2):
                add(self.scr_o[l][hf], wo_[:, :, hf * 512:(hf + 1) * 512], ("so", l, hf))
            add(self.scr_uq[l][:], self.W["mla_w_uq"][l].rearrange("(i p) c -> p i c", p=128), ("suq", l))
            add(self.scr_ukv[l][:], self.W["mla_w_ukv"][l].rearrange("(i p) c -> p i c", p=128), ("sukv", l))
        return lst

    def load_gains(self, l):
        W = self.W
        for i, n in enumerate(("ffn1_norm", "mix_norm", "ffn2_norm")):
            self.dma("sp", self.gains[:, l % 2, i, :], W[n][l].rearrange("(k p) -> p k", p=128), (), (("gains", l % 2),),
                     slow=True)

    def load_params(self, l):
        W = self.W

        def bc(ap, shape):
            return ap.broadcast_to(shape)

        self.dma("sp", self.gvb[:], bc(W["gm_v_norm"][l:l + 1, :], [128, 512]), (), ("gvb",), slow=True)
        self.dma("sp", self.bsb[:], bc(W["gm_b_s"][l:l + 1], [128, 4, 128]), (), ("bsb",), slow=True)
        self.dma("sp", self.Wsraw[:], W["gm_w_s"][l].rearrange("g t s -> t g s"), (), ("Wsraw",))
        self.dma("sp", self.gqn[:], W["mla_q_norm"][l].rearrange("(i p) -> p i", p=128), (), ("gqn",), slow=True)
        self.dma("sp", self.gkvn[:], W["mla_kv_norm"][l].rearrange("(i p) -> p i", p=128), (), ("gkvn",), slow=True)
        self.dma("sp", self.gqb[:], bc(W["mla_q_gain"][l:l + 1, :], [128, 96]), (), ("gqb",), slow=True)
        self.dma("sp", self.gkb[:], bc(W["mla_k_gain"][l:l + 1, :], [128, 96]), (), ("gkb",), slow=True)
        for k in range(4):
            self.dma("sp", self.cw[:, :, k], W["ssd_conv_w"][l, k].rearrange("(c p) -> p c", p=128), (), ("cw",), slow=True)
        self.dma("sp", self.cb[:], W["ssd_conv_b"][l].rearrange("(c p) -> p c", p=128), (), ("cb",), slow=True)
        self.dma("sp", self.dtb[:], bc(W["ssd_dt_bias"][l:l + 1, :], [128, 8]), (), ("dtb",), slow=True)
        self.dma("sp", self.aneg[:], bc(W["ssd_a_log"][l:l + 1, :], [128, 8]), (), ("aneg",), slow=True)
        self.dma("sp", self.Dsk[:], bc(W["ssd_d"][l:l + 1, :], [128, 8]), (), ("Dsk",), slow=True)
        self.dma("sp", self.gsn[:], bc(W["ssd_norm"][l:l + 1, :], [128, 512]), (), ("gsn",), slow=True)
        self.act(self.aneg[:], self.aneg[:], AF.Exp, ("aneg",), ("aneg",))
        self.ts("dve", self.aneg[:], self.aneg[:], -1.0, None, ALU.mult, None, ("aneg",), ("aneg",))
        self.tt("dve", self.Wsm[:], self.Wsraw[:], self.Lm_f[:, None, :].broadcast_to([128, 4, 128]), ALU.mult,
                ("Wsraw", "Lm_f"), ("Wsm",))
        b = self.bank()
        for g in range(4):
            self.tr(b, self.psb(b)[:, g * 128:(g + 1) * 128], self.Wsm[:, g, :], self.ident_b[:], ("Wsm", "ident_b"))
        self.cp("dve", self.WsT[:], self.psb(b)[:, 0:512].rearrange("p (g t) -> p g t", g=4), (psk(b),), ("WsT",))
        self.P.add("dve", lambda e: e.tensor_reduce(out=self.gmx[:, 0:1], in_=self.gqb[:], axis=AX.X, op=ALU.max,
                                                    apply_absolute_value=True), ("gqb",), ("gmx",))
        self.P.add("dve", lambda e: e.tensor_reduce(out=self.gmx[:, 1:2], in_=self.gkb[:], axis=AX.X, op=ALU.max,
                                                    apply_absolute_value=True), ("gkb",), ("gmx",))
        self.stt("dve", self.negC[:], self.gmx[:, 0:1], -float(np.sqrt(96.0)), self.gmx[:, 1:2], ALU.mult, ALU.mult,
                 ("gmx",), ("negC",))
        self.P.add("dve", lambda e: e.memset(self.Sst[:], 0.0), (), ("Sst",))
        self.P.add("dve", lambda e: e.memset(self.Sbf[:], 0.0), (), ("Sbf",))
        self.P.add("dve", lambda e: e.memset(self.hist[:], 0.0), (), ("hist",))
        if l == 0:
            self.P.add("dve", lambda e: e.memset(self.vst[:, :, 64:65], 1.0), (), ("vst1",))

    def load_x(self, l, I, x, xk):
        if l > 0:
            self.dma("sp", x[:], self.xscr[:, :, I * TS:(I + 1) * TS], (("xscr", I),), (xk,))
            return
        XK = ("R", "arg")
        src = self.x_in[I * TS:(I + 1) * TS, :].rearrange("(c p) d -> p c d", p=128)
        self.dma("sp", self.xtm, src, (), XK)
        for k in range(NK):
            b = self.bank()
            for c in range(NC):
                self.tr(b, self.ps[b][:, c * 128:(c + 1) * 128], self.xtm[:, c, k * 128:(k + 1) * 128], self.ident_f[:],
                        XK + ("ident_f",))
            eng = "dve" if k % 2 == 0 else "act"
            self.cp(eng, x[:, k, :], self.pt(b), (psk(b),), (xk,))

    def store_x(self, l, I, x, xk):
        if l < self.L - 1:
            return [self.dma("sp", self.xscr[:, :, I * TS:(I + 1) * TS], x[:], (xk,), (("xscr", I),))]
        XK = ("R", "arg")
        for c in range(NC):
            for hlf in range(2):
                b = self.bank()
                for kk in range(4):
                    k = hlf * 4 + kk
                    self.tr(b, self.ps[b][:, kk * 128:(kk + 1) * 128], x[:, k, c * 128:(c + 1) * 128],
                            self.ident_f[:], (xk, "ident_f"))
                eng = "dve" if hlf == 0 else "act"
                self.cp(eng, self.xtm[:, c, hlf * 512:(hlf + 1) * 512], self.ps[b][:], (psk(b),), XK)
        dst = self.out[I * TS:(I + 1) * TS, :].rearrange("(c p) d -> p c d", p=128)
        return [self.dma("sp", dst, self.xtm, XK, ())]

    def rmsnorm_T(self, l, gi, x, xk, ffn):
        hT, hname = (self.hTf, "hTf") if ffn else (self.hT, "hT")
        rstd, rk = (self.rstdf, "rstdf") if ffn else (self.rstd, "rstd")
        b = self.bank()
        for k in range(NK):
            if ffn:
                sq, sqk = self.sqf[self.sqf_i % 2], "sqf%d" % (self.sqf_i % 2)
                self.sqf_i += 1
            else:
                sq, sqk = self.sq[self.sq_i % 2], "sq%d" % (self.sq_i % 2)
                self.sq_i += 1
            self.act(sq[:], x[:, k, :], AF.Square, (xk,), (sqk,))
            self.mm(b, self.pt(b), self.ones_b[:], sq[:], k == 0, k == NK - 1, (sqk, "ones_b"))
        self.rsqrt(rstd[:], self.pt(b), 1.0 / D, (psk(b),), (rk,))
        for k in range(NK):
            self.stt("dve", hT[:, k, :], x[:, k, :], self.gains[:, l % 2, gi, k:k + 1], rstd[:], ALU.mult, ALU.mult,
                     (xk, ("gains", l % 2), rk), ((hname, k),))

    def ffn(self, l, f, x, xk):
        gi = 0 if f == 1 else 2
        self.rmsnorm_T(l, gi, x, xk, True)
        yield
        swi, swo = self.scr_wi[(l, f)], self.scr_wo[(l, f)]
        for jj in range(NJJ):
            wi = self.wif[self.wif_i % 2]
            wik = "wif%d" % (self.wif_i % 2)
            self.wif_i += 1
            self.dma("sp", wi[:], swi[jj], (("swi", l, f, jj, 0), ("swi", l, f, jj, 1)), (wik,))
            for j2 in range(2):
                j = jj * 2 + j2
                bg, bu = self.bank(), self.bank()
                for (b, off) in ((bg, 0), (bu, 256)):
                    for k in range(NK):
                        self.mm(b, self.pt(b), wi[:, k, off + j2 * 128: off + (j2 + 1) * 128], self.hTf[:, k, :],
                                k == 0, k == NK - 1, (wik, ("hTf", k)))
                sg = self.sg[self.sg_i % 2]
                sgk = "sg%d" % (self.sg_i % 2)
                self.sg_i += 1
                self.act(sg[:], self.pt(bg), AF.Silu, (psk(bg),), (sgk,))
                self.tt("dve", self.actT[:, j, :], sg[:], self.pt(bu), ALU.mult, (sgk, psk(bu)), (("actT", j),))
                yield
        for mg in range(2):
            banks = [self.bank() for _ in range(4)]
            self.reserved.update(banks)
            for jj in range(NJJ):
                wo = self.wo[self.wo_i % 3]
                wok = "wo%d" % (self.wo_i % 3)
                self.wo_i += 1
                self.dma("sp", wo[:], swo[mg, jj], (("swo", l, f, mg, jj),), (wok,))
                for j2 in range(2):
                    j = jj * 2 + j2
                    for m in range(4):
                        b = banks[m]
                        self.mm(b, self.pt(b), wo[:, j2, m * 128:(m + 1) * 128], self.actT[:, j, :],
                                j == 0, j == NJ - 1, (wok, ("actT", j)))
                yield
            for m in range(4):
                b = banks[m]
                k = mg * 4 + m
                self.stt("dve", x[:, k, :], self.pt(b), 0.5, x[:, k, :], ALU.mult, ALU.add, (psk(b), xk), (xk,))
            self.reserved.difference_update(banks)
            yield
    def load_blk(self, src, key, ncols=512, nk=NK):
        wi = self.wi[self.wi_i % NWI]
        wik = "wi%d" % (self.wi_i % NWI)
        self.wi_i += 1
        self.dma("sp", wi[:, 0:nk, 0:ncols], src, key, (wik,))
        return wi, wik

    def load_flat(self, src, key, n_i, n_c):
        wi = self.wi[self.wi_i % NWI]
        wik = "wi%d" % (self.wi_i % NWI)
        self.wi_i += 1
        v = wi[:].rearrange("p k c -> p (k c)")[:, 0:n_i * n_c].rearrange("p (i c) -> p i c", i=n_i)
        self.dma("sp", v, src, key, (wik,))
        return v, wik

    def fm_chunk(self, wi, wik, col0, nk=NK, rhs=None, rkey=None):
        b = self.bank()
        for k in range(nk):
            r = self.hT[:, k, :] if rhs is None else rhs[:, k, :]
            rk = ("hT", k) if rhs is None else rkey
            self.mm(b, self.pt(b), wi[:, k, col0:col0 + 128], r, k == 0, k == nk - 1, (wik, rk))
        return b

    def tm_chunk(self, wi, wik, c, col0, ncols):
        b = self.bank()
        for k in range(NK):
            self.mm(b, self.ps[b][:, 0:ncols], self.hT[:, k, c * 128:(c + 1) * 128], wi[:, k, col0:col0 + ncols],
                    k == 0, k == NK - 1, (wik, ("hT", k)))
        return b

    def head_norm_rope(self, cg, gb, gbk, dst_T, dst_key, col0):
        q3 = self.qtm[:]
        self.tt("dve", self.junk[:].rearrange("p (h d) -> p h d", h=8), q3, q3, ALU.mult, ("qtm",), ("junk",))
        self.red(self.hss[:], self.junk[:].rearrange("p (h d) -> p h d", h=8), ("junk",), ("hss",))
        self.rsqrt(self.hss[:], self.hss[:], 1.0 / 96, ("hss",), ("hss",))
        self.tt("dve", q3, q3, self.hss[:, :, None].broadcast_to([128, 8, 96]), ALU.mult, ("qtm", "hss"), ("qtm",))
        self.tt("dve", q3, q3, gb[:, None, :].broadcast_to([128, 8, 96]), ALU.mult, ("qtm", gbk), ("qtm",))
        cos = self.cosT[:, cg, :][:, None, :].broadcast_to([128, 8, 16])
        sin = self.sinT[:, cg, :][:, None, :].broadcast_to([128, 8, 16])
        x1, x2 = self.qtm[:, :, 64:80], self.qtm[:, :, 80:96]
        self.cp("act", self.qr[:, :, 0:64], self.qtm[:, :, 0:64], ("qtm",), ("qr",))
        self.tt("dve", self.r1[:], x1, cos, ALU.mult, ("qtm", "cosT"), ("r1",))
        self.tt("dve", self.r2[:], x2, sin, ALU.mult, ("qtm", "sinT"), ("r2",))
        self.tt("dve", self.qr[:, :, 64:80], self.r1[:], self.r2[:], ALU.subtract, ("r1", "r2"), ("qr",))
        self.tt("dve", self.r1[:], x2, cos, ALU.mult, ("qtm", "cosT"), ("r1",))
        self.tt("dve", self.r2[:], x1, sin, ALU.mult, ("qtm", "sinT"), ("r2",))
        self.tt("dve", self.qr[:, :, 80:96], self.r1[:], self.r2[:], ALU.add, ("r1", "r2"), ("qr",))
        for g in range(2):
            b = self.bank()
            pb = self.psb(b)
            for hh in range(4):
                self.tr(b, pb[0:96, hh * 128:(hh + 1) * 128], self.qr[:, g * 4 + hh, :], self.ident_b[:], ("qr", "ident_b"))
            self.cp("act" if g else "dve", dst_T[0:96, g * 4:g * 4 + 4, col0:col0 + 128],
                    pb[0:96, 0:512].rearrange("p (h t) -> p h t", h=4), (psk(b),), (dst_key,))

    def mixer(self, l, I, x, xk):
        sin_ = self.scr_in[l]
        self.rmsnorm_T(l, 1, x, xk, False)
        yield
        wi, wik = self.load_blk(sin_[0], (("sin", l, 0),))
        for ch in range(4):
            b = self.fm_chunk(wi, wik, ch * 128)
            self.act(self.uT[:, ch, :], self.pt(b), AF.Gelu, (psk(b),), ("uT",))
        yield
        wi, wik = self.load_blk(sin_[1], (("sin", l, 1),))
        for c in range(NC):
            b = self.tm_chunk(wi, wik, c, 0, 512)
            self.act(self.vg[:], self.ps[b][:], AF.Gelu, (psk(b),), ("vg",))
            self.act(self.junk[:, 0:512], self.vg[:], AF.Square, ("vg",), ("junk",))
            self.red(self.s1[:, 0:1], self.junk[:, 0:512], ("junk",), ("s1",))
            self.rsqrt(self.s1[:, 0:1], self.s1[:, 0:1], 1.0 / 512, ("s1",), ("s1",))
            self.stt("dve", self.vtm[:, c, :], self.vg[:], self.s1[:, 0:1], self.gvb[:], ALU.mult, ALU.mult,
                     ("vg", "s1", "gvb"), (("vtm", c),))
        yield
        wi2, wi2k = self.load_blk(sin_[2], (("sin", l, 2),))
        wi3, wi3k = self.load_blk(sin_[3][:, :, 0:168], (("sin", l, 3, 0), ("sin", l, 3, 1)), ncols=168)
        for i in range(5):
            if i < 4:
                b = self.fm_chunk(wi2, wi2k, i * 128)
            else:
                b = self.fm_chunk(wi3, wi3k, 0)
            self.act(self.sqq[:, i, :], self.pt(b), AF.Square, (psk(b),), ("sqq",))
            if i < 3:
                self.ts("dve", self.cqnT[:, i, :], self.pt(b), self.gqn[:, i:i + 1], None, ALU.mult, None,
                        (psk(b), "gqn"), ("cqnT",))
            else:
                self.ts("dve", self.ckvnT[:, i - 3, :], self.pt(b), self.gkvn[:, i - 3:i - 2], None, ALU.mult, None,
                        (psk(b), "gkvn"), ("ckvnT",))
        for c in range(NC):
            b = self.tm_chunk(wi3, wi3k, c, 128, 40)
            self.cp("dve", self.krdt[:, c, :], self.ps[b][:, 0:40], (psk(b),), ("krdt",))
        yield
        wi, wik = self.load_blk(sin_[4], (("sin", l, 4),))
        for c in range(NC):
            b = self.tm_chunk(wi, wik, c, 0, 512)
            self.act(self.zs[:, c, :], self.ps[b][:], AF.Silu, (psk(b),), ("zs",))
        for blk in range(2):
            yield
            wi, wik = self.load_blk(sin_[5 + blk], (("sin", l, 5 + blk),))
            for cc in range(4):
                ch = blk * 4 + cc
                b = self.fm_chunk(wi, wik, cc * 128)
                xp = self.xbcp[ch % 2]
                xk = "xbcp%d" % (ch % 2)
                self.cp("act", xp[:, 3:TS + 3], self.pt(b), (psk(b),), (xk,))
                self.cp("dve", xp[:, 0:3], self.hist[:, ch, :], ("hist",), (xk,))
                self.ts("dve", self.cacc, xp[:, 3:TS + 3], self.cw[:, ch, 3:4], self.cb[:, ch:ch + 1], ALU.mult, ALU.add,
                        (xk, "cw", "cb"), ("vg",))
                for t in range(3):
                    self.stt("dve", self.cacc, xp[:, t:TS + t], self.cw[:, ch, t:t + 1], self.cacc, ALU.mult, ALU.add,
                             (xk, "cw", "vg"), ("vg",))
                self.cp("dve", self.hist[:, ch, :], xp[:, TS:TS + 3], (xk,), ("hist",))
                self.act(self.xbcT[:, ch, :], self.cacc, AF.Silu, ("vg",), ("xbcT",))
                yield
        wq = self.load_flat(self.scr_uq[l][:], (("suq", l),), 3, 768)
        wkv = self.load_flat(self.scr_ukv[l][:], (("sukv", l),), 2, 1024)
        for c in range(NC):
            self.mla_chunk(c, I * NC + c, slice(c * 128, (c + 1) * 128), wq, wkv)
            yield

        def side():
            for c in range(NC):
                cg = I * NC + c
                ck = slice(c * 128, (c + 1) * 128)
                b = self.bank()
                for g in range(4):
                    self.mm(b, self.ps[b][:, g * 128:(g + 1) * 128], self.vtm[:, c, g * 128:(g + 1) * 128], self.WsT[:, g, :],
                            True, True, (("vtm", c), "WsT"))
                self.tt("dve", self.spb[:], self.ps[b][:].rearrange("p (g t) -> p g t", g=4), self.bsb[:], ALU.add,
                        (psk(b), "bsb"), ("spb",))
                self.tt("dve", self.yaT[:, :, ck], self.spb[:], self.uT[:, :, ck], ALU.mult, ("spb", "uT"), ("yaT",))
                yield from self.ssd_chunk(c, cg, ck)

        att = self.attention(I)
        sd = side()
        n_att = 8 * (NC * I + NC) + 8
        n_side = NC * 12
        acc = 0.0
        att_done = side_done = False
        while not (att_done and side_done):
            if not att_done:
                try:
                    next(att)
                except StopIteration:
                    att_done = True
            yield
            acc += n_side / n_att
            while (acc >= 1.0 or att_done) and not side_done:
                acc -= 1.0
                try:
                    next(sd)
                except StopIteration:
                    side_done = True
        yield from self.merge(l, x, xk)

    def ssd_chunk(self, c, cg, ck):
        xs = self.xbcT
        self.tt("dve", self.t8a[:], self.krdt[:, c, 32:40], self.dtb[:], ALU.add, ("krdt", "dtb"), ("t8a",))
        self.ts("dve", self.t8b[:], self.t8a[:], -1.0, None, ALU.mult, None, ("t8a",), ("t8b",))
        self.tt("dve", self.t8b[:], self.t8b[:], self.t8a[:], ALU.max, ("t8a", "t8b"), ("t8b",))
        self.act(self.t8b[:], self.t8b[:], AF.Exp, ("t8b",), ("t8b",), scale=-1.0)
        self.act(self.t8b[:], self.t8b[:], AF.Ln, ("t8b", "one_t"), ("t8b",), bias=self.one_t[:])
        self.stt("dve", self.dt8[:], self.t8a[:], 0.0, self.t8b[:], ALU.max, ALU.add, ("t8a", "t8b"), ("dt8",))
        self.tt("dve", self.da8[:], self.dt8[:], self.aneg[:], ALU.mult, ("dt8", "aneg"), ("da8",))
        yield
        b = self.bank()
        self.mm(b, self.ps[b][:, 0:8], self.U_f[:], self.da8[:], True, True, ("U_f", "da8"))
        self.cp("dve", self.cs8[:], self.ps[b][:, 0:8], (psk(b),), ("cs8",))
        self.tt("dve", self.R[:], self.U_f[:, None, :].broadcast_to([128, 8, 128]),
                self.da8[:, :, None].broadcast_to([128, 8, 128]), ALU.mult, ("U_f", "da8"), ("R",))
        yield
        bb = [self.bank(), self.bank()]
        for hh in range(2):
            self.mm(bb[hh], self.ps[bb[hh]][:], self.ones_f[:], self.R[:, hh * 4:hh * 4 + 4, :].rearrange("p h l -> p (h l)"),
                    True, True, ("ones_f", "R"))
        for hh in range(2):
            p3 = self.ps[bb[hh]][:].rearrange("p (h l) -> p h l", h=4)
            hs = slice(hh * 4, hh * 4 + 4)
            self.tt("dve", self.arg[:, hs, :], p3, self.cs8[:, hs, None].broadcast_to([128, 4, 128]), ALU.subtract,
                    (psk(bb[hh]), "cs8"), ("arg",))
            self.act(self.ecs[:, hs, :], p3, AF.Exp, (psk(bb[hh]),), ("ecs",))
            self.cp("dve", self.tot8[:, hs], p3[:, :, 127], (psk(bb[hh]),), ("tot8",))
        self.act(self.arg[:], self.arg[:], AF.Relu, ("arg",), ("arg",), scale=-1.0)
        self.act(self.arg[:], self.arg[:], AF.Exp, ("arg",), ("arg",), scale=-1.0)
        self.tt("dve", self.t8a[:], self.tot8[:], self.cs8[:], ALU.subtract, ("tot8", "cs8"), ("t8a",))
        self.act(self.dout8[:], self.t8a[:], AF.Exp, ("t8a",), ("dout8",))
        self.act(self.etot8[:], self.tot8[:], AF.Exp, ("tot8",), ("etot8",))
        yield
        b = self.bank()
        for g in range(2):
            self.mm(b, self.ps[b][:, g * 128:(g + 1) * 128], xs[:, 4 + g, ck], xs[:, 6 + g, ck], True, True, ("xbcT",))
        self.tt("dve", self.cbm[:], self.ps[b][:, 0:256].rearrange("p (g l) -> p g l", g=2),
                self.U_f[:, None, :].broadcast_to([128, 2, 128]), ALU.mult, (psk(b), "U_f"), ("cbm",))
        self.tt("dve", self.MT[:].rearrange("p (g r) l -> p g r l", g=2), self.arg[:].rearrange("p (g r) l -> p g r l", g=2),
                self.cbm[:, :, None, :].broadcast_to([128, 2, 4, 128]), ALU.mult, ("arg", "cbm"), ("MT",))
        self.tt("dve", self.Cs[:].rearrange("p (g r) l -> p g r l", g=2), self.ecs[:].rearrange("p (g r) l -> p g r l", g=2),
                xs[:, 6:8, ck][:, :, None, :].broadcast_to([128, 2, 4, 128]), ALU.mult, ("ecs", "xbcT"), ("Cs",))
        yield
        b = self.bank()
        pb = self.psb(b)
        for i in range(4):
            self.tr(b, pb[:, i * 128:(i + 1) * 128], xs[:, i, ck], self.ident_b[:], ("xbcT", "ident_b"))
        x3 = pb[:, 0:512].rearrange("p (h d) -> p h d", h=8)
        self.tt("dve", self.xdt[:], x3, self.dt8[:, :, None].broadcast_to([128, 8, 64]), ALU.mult, (psk(b), "dt8"), ("xdt",))
        self.tt("dve", self.xsd[:], x3, self.Dsk[:, :, None].broadcast_to([128, 8, 64]), ALU.mult, (psk(b), "Dsk"), ("xsd",))
        self.tt("dve", self.xdd[:], self.xdt[:], self.dout8[:, :, None].broadcast_to([128, 8, 64]), ALU.mult,
                ("xdt", "dout8"), ("xdd",))
        yield
        b = self.bank()
        pb = self.psb(b)
        for g in range(2):
            self.tr(b, pb[:, g * 128:(g + 1) * 128], xs[:, 4 + g, ck], self.ident_b[:], ("xbcT", "ident_b"))
        self.cp("act", self.Btm[:], pb[:, 0:256].rearrange("p (g n) -> p g n", g=2), (psk(b),), ("Btm",))
        yield
        b = self.bank()
        for h in range(8):
            o = self.ps[b][:, h * 64:(h + 1) * 64]
            self.mm(b, o, self.MT[:, h, :], self.xdt[:, h, :], True, False, ("MT", "xdt"))
            self.mm(b, o, self.Cs[:, h, :], self.Sbf[:, h, :], False, True, ("Cs", "Sbf"))
        self.tt("dve", self.y1[:], self.ps[b][:], self.xsd[:].rearrange("p h d -> p (h d)"), ALU.add, (psk(b), "xsd"), ("vg",))
        self.tt("dve", self.y1[:], self.y1[:], self.zs[:, c, :], ALU.mult, ("vg", "zs"), ("vg",))
        self.act(self.junk[:, 0:512], self.y1[:], AF.Square, ("vg",), ("junk",))
        self.red(self.s2[:, 0:2], self.junk[:, 0:512].rearrange("p (g d) -> p g d", g=2), ("junk",), ("s2",))
        self.rsqrt(self.s2[:, 0:2], self.s2[:, 0:2], 1.0 / 256, ("s2",), ("s2",))
        for g in range(2):
            gs = slice(g * 256, (g + 1) * 256)
            self.stt("dve", self.yctm[:, gs], self.y1[:, gs], self.s2[:, g:g + 1], self.gsn[:, gs], ALU.mult, ALU.mult,
                     ("vg", "s2", "gsn"), ("yctm",))
        yield
        b = self.bank()
        pb = self.psb(b)
        for i in range(4):
            self.tr(b, pb[:, i * 128:(i + 1) * 128], self.yctm[:, i * 128:(i + 1) * 128], self.ident_b[:], ("yctm", "ident_b"))
        self.cp("act", self.ycT[:, :, ck], pb[:, 0:512].rearrange("p (i t) -> p i t", i=4), (psk(b),), ("ycT",))
        yield
        b = self.bank()
        for h in range(8):
            self.mm(b, self.ps[b][:, h * 64:(h + 1) * 64], self.Btm[:, h // 4, :], self.xdd[:, h, :], True, True, ("Btm", "xdd"))
        self.tt("dve", self.Sst[:], self.Sst[:], self.etot8[:, :, None].broadcast_to([128, 8, 64]), ALU.mult,
                ("Sst", "etot8"), ("Sst",))
        self.tt("dve", self.Sst[:], self.Sst[:], self.ps[b][:].rearrange("p (h d) -> p h d", h=8), ALU.add,
                ("Sst", psk(b)), ("Sst",))
        self.cp("dve", self.Sbf[:], self.Sst[:], ("Sst",), ("Sbf",))
        yield

    def mla_chunk(self, c, cg, ck, wq, wkv):
        Wuq, Wuqk = wq
        Wukv, Wukvk = wkv
        b = self.bank()
        for i in range(3):
            self.mm(b, self.ps[b][:, 0:1], self.sqq[:, i, ck], self.ones_b[:, 0:1], i == 0, i == 2, ("sqq", "ones_b"))
        for i in range(2):
            self.mm(b, self.ps[b][:, 1:2], self.sqq[:, 3 + i, ck], self.ones_b[:, 0:1], i == 0, i == 1, ("sqq", "ones_b"))
        self.rsqrt(self.rq2[:, 0:1], self.ps[b][:, 0:1], 1.0 / 384, (psk(b),), ("rq2",))
        self.rsqrt(self.rq2[:, 1:2], self.ps[b][:, 1:2], 1.0 / 256, (psk(b),), ("rq2",))
        for half in range(2):
            b = self.bank()
            for i in range(3):
                self.mm(b, self.ps[b][:, 0:384], self.cqnT[:, i, ck], Wuq[:, i, half * 384:(half + 1) * 384], i == 0,
                        i == 2, ("cqnT", Wuqk))
            self.ts("dve", self.qtm[:, half * 4:half * 4 + 4, :], self.ps[b][:, 0:384].rearrange("p (h d) -> p h d", h=4),
                    self.rq2[:, 0:1], None, ALU.mult, None, (psk(b), "rq2"), ("qtm",))
        self.head_norm_rope(cg, self.gqb, "gqb", self.qT, "qT", c * 128)
        for half in range(2):
            b = self.bank()
            for i in range(2):
                self.mm(b, self.ps[b][:], self.ckvnT[:, i, ck], Wukv[:, i, half * 512:(half + 1) * 512], i == 0, i == 1,
                        ("ckvnT", Wukvk))
            p3 = self.ps[b][:].rearrange("p (h d) -> p h d", h=4)
            hs = slice(half * 4, half * 4 + 4)
            self.ts("dve", self.vst[:, hs, 0:64], p3[:, :, 64:128], self.rq2[:, 1:2], None, ALU.mult, None,
                    (psk(b), "rq2"), ("vst",))
            self.ts("dve", self.qtm[:, hs, 0:64], p3[:, :, 0:64], self.rq2[:, 1:2], None, ALU.mult, None,
                    (psk(b), "rq2"), ("qtm",))
        self.cp("dve", self.qtm[:, :, 64:96], self.krdt[:, c, 0:32][:, None, :].broadcast_to([128, 8, 32]), ("krdt",), ("qtm",))
        self.dma("sp", self.vscr[:, :, cg, :].rearrange("h p d -> p h d"), self.vst[:], ("vst", "vst1"), (("vscr", cg),))
        self.head_norm_rope(cg, self.gkb, "gkb", self.kst, "kst", 0)
        self.dma("sp", self.kscr[:, :, cg * 128:(cg + 1) * 128].rearrange("h d t -> d h t"), self.kst[0:96, :, :], ("kst",),
                 (("kscr", cg),))

    def attention(self, I):
        scale = 96.0 ** -0.5
        nj = NC * I + NC
        for h in range(8):
            kh = self.kh[self.kv_i % 2]
            vh = self.vh[0]
            kk, vk = "kh%d" % (self.kv_i % 2), "vh0"
            self.kv_i += 1
            skeys = tuple(("kscr", j) for j in range(nj))
            vkeys = tuple(("vscr", j) for j in range(nj))
            self.dma("sp", kh[0:96, 0:nj * 128], self.kscr[h, :, 0:nj * 128], skeys, (kk,))
            self.dma("sp", vh[:, 0:nj, :], self.vscr[h, :, 0:nj, :], vkeys, (vk,))
            bacc = self.bank()
            self.reserved.add(bacc)
            for j in range(nj):
                c0 = max(0, j - NC * I)
                bs = self.bank()
                self.mm(bs, self.ps[bs][:, c0 * 128:TS], kh[0:96, j * 128:(j + 1) * 128], self.qT[0:96, h, c0 * 128:TS],
                        True, True, (kk, "qT"))
                pT = self.pT[self.pt_i % 2]
                pk = "pT%d" % (self.pt_i % 2)
                self.pt_i += 1
                self.act(pT[:, c0 * 128:TS], self.ps[bs][:, c0 * 128:TS], AF.Exp, (psk(bs), "negC"), (pk,),
                         bias=self.negC[:], scale=scale)
                if j >= NC * I:
                    self.tt("dve", pT[:, c0 * 128:(c0 + 1) * 128], pT[:, c0 * 128:(c0 + 1) * 128], self.U_b[:], ALU.mult,
                            (pk, "U_b"), (pk,))
                for c in range(c0, NC):
                    self.mm(bacc, self.ps[bacc][:, c * 128:c * 128 + 65], pT[:, c * 128:(c + 1) * 128], vh[:, j, :],
                            (j == 0 and c == 0), (j == NC * I + c), (pk, vk), skip=True)
                yield
            a3 = self.ps[bacc][:].rearrange("p (c d) -> p c d", c=4)
            self.P.add("dve", lambda e, a3=a3: e.reciprocal(out=self.rden[:, 0:NC], in_=a3[:, 0:NC, 64]), (psk(bacc),), ("rden",))
            self.tt("dve", self.ybtm[:, :, h * 64:(h + 1) * 64], a3[:, 0:NC, 0:64],
                    self.rden[:, 0:NC, None].broadcast_to([128, NC, 64]), ALU.mult, (psk(bacc), "rden"), ("ybtm",))
            self.reserved.discard(bacc)
            yield
        for c in range(NC):
            b = self.bank()
            pb = self.psb(b)
            for i in range(4):
                self.tr(b, pb[:, i * 128:(i + 1) * 128], self.ybtm[:, c, i * 128:(i + 1) * 128], self.ident_b[:],
                        ("ybtm", "ident_b"))
            self.cp("act", self.ybT[:, :, c * 128:(c + 1) * 128], pb[:, 0:512].rearrange("p (i t) -> p i t", i=4),
                    (psk(b),), ("ybT",))

    def merge(self, l, x, xk):
        ys = ((self.yaT, "yaT"), (self.ybT, "ybT"), (self.ycT, "ycT"))
        for i in range(3):
            if not (self.brmask >> i) & 1:
                self.P.add("dve", lambda e, t=ys[i][0]: e.memset(t[:], 0.0), (), (ys[i][1],))
        for half in range(2):
            for i in range(3):
                wg, wgk = self.load_blk(self.scr_in[l][7 + 2 * i + half], (("sin", l, 7 + 2 * i + half),))
                wb, wbk = self.load_blk(self.scr_br[l][i][:, :, half * 512:(half + 1) * 512], (("sbr", l, i),), nk=4)
                for m in range(4):
                    bg = self.fm_chunk(wg, wgk, m * 128)
                    sg = self.sig[self.sig_i % 2]
                    sgk = "sig%d" % (self.sig_i % 2)
                    self.sig_i += 1
                    self.act(sg[:], self.pt(bg), AF.Sigmoid, (psk(bg),), (sgk,))
                    bb = self.fm_chunk(wb, wbk, m * 128, nk=4, rhs=ys[i][0], rkey=ys[i][1])
                    if i == 0:
                        self.tt("dve", self.mgacc[:, m, :], sg[:], self.pt(bb), ALU.mult, (sgk, psk(bb)), ("mgacc",))
                    else:
                        self.tt("dve", sg[:], sg[:], self.pt(bb), ALU.mult, (sgk, psk(bb)), (sgk,))
                        if i == 1:
                            self.tt("dve", self.mgacc[:, m, :], self.mgacc[:, m, :], sg[:], ALU.add, (sgk, "mgacc"), ("mgacc",))
                        else:
                            self.tt("dve", self.mgT[:, half * 4 + m, :], self.mgacc[:, m, :], sg[:], ALU.add, (sgk, "mgacc"),
                                    ("mgT",))
                    yield
        for half in range(2):
            wo, wok = self.load_blk(self.scr_o[l][half], (("so", l, half),))
            for m in range(4):
                b = self.fm_chunk(wo, wok, m * 128, rhs=self.mgT, rkey="mgT")
                k = half * 4 + m
                self.tt("dve", x[:, k, :], x[:, k, :], self.pt(b), ALU.add, (xk, psk(b)), (xk,))
            yield

    def build(self):
        assert NC == 2
        self.declare()
        self.alloc()
        self.consts()
        finals = []
        L, NT = self.L, self.NT
        convs = [self.conv_list(l) for l in range(L)]
        for fn in convs[0]:
            fn()
        slots = [(l, I) for l in range(L) for I in range(NT)]
        nslot = len(slots)

        def xt(s):
            return self.xTs[s % 2], "xT%d" % (s % 2)

        def fstream(s):
            if s - 1 >= 0:
                l, I = slots[s - 1]
                x, xk = xt(s - 1)
                if self.do_ffn:
                    yield from self.ffn(l, 2, x, xk)
                finals.extend(self.store_x(l, I, x, xk))
                yield
            if s + 1 < nslot:
                l, I = slots[s + 1]
                x, xk = xt(s + 1)
                if I == 0:
                    self.load_gains(l)
                self.load_x(l, I, x, xk)
                yield
                if self.do_ffn:
                    yield from self.ffn(l, 1, x, xk)

        def run(*gens, weights=None):
            gens = [g for g in gens if g is not None]
            alive = [True] * len(gens)
            acc = [0.0] * len(gens)
            w = weights or [1.0] * len(gens)
            while any(alive):
                for i, g in enumerate(gens):
                    if not alive[i]:
                        continue
                    acc[i] += w[i]
                    while acc[i] >= 1.0 and alive[i]:
                        acc[i] -= 1.0
                        try:
                            next(g)
                        except StopIteration:
                            alive[i] = False

        self.load_gains(0)
        self.load_x(0, 0, *xt(0))
        if self.do_ffn:
            run(self.ffn(0, 1, *xt(0)))
        for s, (l, I) in enumerate(slots):
            if I == 0 and self.do_mix:
                self.load_params(l)
            nxt = convs[l + 1] if l + 1 < L else []
            per = (len(nxt) + NT - 1) // NT if nxt else 0
            for fn in nxt[I * per:(I + 1) * per]:
                fn()
            x, xk = xt(s)
            n_f = 2 * (1 + NJ + 2 * NJJ + 2) + 2
            if self.do_mix:
                n_m = 24 + 2 * (8 * (NC * I + NC) + 8) + 30
                run(self.mixer(l, I, x, xk), fstream(s), weights=[1.0, min(1.0, n_f / n_m)])
            else:
                run(fstream(s))
        run(fstream(nslot))
        self.P.emit(finals)
        self.es.close()
        return self.nc


_CACHE = {}


def kernel(**inputs):
    x = np.asarray(inputs["x"])
    B, S, _ = x.shape
    L = int(np.asarray(inputs["ffn1_norm"]).shape[0])
    key = (S, L)
    if key not in _CACHE:
        _CACHE[key] = Builder(S, L).build()
    nc = _CACHE[key]
    shared = {k: np.ascontiguousarray(np.asarray(v)) for k, v in inputs.items() if k not in ("x", "positions")}
    pos = np.asarray(inputs["positions"]).astype(np.int32)
    in_maps = []
    for b in range(B):
        m = dict(shared)
        m["x"] = np.ascontiguousarray(x[b])
        m["positions"] = np.ascontiguousarray(pos[b])
        in_maps.append(m)
    res = run_bass_kernel_spmd(nc, in_maps, core_ids=list(range(B)))
    return np.stack([np.asarray(r["out"]) for r in res.results], axis=0).astype(np.float32)
```

```python
import contextlib
import numpy as np
import concourse.bass as bass
import concourse.mybir as mybir
from concourse.bass_utils import run_bass_kernel_spmd

F32 = mybir.dt.float32
BF16 = mybir.dt.bfloat16
I32 = mybir.dt.int32
AF = mybir.ActivationFunctionType
ALU = mybir.AluOpType
AX = mybir.AxisListType

D = 1024
DFF = 2816
EPS = 1e-6
TS = 256
NC = TS // 128
NK = 8
NJ = DFF // 128
NJJ = NJ // 2
IN_COLS = 6312

ENGS = ("pe", "act", "dve", "pool", "sp")
NDMA_SEM = 12


class Op:
    __slots__ = ("eng", "fn", "dma", "idx", "deps", "signal", "sigval", "dma_i")

    def __init__(self, eng, fn, dma):
        self.eng = eng
        self.fn = fn
        self.dma = dma
        self.deps = None
        self.signal = False
        self.sigval = 0
        self.dma_i = -1


class Prog:
    def __init__(self, nc):
        self.nc = nc
        self.ops = {e: [] for e in ENGS}
        self.last_w = {}
        self.readers = {}
        self.dma_count = {e: 0 for e in ENGS}
        self.dma_ops = {e: [] for e in ENGS}

    @staticmethod
    def _stream(op):
        return op.eng + ":dma" if op.dma else op.eng

    def add(self, eng, fn, reads=(), writes=(), dma=False):
        op = Op(eng, fn, dma)
        op.idx = len(self.ops[eng])
        deps = {}
        if eng != "pe" and not dma:
            pk = tuple(k for k in reads if isinstance(k, str) and k[:2] == "ps" and k[2:].isdigit())
            if pk:
                writes = tuple(writes) + pk

        def dep(o, raw):
            if o is None:
                return
            if (not o.dma) and o.eng == eng and not dma:
                if eng == "pe" or not raw:
                    return
            s = self._stream(o)
            cur = deps.get(s)
            if cur is None or cur.idx < o.idx:
                deps[s] = o

        for k in reads:
            dep(self.last_w.get(k), True)
        for k in writes:
            dep(self.last_w.get(k), False)
            rd = self.readers.get(k)
            if rd:
                for o in rd.values():
                    dep(o, False)
        if dma:
            i = self.dma_count[eng]
            op.dma_i = i
            self.dma_count[eng] = i + 1
            self.dma_ops[eng].append(op)
            lim = NDMA_SEM if eng != "pool" else 4
            if i >= lim:
                o = self.dma_ops[eng][i - lim]
                s = self._stream(o)
                cur = deps.get(s)
                if cur is None or cur.idx < o.idx:
                    deps[s] = o
        op.deps = list(deps.values())
        for o in op.deps:
            o.signal = True
        st = self._stream(op)
        for k in reads:
            self.readers.setdefault(k, {})[st] = op
        for k in writes:
            self.last_w[k] = op
            self.readers[k] = {}
        self.ops[eng].append(op)
        return op

    def emit(self, final_ops):
        nc = self.nc
        for o in final_ops:
            o.signal = True
        with contextlib.ExitStack() as es:
            csem = {}
            for e in ("pe", "act", "dve", "pool"):
                csem[e] = es.enter_context(nc.semaphore("c_" + e))
            dsem = {}
            for e in ENGS:
                if self.dma_count[e]:
                    dsem[e] = [es.enter_context(nc.semaphore("d_%s_%d" % (e, i))) for i in range(NDMA_SEM)]
            for e in ENGS:
                c = 0
                for op in self.ops[e]:
                    if op.dma:
                        op.sigval = 16 * (op.dma_i // NDMA_SEM + 1)
                    elif op.signal:
                        c += 1
                        op.sigval = c
            block = es.enter_context(nc.Block())
            engobj = {"pe": block.tensor, "act": block.scalar, "dve": block.vector, "pool": block.gpsimd,
                      "sp": block.sync}

            def make(e):
                def body(eng):
                    waited = {}
                    for op in self.ops[e]:
                        for d in op.deps:
                            if d.dma:
                                sem = dsem[d.eng][d.dma_i % NDMA_SEM]
                                key = (d.eng, d.dma_i % NDMA_SEM)
                            else:
                                sem = csem[d.eng]
                                key = d.eng
                            if waited.get(key, 0) >= d.sigval:
                                continue
                            waited[key] = d.sigval
                            eng.wait_ge(sem, d.sigval)
                        ins = op.fn(eng)
                        if op.dma:
                            ins.then_inc(dsem[e][op.dma_i % NDMA_SEM], 16)
                        elif op.signal:
                            ins.then_inc(csem[e], 1)
                    if e == "sp":
                        for o in final_ops:
                            if o.dma:
                                eng.wait_ge(dsem[o.eng][o.dma_i % NDMA_SEM], o.sigval)
                            else:
                                eng.wait_ge(csem[o.eng], o.sigval)
                return body

            for e in ENGS:
                if self.ops[e] or e == "sp":
                    engobj[e](make(e))


NWI = 3
FQ = "sp"
INV_FREQ = (1.0 / (np.float32(10000.0) ** (np.arange(0, 32, 2, dtype=np.float32) / np.float32(32)))).astype(np.float32)
PI = float(np.pi)


def psk(b):
    return "ps%d" % b


class Builder:
    def __init__(self, S, L, do_mix=True, do_ffn=True, mix_parts=(1, 1, 1)):
        self.S = S
        self.L = L
        self.NT = S // TS
        self.NCH = S // 128
        self.do_mix = do_mix
        self.do_ffn = do_ffn
        self.mix_parts = mix_parts
        self.nc = bass.Bass("TRN2", target_bir_lowering=False)
        self.P = Prog(self.nc)
        self.es = contextlib.ExitStack()
        self.bank_i = 0
        self.reserved = set()
        self.pending_conv = []
        self.stage = 8
        self.brmask = 7

    def sb(self, name, shape, dt):
        return self.es.enter_context(self.nc.sbuf_tensor(name, shape, dt))

    def dram_in(self, name, shape, dt=F32):
        return self.nc.dram_tensor(name, list(shape), dt, kind="ExternalInput")

    def bank(self):
        while self.bank_i in self.reserved:
            self.bank_i = (self.bank_i + 1) % 8
        b = self.bank_i
        self.bank_i = (self.bank_i + 1) % 8
        return b

    def pt(self, b):
        return self.ps[b][:, 0:TS]

    def psb(self, b):
        return self.ps[b][:].bitcast(BF16)

    def mm(self, b, out, lhsT, rhs, start, stop, reads, skip=False):
        if skip:
            self.P.add("pe", lambda e: e.matmul(out, lhsT, rhs, start=start, stop=stop, skip_group_check=True), reads,
                       (psk(b),))
        else:
            self.P.add("pe", lambda e: e.matmul(out, lhsT, rhs, start=start, stop=stop), reads, (psk(b),))

    def tr(self, b, out, in_, ident, reads):
        self.P.add("pe", lambda e: e.transpose(out, in_, ident), reads, (psk(b),))

    def act(self, out, in_, func, reads, writes, bias=None, scale=None):
        kw = {}
        if bias is not None:
            kw["bias"] = bias
        if scale is not None:
            kw["scale"] = scale
        self.P.add("act", lambda e: e.activation(out=out, in_=in_, func=func, **kw), reads, writes)

    def ts(self, eng, out, in0, s1, s2, op0, op1, reads, writes):
        if op1 is None:
            self.P.add(eng, lambda e: e.tensor_scalar(out=out, in0=in0, scalar1=s1, scalar2=None, op0=op0), reads, writes)
        else:
            self.P.add(eng, lambda e: e.tensor_scalar(out=out, in0=in0, scalar1=s1, scalar2=s2, op0=op0, op1=op1),
                       reads, writes)

    def rsqrt(self, out, in_, scale, reads, writes):
        self.act(out, in_, AF.Sqrt, tuple(reads) + ("eps_t",), writes, bias=self.eps_t[:out.shape[0], :], scale=scale)
        self.P.add("dve", lambda e: e.reciprocal(out=out, in_=out), writes, writes)

    def stt(self, eng, out, in0, scalar, in1, op0, op1, reads, writes):
        self.P.add(eng, lambda e: e.scalar_tensor_tensor(out=out, in0=in0, scalar=scalar, in1=in1, op0=op0, op1=op1),
                   reads, writes)

    def tt(self, eng, out, in0, in1, op, reads, writes):
        self.P.add(eng, lambda e: e.tensor_tensor(out=out, in0=in0, in1=in1, op=op), reads, writes)

    def red(self, out, in_, reads, writes):
        self.P.add("dve", lambda e: e.tensor_reduce(out=out, in_=in_, axis=AX.X, op=ALU.add), reads, writes)

    def cp(self, eng, out, in_, reads, writes):
        if eng == "act":
            self.P.add("act", lambda e: e.copy(out=out, in_=in_), reads, writes)
        else:
            self.P.add(eng, lambda e: e.tensor_copy(out=out, in_=in_), reads, writes)

    def dma(self, q, out, in_, reads, writes, slow=False):
        if slow:
            return self.P.add(q, lambda e: e.dma_start(out=out, in_=in_, allow_slow_non_contiguous=True), reads, writes,
                              dma=True)
        return self.P.add(q, lambda e: e.dma_start(out=out, in_=in_), reads, writes, dma=True)

    def declare(self):
        nc, S, L = self.nc, self.S, self.L
        self.x_in = self.dram_in("x", (S, D))
        self.pos_in = self.dram_in("positions", (S,), I32)
        names = [
            ("ffn1_norm", (L, D)), ("ffn1_w_in", (L, D, 2 * DFF)), ("ffn1_w_out", (L, DFF, D)),
            ("mix_norm", (L, D)), ("w_in", (L, D, IN_COLS)), ("gm_v_norm", (L, 512)),
            ("gm_w_s", (L, 4, 128, 128)), ("gm_b_s", (L, 4, 128)), ("mla_q_norm", (L, 384)),
            ("mla_kv_norm", (L, 256)), ("mla_w_uq", (L, 384, 768)), ("mla_w_ukv", (L, 256, 1024)),
            ("mla_q_gain", (L, 96)), ("mla_k_gain", (L, 96)), ("ssd_conv_w", (L, 4, 1024)),
            ("ssd_conv_b", (L, 1024)), ("ssd_dt_bias", (L, 8)), ("ssd_a_log", (L, 8)), ("ssd_d", (L, 8)),
            ("ssd_norm", (L, 512)), ("w_branch", (L, 3, 512, D)), ("w_out", (L, D, D)),
            ("ffn2_norm", (L, D)), ("ffn2_w_in", (L, D, 2 * DFF)), ("ffn2_w_out", (L, DFF, D)),
        ]
        self.W = {}
        for n, shp in names:
            self.W[n] = self.dram_in(n, shp)
        self.out = nc.dram_tensor("out", [S, D], F32, kind="ExternalOutput")
        self.xscr = nc.dram_tensor("xscr", [128, NK, S], F32, kind="Internal")
        self.kscr = nc.dram_tensor("kscr", [8, 96, S], BF16, kind="Internal")
        self.vscr = nc.dram_tensor("vscr", [8, 128, S // 128, 65], BF16, kind="Internal")
        self.scr_wi, self.scr_wo, self.scr_in, self.scr_br, self.scr_o, self.scr_uq, self.scr_ukv = {}, {}, {}, {}, {}, {}, {}
        for l in range(L):
            for f in (1, 2):
                self.scr_wi[(l, f)] = nc.dram_tensor("swi_%d_%d" % (l, f), [NJ, 128, NK, 256], BF16, kind="Internal")
                self.scr_wo[(l, f)] = nc.dram_tensor("swo_%d_%d" % (l, f), [2, NJJ, 128, 2, 512], BF16, kind="Internal")
            self.scr_in[l] = nc.dram_tensor("sin_%d" % l, [13, 128, NK, 512], BF16, kind="Internal")
            self.scr_br[l] = nc.dram_tensor("sbr_%d" % l, [3, 128, 4, D], BF16, kind="Internal")
            self.scr_o[l] = nc.dram_tensor("so_%d" % l, [2, 128, NK, 512], BF16, kind="Internal")
            self.scr_uq[l] = nc.dram_tensor("suq_%d" % l, [128, 3, 768], BF16, kind="Internal")
            self.scr_ukv[l] = nc.dram_tensor("sukv_%d" % l, [128, 2, 1024], BF16, kind="Internal")

    def alloc(self):
        nc = self.nc
        sb = self.sb
        NCH = self.NCH
        self.ident_f = sb("ident_f", [128, 128], F32)
        self.ident_b = sb("ident_b", [128, 128], BF16)
        self.ones_b = sb("ones_b", [128, 128], BF16)
        self.ones_f = sb("ones_f", [128, 128], F32)
        self.U_f = sb("U_f", [128, 128], F32)
        self.U_b = sb("U_b", [128, 128], BF16)
        self.Lm_f = sb("Lm_f", [128, 128], F32)
        self.eps_t = sb("eps_t", [128, 1], F32)
        self.one_t = sb("one_t", [128, 1], F32)
        self.negpi_t = sb("negpi_t", [128, 1], F32)
        self.xTs = [sb("xT%d" % i, [128, NK, TS], F32) for i in range(2)]
        self.hT = sb("hT", [128, NK, TS], BF16)
        self.hTf = sb("hTf", [128, NK, TS], BF16)
        self.sq = [sb("sq%d" % i, [128, TS], BF16) for i in range(2)]
        self.sqf = [sb("sqf%d" % i, [128, TS], BF16) for i in range(2)]
        self.rstd = sb("rstd", [128, TS], F32)
        self.rstdf = sb("rstdf", [128, TS], F32)
        self.gains = sb("gains", [128, 2, 3, NK], F32)
        self.Rarg = sb("Rarg", [128, 16, 128], F32)
        self.xtm = self.Rarg[:].rearrange("p a b -> p (a b)").rearrange("p (c d) -> p c d", c=NC)
        self.wif = [sb("wif%d" % i, [128, NK, 256], BF16) for i in range(4)]
        self.wif_i = 0
        self.actT = sb("actT", [128, NJ, TS], BF16)
        self.sg = [sb("sg%d" % i, [128, TS], F32) for i in range(2)]
        self.wi = [sb("wi%d" % i, [128, NK, 512], BF16) for i in range(NWI)]
        self.wo = [sb("wo%d" % i, [128, 2, 512], BF16) for i in range(3)]
        self.ps = [self.es.enter_context(nc.psum_tensor("ps%d" % i, [128, 512], F32)) for i in range(8)]
        self.wi_i = self.wo_i = self.sq_i = self.sqf_i = self.sg_i = self.sig_i = self.pt_i = 0
        if not self.do_mix:
            return
        self.pos_i = sb("pos_i", [128, NCH], I32)
        self.pos_f = sb("pos_f", [128, NCH], F32)
        self.invf = sb("invf", [128, 16], F32)
        av = self.actT[:].rearrange("p j t -> p (j t)").bitcast(F32)
        n16 = NCH * 16
        self.ang = av[:, 0:n16].rearrange("p (c j) -> p c j", j=16)
        self.angf = av[:, n16:2 * n16].rearrange("p (c j) -> p c j", j=16)
        self.angi = av[:, 2 * n16:3 * n16].bitcast(I32).rearrange("p (c j) -> p c j", j=16)
        self.cosT = sb("cosT", [128, NCH, 16], F32)
        self.sinT = sb("sinT", [128, NCH, 16], F32)
        self.gvb = sb("gvb", [128, 512], F32)
        self.bsb = sb("bsb", [128, 4, 128], F32)
        self.Wsraw = sb("Wsraw", [128, 4, 128], F32)
        self.Wsm = sb("Wsm", [128, 4, 128], BF16)
        self.WsT = sb("WsT", [128, 4, 128], BF16)
        self.gqn = sb("gqn", [128, 3], F32)
        self.gkvn = sb("gkvn", [128, 2], F32)
        self.gqb = sb("gqb", [128, 96], F32)
        self.gkb = sb("gkb", [128, 96], F32)
        self.gmx = sb("gmx", [128, 2], F32)
        self.negC = sb("negC", [128, 1], F32)
        self.cw = sb("cw", [128, NK, 4], F32)
        self.cb = sb("cb", [128, NK], F32)
        self.dtb = sb("dtb", [128, 8], F32)
        self.aneg = sb("aneg", [128, 8], F32)
        self.Dsk = sb("Dsk", [128, 8], F32)
        self.gsn = sb("gsn", [128, 512], F32)
        self.uT = sb("uT", [128, 4, TS], BF16)
        self.vg = sb("vg", [128, 512], F32)
        self.junk = sb("junk", [128, 768], F32)
        self.s1 = sb("s1", [128, 8], F32)
        self.s2 = sb("s2", [128, 8], F32)
        self.vtm = sb("vtm", [128, NC, 512], BF16)
        self.zs = sb("zs", [128, NC, 512], BF16)
        self.cqnT = sb("cqnT", [128, 3, TS], BF16)
        self.ckvnT = sb("ckvnT", [128, 2, TS], BF16)
        self.sqq = sb("sqq", [128, 5, TS], BF16)
        self.krdt = sb("krdt", [128, NC, 40], F32)
        self.xbcp = [sb("xbcp%d" % i, [128, TS + 3], F32) for i in range(2)]
        self.hist = sb("hist", [128, NK, 3], F32)
        self.cacc = self.vg[:, 0:TS]
        self.xbcT = sb("xbcT", [128, NK, TS], BF16)
        self.spb = sb("spb", [128, 4, 128], F32)
        self.yaT = sb("yaT", [128, 4, TS], BF16)
        self.dt8 = sb("dt8", [128, 8], F32)
        self.da8 = sb("da8", [128, 8], F32)
        self.t8a = sb("t8a", [128, 8], F32)
        self.t8b = sb("t8b", [128, 8], F32)
        self.cs8 = sb("cs8", [128, 8], F32)
        self.tot8 = sb("tot8", [128, 8], F32)
        self.dout8 = sb("dout8", [128, 8], F32)
        self.etot8 = sb("etot8", [128, 8], F32)
        self.R = self.Rarg[:, 0:8, :]
        self.arg = self.Rarg[:, 8:16, :]
        self.ecs = sb("ecs", [128, 8, 128], F32)
        self.cbm = sb("cbm", [128, 2, 128], F32)
        self.MT = sb("MT", [128, 8, 128], BF16)
        self.Cs = sb("Cs", [128, 8, 128], BF16)
        self.xdt = sb("xdt", [128, 8, 64], BF16)
        self.xdd = sb("xdd", [128, 8, 64], BF16)
        self.xsd = sb("xsd", [128, 8, 64], F32)
        self.Btm = sb("Btm", [128, 2, 128], BF16)
        self.Sst = sb("Sst", [128, 8, 64], F32)
        self.Sbf = sb("Sbf", [128, 8, 64], BF16)
        self.y1 = self.vg
        self.yctm = sb("yctm", [128, 512], BF16)
        self.ycT = sb("ycT", [128, 4, TS], BF16)
        self.rq2 = sb("rq2", [128, 2], F32)
        self.qtm = sb("qtm", [128, 8, 96], F32)
        self.hss = sb("hss", [128, 8], F32)
        self.r1 = sb("r1", [128, 8, 16], F32)
        self.r2 = sb("r2", [128, 8, 16], F32)
        self.qr = sb("qr", [128, 8, 96], BF16)
        self.qT = sb("qT", [128, 8, TS], BF16)
        self.kst = sb("kst", [128, 8, 128], BF16)
        self.vst = sb("vst", [128, 8, 65], BF16)
        self.kh = [sb("kh%d" % i, [128, self.S], BF16) for i in range(2)]
        self.vh = [sb("vh%d" % i, [128, NCH, 65], BF16) for i in range(1)]
        self.kv_i = 0
        self.pT = [sb("pT%d" % i, [128, TS], BF16) for i in range(2)]
        self.rden = sb("rden", [128, 4], F32)
        self.ybtm = sb("ybtm", [128, NC, 512], BF16)
        self.ybT = sb("ybT", [128, 4, TS], BF16)
        self.sig = [sb("sig%d" % i, [128, TS], F32) for i in range(2)]
        self.mgacc = sb("mgacc", [128, 4, TS], F32)
        self.mgT = sb("mgT", [128, NK, TS], BF16)

    def consts(self):
        P = self.P
        idf, idb, ones_b, ones_f, U_f, U_b, Lm_f = self.ident_f, self.ident_b, self.ones_b, self.ones_f, self.U_f, self.U_b, self.Lm_f
        P.add("pool", lambda e: e.memset(idf[:], 0.0), (), ("ident_f",))
        P.add("pool", lambda e: e.affine_select(out=idf[:], in_=idf[:], pattern=[[-1, 128]], compare_op=ALU.not_equal,
                                                fill=1.0, base=0, channel_multiplier=1), ("ident_f",), ("ident_f",))
        P.add("pool", lambda e: e.tensor_copy(out=idb[:], in_=idf[:]), ("ident_f",), ("ident_b",))
        P.add("pool", lambda e: e.memset(ones_b[:], 1.0), (), ("ones_b",))
        P.add("pool", lambda e: e.memset(ones_f[:], 1.0), (), ("ones_f",))
        P.add("pool", lambda e: e.memset(U_f[:], 1.0), (), ("U_f",))
        P.add("pool", lambda e: e.affine_select(out=U_f[:], in_=U_f[:], pattern=[[1, 128]], compare_op=ALU.is_ge,
                                                fill=0.0, base=0, channel_multiplier=-1), ("U_f",), ("U_f",))
        P.add("pool", lambda e: e.tensor_copy(out=U_b[:], in_=U_f[:]), ("U_f",), ("U_b",))
        P.add("pool", lambda e: e.memset(Lm_f[:], 1.0), (), ("Lm_f",))
        P.add("pool", lambda e: e.affine_select(out=Lm_f[:], in_=Lm_f[:], pattern=[[-1, 128]], compare_op=ALU.is_ge,
                                                fill=0.0, base=0, channel_multiplier=1), ("Lm_f",), ("Lm_f",))
        eps_t, one_t, negpi_t = self.eps_t, self.one_t, self.negpi_t
        P.add("pool", lambda e: e.memset(eps_t[:], EPS), (), ("eps_t",))
        P.add("pool", lambda e: e.memset(one_t[:], 1.0), (), ("one_t",))
        P.add("pool", lambda e: e.memset(negpi_t[:], -PI), (), ("negpi_t",))
        if not self.do_mix:
            return
        invf = self.invf
        for j in range(16):
            P.add("pool", lambda e, j=j: e.memset(invf[:, j:j + 1], float(INV_FREQ[j])), (), ("invf",))
        NCH = self.NCH
        self.dma("sp", self.pos_i[:], self.pos_in.ap().rearrange("(c p) -> p c", p=128), (), ("pos_i",), slow=True)
        self.cp("dve", self.pos_f[:], self.pos_i[:], ("pos_i",), ("pos_f",))
        self.tt("dve", self.ang, self.pos_f[:, :, None].broadcast_to([128, NCH, 16]),
                self.invf[:, None, :].broadcast_to([128, NCH, 16]), ALU.mult, ("pos_f", "invf"), ("ang",))
        for dst, dk, shift in ((self.sinT, "sinT", 0.0), (self.cosT, "cosT", 0.5 * PI)):
            self.ts("dve", dst[:], self.ang, shift, 1.0 / (2 * PI), ALU.add, ALU.mult, ("ang",), (dk,))
            self.cp("dve", self.angi, dst[:], (dk,), ("angi",))
            self.cp("dve", self.angf, self.angi, ("angi",), ("angf",))
            self.ts("dve", dst[:], self.ang, shift, None, ALU.add, None, ("ang",), (dk,))
            self.stt("dve", dst[:], self.angf, -2 * PI, dst[:], ALU.mult, ALU.add, ("angf", dk), (dk,))
            self.ts("dve", self.angf, dst[:], PI, 2 * PI, ALU.is_gt, ALU.mult, (dk,), ("angf",))
            self.tt("dve", dst[:], dst[:], self.angf, ALU.subtract, (dk, "angf"), (dk,))
            self.ts("dve", self.angf, dst[:], -PI, 2 * PI, ALU.is_lt, ALU.mult, (dk,), ("angf",))
            self.tt("dve", dst[:], dst[:], self.angf, ALU.add, (dk, "angf"), (dk,))
            self.act(dst[:], dst[:], AF.Sin, (dk,), (dk,))

    def conv_list(self, l):
        lst = []

        def add(out, in_, key):
            lst.append(lambda: self.dma("pool", out, in_, (), (key,)))

        if self.do_ffn:
            for f in (1, 2):
                w_in = self.W["ffn%d_w_in" % f][l].rearrange("(k p) c -> p k c", p=128)
                w_out = self.W["ffn%d_w_out" % f][l].rearrange("(jj j2 p) c -> jj p j2 c", j2=2, p=128)
                swi, swo = self.scr_wi[(l, f)], self.scr_wo[(l, f)]
                for j in range(NJ):
                    for h in range(2):
                        add(swi[j][:, :, h * 128:(h + 1) * 128],
                            w_in[:, :, h * DFF + j * 128: h * DFF + (j + 1) * 128], ("swi", l, f, j, h))
                for mg in range(2):
                    for jj in range(NJJ):
                        add(swo[mg, jj], w_out[jj][:, :, mg * 512:(mg + 1) * 512], ("swo", l, f, mg, jj))
        if self.do_mix:
            w = self.W["w_in"][l].rearrange("(k p) c -> p k c", p=128)
            sin = self.scr_in[l]
            srcs = {0: (0, 512), 1: (512, 1024), 2: (1024, 1536), 4: (1696, 2208), 5: (2208, 2720), 6: (2720, 3232)}
            for i in range(6):
                srcs[7 + i] = (3240 + 512 * i, 3240 + 512 * (i + 1))
            for bi, (a, b_) in srcs.items():
                add(sin[bi], w[:, :, a:b_], ("sin", l, bi))
            add(sin[3][:, :, 0:160], w[:, :, 1536:1696], ("sin", l, 3, 0))
            add(sin[3][:, :, 160:168], w[:, :, 3232:3240], ("sin", l, 3, 1))
            for i in range(3):
                add(self.scr_br[l][i], self.W["w_branch"][l, i].rearrange("(kk p) c -> p kk c", p=128), ("sbr", l, i))
            wo_ = self.W["w_out"][l].rearrange("(k p) c -> p k c", p=128)
            for hf in range(2):
                add(self.scr_o[l][hf], wo_[:, :, hf * 512:(hf + 1) * 512], ("so", l, hf))
            add(self.scr_uq[l][:], self.W["mla_w_uq"][l].rearrange("(i p) c -> p i c", p=128), ("suq", l))
            add(self.scr_ukv[l][:], self.W["mla_w_ukv"][l].rearrange("(i p) c -> p i c", p=128), ("sukv", l))
        return lst

    def load_gains(self, l):
        W = self.W
        for i, n in enumerate(("ffn1_norm", "mix_norm", "ffn2_norm")):
            self.dma("sp", self.gains[:, l % 2, i, :], W[n][l].rearrange("(k p) -> p k", p=128), (), (("gains", l % 2),),
                     slow=True)

    def load_params(self, l):
        W = self.W

        def bc(ap, shape):
            return ap.broadcast_to(shape)

        self.dma("sp", self.gvb[:], bc(W["gm_v_norm"][l:l + 1, :], [128, 512]), (), ("gvb",), slow=True)
        self.dma("sp", self.bsb[:], bc(W["gm_b_s"][l:l + 1], [128, 4, 128]), (), ("bsb",), slow=True)
        self.dma("sp", self.Wsraw[:], W["gm_w_s"][l].rearrange("g t s -> t g s"), (), ("Wsraw",))
        self.dma("sp", self.gqn[:], W["mla_q_norm"][l].rearrange("(i p) -> p i", p=128), (), ("gqn",), slow=True)
        self.dma("sp", self.gkvn[:], W["mla_kv_norm"][l].rearrange("(i p) -> p i", p=128), (), ("gkvn",), slow=True)
        self.dma("sp", self.gqb[:], bc(W["mla_q_gain"][l:l + 1, :], [128, 96]), (), ("gqb",), slow=True)
        self.dma("sp", self.gkb[:], bc(W["mla_k_gain"][l:l + 1, :], [128, 96]), (), ("gkb",), slow=True)
        for k in range(4):
            self.dma("sp", self.cw[:, :, k], W["ssd_conv_w"][l, k].rearrange("(c p) -> p c", p=128), (), ("cw",), slow=True)
        self.dma("sp", self.cb[:], W["ssd_conv_b"][l].rearrange("(c p) -> p c", p=128), (), ("cb",), slow=True)
        self.dma("sp", self.dtb[:], bc(W["ssd_dt_bias"][l:l + 1, :], [128, 8]), (), ("dtb",), slow=True)
        self.dma("sp", self.aneg[:], bc(W["ssd_a_log"][l:l + 1, :], [128, 8]), (), ("aneg",), slow=True)
        self.dma("sp", self.Dsk[:], bc(W["ssd_d"][l:l + 1, :], [128, 8]), (), ("Dsk",), slow=True)
        self.dma("sp", self.gsn[:], bc(W["ssd_norm"][l:l + 1, :], [128, 512]), (), ("gsn",), slow=True)
        self.act(self.aneg[:], self.aneg[:], AF.Exp, ("aneg",), ("aneg",))
        self.ts("dve", self.aneg[:], self.aneg[:], -1.0, None, ALU.mult, None, ("aneg",), ("aneg",))
        self.tt("dve", self.Wsm[:], self.Wsraw[:], self.Lm_f[:, None, :].broadcast_to([128, 4, 128]), ALU.mult,
                ("Wsraw", "Lm_f"), ("Wsm",))
        b = self.bank()
        for g in range(4):
            self.tr(b, self.psb(b)[:, g * 128:(g + 1) * 128], self.Wsm[:, g, :], self.ident_b[:], ("Wsm", "ident_b"))
        self.cp("dve", self.WsT[:], self.psb(b)[:, 0:512].rearrange("p (g t) -> p g t", g=4), (psk(b),), ("WsT",))
        self.P.add("dve", lambda e: e.tensor_reduce(out=self.gmx[:, 0:1], in_=self.gqb[:], axis=AX.X, op=ALU.max,
                                                    apply_absolute_value=True), ("gqb",), ("gmx",))
        self.P.add("dve", lambda e: e.tensor_reduce(out=self.gmx[:, 1:2], in_=self.gkb[:], axis=AX.X, op=ALU.max,
                                                    apply_absolute_value=True), ("gkb",), ("gmx",))
        self.stt("dve", self.negC[:], self.gmx[:, 0:1], -float(np.sqrt(96.0)), self.gmx[:, 1:2], ALU.mult, ALU.mult,
                 ("gmx",), ("negC",))
        self.P.add("dve", lambda e: e.memset(self.Sst[:], 0.0), (), ("Sst",))
        self.P.add("dve", lambda e: e.memset(self.Sbf[:], 0.0), (), ("Sbf",))
        self.P.add("dve", lambda e: e.memset(self.hist[:], 0.0), (), ("hist",))
        if l == 0:
            self.P.add("dve", lambda e: e.memset(self.vst[:, :, 64:65], 1.0), (), ("vst1",))

    def load_x(self, l, I, x, xk):
        if l > 0:
            self.dma("sp", x[:], self.xscr[:, :, I * TS:(I + 1) * TS], (("xscr", I),), (xk,))
            return
        XK = ("R", "arg")
        src = self.x_in[I * TS:(I + 1) * TS, :].rearrange("(c p) d -> p c d", p=128)
        self.dma("sp", self.xtm, src, (), XK)
        for k in range(NK):
            b = self.bank()
            for c in range(NC):
                self.tr(b, self.ps[b][:, c * 128:(c + 1) * 128], self.xtm[:, c, k * 128:(k + 1) * 128], self.ident_f[:],
                        XK + ("ident_f",))
            eng = "dve" if k % 2 == 0 else "act"
            self.cp(eng, x[:, k, :], self.pt(b), (psk(b),), (xk,))

    def store_x(self, l, I, x, xk):
        if l < self.L - 1:
            return [self.dma("sp", self.xscr[:, :, I * TS:(I + 1) * TS], x[:], (xk,), (("xscr", I),))]
        XK = ("R", "arg")
        for c in range(NC):
            for hlf in range(2):
                b = self.bank()
                for kk in range(4):
                    k = hlf * 4 + kk
                    self.tr(b, self.ps[b][:, kk * 128:(kk + 1) * 128], x[:, k, c * 128:(c + 1) * 128],
                            self.ident_f[:], (xk, "ident_f"))
                eng = "dve" if hlf == 0 else "act"
                self.cp(eng, self.xtm[:, c, hlf * 512:(hlf + 1) * 512], self.ps[b][:], (psk(b),), XK)
        dst = self.out[I * TS:(I + 1) * TS, :].rearrange("(c p) d -> p c d", p=128)
        return [self.dma("sp", dst, self.xtm, XK, ())]

    def rmsnorm_T(self, l, gi, x, xk, ffn):
        hT, hname = (self.hTf, "hTf") if ffn else (self.hT, "hT")
        rstd, rk = (self.rstdf, "rstdf") if ffn else (self.rstd, "rstd")
        b = self.bank()
        for k in range(NK):
            if ffn:
                sq, sqk = self.sqf[self.sqf_i % 2], "sqf%d" % (self.sqf_i % 2)
                self.sqf_i += 1
            else:
                sq, sqk = self.sq[self.sq_i % 2], "sq%d" % (self.sq_i % 2)
                self.sq_i += 1
            self.act(sq[:], x[:, k, :], AF.Square, (xk,), (sqk,))
            self.mm(b, self.pt(b), self.ones_b[:], sq[:], k == 0, k == NK - 1, (sqk, "ones_b"))
        self.rsqrt(rstd[:], self.pt(b), 1.0 / D, (psk(b),), (rk,))
        for k in range(NK):
            self.stt("dve", hT[:, k, :], x[:, k, :], self.gains[:, l % 2, gi, k:k + 1], rstd[:], ALU.mult, ALU.mult,
                     (xk, ("gains", l % 2), rk), ((hname, k),))

    def ffn(self, l, f, x, xk):
        gi = 0 if f == 1 else 2
        self.rmsnorm_T(l, gi, x, xk, True)
        yield
        swi, swo = self.scr_wi[(l, f)], self.scr_wo[(l, f)]
        for j in range(NJ):
            wi = self.wif[self.wif_i % 4]
            wik = "wif%d" % (self.wif_i % 4)
            self.wif_i += 1
            self.dma(FQ, wi[:], swi[j], (("swi", l, f, j, 0), ("swi", l, f, j, 1)), (wik,))
            bg, bu = self.bank(), self.bank()
            for (b, off) in ((bg, 0), (bu, 128)):
                for k in range(NK):
                    self.mm(b, self.pt(b), wi[:, k, off:off + 128], self.hTf[:, k, :],
                            k == 0, k == NK - 1, (wik, ("hTf", k)))
            sg = self.sg[self.sg_i % 2]
            sgk = "sg%d" % (self.sg_i % 2)
            self.sg_i += 1
            self.act(sg[:], self.pt(bg), AF.Silu, (psk(bg),), (sgk,))
            self.tt("dve", self.actT[:, j, :], sg[:], self.pt(bu), ALU.mult, (sgk, psk(bu)), (("actT", j),))
            yield
        for mg in range(2):
            banks = [self.bank() for _ in range(4)]
            self.reserved.update(banks)
            for jj in range(NJJ):
                wo = self.wo[self.wo_i % 3]
                wok = "wo%d" % (self.wo_i % 3)
                self.wo_i += 1
                self.dma(FQ, wo[:], swo[mg, jj], (("swo", l, f, mg, jj),), (wok,))
                for j2 in range(2):
                    j = jj * 2 + j2
                    for m in range(4):
                        b = banks[m]
                        self.mm(b, self.pt(b), wo[:, j2, m * 128:(m + 1) * 128], self.actT[:, j, :],
                                j == 0, j == NJ - 1, (wok, ("actT", j)))
                yield
            for m in range(4):
                b = banks[m]
                k = mg * 4 + m
                self.stt("dve", x[:, k, :], self.pt(b), 0.5, x[:, k, :], ALU.mult, ALU.add, (psk(b), xk), (xk,))
            self.reserved.difference_update(banks)
            yield
    def load_blk(self, src, key, ncols=512, nk=NK):
        wi = self.wi[self.wi_i % NWI]
        wik = "wi%d" % (self.wi_i % NWI)
        self.wi_i += 1
        self.dma("sp", wi[:, 0:nk, 0:ncols], src, key, (wik,))
        return wi, wik

    def load_flat(self, src, key, n_i, n_c):
        wi = self.wi[self.wi_i % NWI]
        wik = "wi%d" % (self.wi_i % NWI)
        self.wi_i += 1
        v = wi[:].rearrange("p k c -> p (k c)")[:, 0:n_i * n_c].rearrange("p (i c) -> p i c", i=n_i)
        self.dma("sp", v, src, key, (wik,))
        return v, wik

    def fm_chunk(self, wi, wik, col0, nk=NK, rhs=None, rkey=None):
        b = self.bank()
        for k in range(nk):
            r = self.hT[:, k, :] if rhs is None else rhs[:, k, :]
            rk = ("hT", k) if rhs is None else rkey
            self.mm(b, self.pt(b), wi[:, k, col0:col0 + 128], r, k == 0, k == nk - 1, (wik, rk))
        return b

    def tm_chunk(self, wi, wik, c, col0, ncols):
        b = self.bank()
        for k in range(NK):
            self.mm(b, self.ps[b][:, 0:ncols], self.hT[:, k, c * 128:(c + 1) * 128], wi[:, k, col0:col0 + ncols],
                    k == 0, k == NK - 1, (wik, ("hT", k)))
        return b

    def head_norm_rope(self, cg, gb, gbk, dst_T, dst_key, col0):
        q3 = self.qtm[:]
        self.tt("dve", self.junk[:].rearrange("p (h d) -> p h d", h=8), q3, q3, ALU.mult, ("qtm",), ("junk",))
        self.red(self.hss[:], self.junk[:].rearrange("p (h d) -> p h d", h=8), ("junk",), ("hss",))
        self.rsqrt(self.hss[:], self.hss[:], 1.0 / 96, ("hss",), ("hss",))
        self.tt("dve", q3, q3, self.hss[:, :, None].broadcast_to([128, 8, 96]), ALU.mult, ("qtm", "hss"), ("qtm",))
        self.tt("dve", q3, q3, gb[:, None, :].broadcast_to([128, 8, 96]), ALU.mult, ("qtm", gbk), ("qtm",))
        cos = self.cosT[:, cg, :][:, None, :].broadcast_to([128, 8, 16])
        sin = self.sinT[:, cg, :][:, None, :].broadcast_to([128, 8, 16])
        x1, x2 = self.qtm[:, :, 64:80], self.qtm[:, :, 80:96]
        self.cp("act", self.qr[:, :, 0:64], self.qtm[:, :, 0:64], ("qtm",), ("qr",))
        self.tt("dve", self.r1[:], x1, cos, ALU.mult, ("qtm", "cosT"), ("r1",))
        self.tt("dve", self.r2[:], x2, sin, ALU.mult, ("qtm", "sinT"), ("r2",))
        self.tt("dve", self.qr[:, :, 64:80], self.r1[:], self.r2[:], ALU.subtract, ("r1", "r2"), ("qr",))
        self.tt("dve", self.r1[:], x2, cos, ALU.mult, ("qtm", "cosT"), ("r1",))
        self.tt("dve", self.r2[:], x1, sin, ALU.mult, ("qtm", "sinT"), ("r2",))
        self.tt("dve", self.qr[:, :, 80:96], self.r1[:], self.r2[:], ALU.add, ("r1", "r2"), ("qr",))
        yield
        for g in range(2):
            b = self.bank()
            pb = self.psb(b)
            for hh in range(4):
                self.tr(b, pb[0:96, hh * 128:(hh + 1) * 128], self.qr[:, g * 4 + hh, :], self.ident_b[:], ("qr", "ident_b"))
            self.cp("act" if g else "dve", dst_T[0:96, g * 4:g * 4 + 4, col0:col0 + 128],
                    pb[0:96, 0:512].rearrange("p (h t) -> p h t", h=4), (psk(b),), (dst_key,))

    def mixer(self, l, I, x, xk):
        sin_ = self.scr_in[l]
        self.rmsnorm_T(l, 1, x, xk, False)
        yield
        wi, wik = self.load_blk(sin_[0], (("sin", l, 0),))
        for ch in range(4):
            b = self.fm_chunk(wi, wik, ch * 128)
            self.act(self.uT[:, ch, :], self.pt(b), AF.Gelu, (psk(b),), ("uT",))
        yield
        wi, wik = self.load_blk(sin_[1], (("sin", l, 1),))
        for c in range(NC):
            b = self.tm_chunk(wi, wik, c, 0, 512)
            self.act(self.vg[:], self.ps[b][:], AF.Gelu, (psk(b),), ("vg",))
            self.act(self.junk[:, 0:512], self.vg[:], AF.Square, ("vg",), ("junk",))
            self.red(self.s1[:, 0:1], self.junk[:, 0:512], ("junk",), ("s1",))
            self.rsqrt(self.s1[:, 0:1], self.s1[:, 0:1], 1.0 / 512, ("s1",), ("s1",))
            self.stt("dve", self.vtm[:, c, :], self.vg[:], self.s1[:, 0:1], self.gvb[:], ALU.mult, ALU.mult,
                     ("vg", "s1", "gvb"), (("vtm", c),))
        yield
        wi2, wi2k = self.load_blk(sin_[2], (("sin", l, 2),))
        wi3, wi3k = self.load_blk(sin_[3][:, :, 0:168], (("sin", l, 3, 0), ("sin", l, 3, 1)), ncols=168)
        for i in range(5):
            if i < 4:
                b = self.fm_chunk(wi2, wi2k, i * 128)
            else:
                b = self.fm_chunk(wi3, wi3k, 0)
            self.act(self.sqq[:, i, :], self.pt(b), AF.Square, (psk(b),), ("sqq",))
            if i < 3:
                self.ts("dve", self.cqnT[:, i, :], self.pt(b), self.gqn[:, i:i + 1], None, ALU.mult, None,
                        (psk(b), "gqn"), ("cqnT",))
            else:
                self.ts("dve", self.ckvnT[:, i - 3, :], self.pt(b), self.gkvn[:, i - 3:i - 2], None, ALU.mult, None,
                        (psk(b), "gkvn"), ("ckvnT",))
        for c in range(NC):
            b = self.tm_chunk(wi3, wi3k, c, 128, 40)
            self.cp("dve", self.krdt[:, c, :], self.ps[b][:, 0:40], (psk(b),), ("krdt",))
        yield
        wi, wik = self.load_blk(sin_[4], (("sin", l, 4),))
        for c in range(NC):
            b = self.tm_chunk(wi, wik, c, 0, 512)
            self.act(self.zs[:, c, :], self.ps[b][:], AF.Silu, (psk(b),), ("zs",))
        for blk in range(2):
            yield
            wi, wik = self.load_blk(sin_[5 + blk], (("sin", l, 5 + blk),))
            for cc in range(4):
                ch = blk * 4 + cc
                b = self.fm_chunk(wi, wik, cc * 128)
                xp = self.xbcp[ch % 2]
                xk = "xbcp%d" % (ch % 2)
                self.cp("act", xp[:, 3:TS + 3], self.pt(b), (psk(b),), (xk,))
                self.cp("dve", xp[:, 0:3], self.hist[:, ch, :], ("hist",), (xk,))
                self.ts("dve", self.cacc, xp[:, 3:TS + 3], self.cw[:, ch, 3:4], self.cb[:, ch:ch + 1], ALU.mult, ALU.add,
                        (xk, "cw", "cb"), ("vg",))
                for t in range(3):
                    self.stt("dve", self.cacc, xp[:, t:TS + t], self.cw[:, ch, t:t + 1], self.cacc, ALU.mult, ALU.add,
                             (xk, "cw", "vg"), ("vg",))
                self.cp("dve", self.hist[:, ch, :], xp[:, TS:TS + 3], (xk,), ("hist",))
                self.act(self.xbcT[:, ch, :], self.cacc, AF.Silu, ("vg",), ("xbcT",))
                yield
        wq = self.load_flat(self.scr_uq[l][:], (("suq", l),), 3, 768)
        wkv = self.load_flat(self.scr_ukv[l][:], (("sukv", l),), 2, 1024)
        for c in range(NC):
            yield from self.mla_chunk(c, I * NC + c, slice(c * 128, (c + 1) * 128), wq, wkv)
            yield

        def side():
            for c in range(NC):
                cg = I * NC + c
                ck = slice(c * 128, (c + 1) * 128)
                b = self.bank()
                for g in range(4):
                    self.mm(b, self.ps[b][:, g * 128:(g + 1) * 128], self.vtm[:, c, g * 128:(g + 1) * 128], self.WsT[:, g, :],
                            True, True, (("vtm", c), "WsT"))
                self.tt("dve", self.spb[:], self.ps[b][:].rearrange("p (g t) -> p g t", g=4), self.bsb[:], ALU.add,
                        (psk(b), "bsb"), ("spb",))
                self.tt("dve", self.yaT[:, :, ck], self.spb[:], self.uT[:, :, ck], ALU.mult, ("spb", "uT"), ("yaT",))
                yield from self.ssd_chunk(c, cg, ck)

        att = self.attention(I)
        sd = side()
        n_att = 8 * (NC * I + NC) + 8
        n_side = NC * 12
        acc = 0.0
        att_done = side_done = False
        while not (att_done and side_done):
            if not att_done:
                try:
                    next(att)
                except StopIteration:
                    att_done = True
            yield
            acc += n_side / n_att
            while (acc >= 1.0 or att_done) and not side_done:
                acc -= 1.0
                try:
                    next(sd)
                except StopIteration:
                    side_done = True
        yield from self.merge(l, x, xk)

    def ssd_chunk(self, c, cg, ck):
        xs = self.xbcT
        self.tt("dve", self.t8a[:], self.krdt[:, c, 32:40], self.dtb[:], ALU.add, ("krdt", "dtb"), ("t8a",))
        self.ts("dve", self.t8b[:], self.t8a[:], -1.0, None, ALU.mult, None, ("t8a",), ("t8b",))
        self.tt("dve", self.t8b[:], self.t8b[:], self.t8a[:], ALU.max, ("t8a", "t8b"), ("t8b",))
        self.act(self.t8b[:], self.t8b[:], AF.Exp, ("t8b",), ("t8b",), scale=-1.0)
        self.act(self.t8b[:], self.t8b[:], AF.Ln, ("t8b", "one_t"), ("t8b",), bias=self.one_t[:])
        self.stt("dve", self.dt8[:], self.t8a[:], 0.0, self.t8b[:], ALU.max, ALU.add, ("t8a", "t8b"), ("dt8",))
        self.tt("dve", self.da8[:], self.dt8[:], self.aneg[:], ALU.mult, ("dt8", "aneg"), ("da8",))
        yield
        b = self.bank()
        self.mm(b, self.ps[b][:, 0:8], self.U_f[:], self.da8[:], True, True, ("U_f", "da8"))
        self.cp("dve", self.cs8[:], self.ps[b][:, 0:8], (psk(b),), ("cs8",))
        self.tt("dve", self.R[:], self.U_f[:, None, :].broadcast_to([128, 8, 128]),
                self.da8[:, :, None].broadcast_to([128, 8, 128]), ALU.mult, ("U_f", "da8"), ("R",))
        yield
        bb = [self.bank(), self.bank()]
        for hh in range(2):
            self.mm(bb[hh], self.ps[bb[hh]][:], self.ones_f[:], self.R[:, hh * 4:hh * 4 + 4, :].rearrange("p h l -> p (h l)"),
                    True, True, ("ones_f", "R"))
        for hh in range(2):
            p3 = self.ps[bb[hh]][:].rearrange("p (h l) -> p h l", h=4)
            hs = slice(hh * 4, hh * 4 + 4)
            self.tt("dve", self.arg[:, hs, :], p3, self.cs8[:, hs, None].broadcast_to([128, 4, 128]), ALU.subtract,
                    (psk(bb[hh]), "cs8"), ("arg",))
            self.act(self.ecs[:, hs, :], p3, AF.Exp, (psk(bb[hh]),), ("ecs",))
            self.cp("dve", self.tot8[:, hs], p3[:, :, 127], (psk(bb[hh]),), ("tot8",))
        self.act(self.arg[:], self.arg[:], AF.Relu, ("arg",), ("arg",), scale=-1.0)
        self.act(self.arg[:], self.arg[:], AF.Exp, ("arg",), ("arg",), scale=-1.0)
        self.tt("dve", self.t8a[:], self.tot8[:], self.cs8[:], ALU.subtract, ("tot8", "cs8"), ("t8a",))
        self.act(self.dout8[:], self.t8a[:], AF.Exp, ("t8a",), ("dout8",))
        self.act(self.etot8[:], self.tot8[:], AF.Exp, ("tot8",), ("etot8",))
        yield
        b = self.bank()
        for g in range(2):
            self.mm(b, self.ps[b][:, g * 128:(g + 1) * 128], xs[:, 4 + g, ck], xs[:, 6 + g, ck], True, True, ("xbcT",))
        self.tt("dve", self.cbm[:], self.ps[b][:, 0:256].rearrange("p (g l) -> p g l", g=2),
                self.U_f[:, None, :].broadcast_to([128, 2, 128]), ALU.mult, (psk(b), "U_f"), ("cbm",))
        self.tt("dve", self.MT[:].rearrange("p (g r) l -> p g r l", g=2), self.arg[:].rearrange("p (g r) l -> p g r l", g=2),
                self.cbm[:, :, None, :].broadcast_to([128, 2, 4, 128]), ALU.mult, ("arg", "cbm"), ("MT",))
        self.tt("dve", self.Cs[:].rearrange("p (g r) l -> p g r l", g=2), self.ecs[:].rearrange("p (g r) l -> p g r l", g=2),
                xs[:, 6:8, ck][:, :, None, :].broadcast_to([128, 2, 4, 128]), ALU.mult, ("ecs", "xbcT"), ("Cs",))
        yield
        b = self.bank()
        pb = self.psb(b)
        for i in range(4):
            self.tr(b, pb[:, i * 128:(i + 1) * 128], xs[:, i, ck], self.ident_b[:], ("xbcT", "ident_b"))
        x3 = pb[:, 0:512].rearrange("p (h d) -> p h d", h=8)
        self.tt("dve", self.xdt[:], x3, self.dt8[:, :, None].broadcast_to([128, 8, 64]), ALU.mult, (psk(b), "dt8"), ("xdt",))
        self.tt("dve", self.xsd[:], x3, self.Dsk[:, :, None].broadcast_to([128, 8, 64]), ALU.mult, (psk(b), "Dsk"), ("xsd",))
        self.tt("dve", self.xdd[:], self.xdt[:], self.dout8[:, :, None].broadcast_to([128, 8, 64]), ALU.mult,
                ("xdt", "dout8"), ("xdd",))
        yield
        b = self.bank()
        pb = self.psb(b)
        for g in range(2):
            self.tr(b, pb[:, g * 128:(g + 1) * 128], xs[:, 4 + g, ck], self.ident_b[:], ("xbcT", "ident_b"))
        self.cp("act", self.Btm[:], pb[:, 0:256].rearrange("p (g n) -> p g n", g=2), (psk(b),), ("Btm",))
        yield
        b = self.bank()
        for h in range(8):
            o = self.ps[b][:, h * 64:(h + 1) * 64]
            self.mm(b, o, self.MT[:, h, :], self.xdt[:, h, :], True, False, ("MT", "xdt"))
            self.mm(b, o, self.Cs[:, h, :], self.Sbf[:, h, :], False, True, ("Cs", "Sbf"))
        self.tt("dve", self.y1[:], self.ps[b][:], self.xsd[:].rearrange("p h d -> p (h d)"), ALU.add, (psk(b), "xsd"), ("vg",))
        self.tt("dve", self.y1[:], self.y1[:], self.zs[:, c, :], ALU.mult, ("vg", "zs"), ("vg",))
        self.act(self.junk[:, 0:512], self.y1[:], AF.Square, ("vg",), ("junk",))
        self.red(self.s2[:, 0:2], self.junk[:, 0:512].rearrange("p (g d) -> p g d", g=2), ("junk",), ("s2",))
        self.rsqrt(self.s2[:, 0:2], self.s2[:, 0:2], 1.0 / 256, ("s2",), ("s2",))
        for g in range(2):
            gs = slice(g * 256, (g + 1) * 256)
            self.stt("dve", self.yctm[:, gs], self.y1[:, gs], self.s2[:, g:g + 1], self.gsn[:, gs], ALU.mult, ALU.mult,
                     ("vg", "s2", "gsn"), ("yctm",))
        yield
        b = self.bank()
        pb = self.psb(b)
        for i in range(4):
            self.tr(b, pb[:, i * 128:(i + 1) * 128], self.yctm[:, i * 128:(i + 1) * 128], self.ident_b[:], ("yctm", "ident_b"))
        self.cp("act", self.ycT[:, :, ck], pb[:, 0:512].rearrange("p (i t) -> p i t", i=4), (psk(b),), ("ycT",))
        yield
        b = self.bank()
        for h in range(8):
            self.mm(b, self.ps[b][:, h * 64:(h + 1) * 64], self.Btm[:, h // 4, :], self.xdd[:, h, :], True, True, ("Btm", "xdd"))
        self.tt("dve", self.Sst[:], self.Sst[:], self.etot8[:, :, None].broadcast_to([128, 8, 64]), ALU.mult,
                ("Sst", "etot8"), ("Sst",))
        self.tt("dve", self.Sst[:], self.Sst[:], self.ps[b][:].rearrange("p (h d) -> p h d", h=8), ALU.add,
                ("Sst", psk(b)), ("Sst",))
        self.cp("dve", self.Sbf[:], self.Sst[:], ("Sst",), ("Sbf",))
        yield

    def mla_chunk(self, c, cg, ck, wq, wkv):
        Wuq, Wuqk = wq
        Wukv, Wukvk = wkv
        b = self.bank()
        for i in range(3):
            self.mm(b, self.ps[b][:, 0:1], self.sqq[:, i, ck], self.ones_b[:, 0:1], i == 0, i == 2, ("sqq", "ones_b"))
        for i in range(2):
            self.mm(b, self.ps[b][:, 1:2], self.sqq[:, 3 + i, ck], self.ones_b[:, 0:1], i == 0, i == 1, ("sqq", "ones_b"))
        self.rsqrt(self.rq2[:, 0:1], self.ps[b][:, 0:1], 1.0 / 384, (psk(b),), ("rq2",))
        self.rsqrt(self.rq2[:, 1:2], self.ps[b][:, 1:2], 1.0 / 256, (psk(b),), ("rq2",))
        for half in range(2):
            b = self.bank()
            for i in range(3):
                self.mm(b, self.ps[b][:, 0:384], self.cqnT[:, i, ck], Wuq[:, i, half * 384:(half + 1) * 384], i == 0,
                        i == 2, ("cqnT", Wuqk))
            self.ts("dve", self.qtm[:, half * 4:half * 4 + 4, :], self.ps[b][:, 0:384].rearrange("p (h d) -> p h d", h=4),
                    self.rq2[:, 0:1], None, ALU.mult, None, (psk(b), "rq2"), ("qtm",))
        yield from self.head_norm_rope(cg, self.gqb, "gqb", self.qT, "qT", c * 128)
        yield
        for half in range(2):
            b = self.bank()
            for i in range(2):
                self.mm(b, self.ps[b][:], self.ckvnT[:, i, ck], Wukv[:, i, half * 512:(half + 1) * 512], i == 0, i == 1,
                        ("ckvnT", Wukvk))
            p3 = self.ps[b][:].rearrange("p (h d) -> p h d", h=4)
            hs = slice(half * 4, half * 4 + 4)
            self.ts("dve", self.vst[:, hs, 0:64], p3[:, :, 64:128], self.rq2[:, 1:2], None, ALU.mult, None,
                    (psk(b), "rq2"), ("vst",))
            self.ts("dve", self.qtm[:, hs, 0:64], p3[:, :, 0:64], self.rq2[:, 1:2], None, ALU.mult, None,
                    (psk(b), "rq2"), ("qtm",))
        self.cp("dve", self.qtm[:, :, 64:96], self.krdt[:, c, 0:32][:, None, :].broadcast_to([128, 8, 32]), ("krdt",), ("qtm",))
        self.dma("sp", self.vscr[:, :, cg, :].rearrange("h p d -> p h d"), self.vst[:], ("vst", "vst1"), (("vscr", cg),))
        yield from self.head_norm_rope(cg, self.gkb, "gkb", self.kst, "kst", 0)
        self.dma("sp", self.kscr[:, :, cg * 128:(cg + 1) * 128].rearrange("h d t -> d h t"), self.kst[0:96, :, :], ("kst",),
                 (("kscr", cg),))

    def attention(self, I):
        scale = 96.0 ** -0.5
        nj = NC * I + NC
        st = {}

        def prep(h):
            kh = self.kh[self.kv_i % 2]
            kk = "kh%d" % (self.kv_i % 2)
            self.kv_i += 1
            vh, vk = self.vh[0], "vh0"
            skeys = tuple(("kscr", j) for j in range(nj))
            vkeys = tuple(("vscr", j) for j in range(nj))
            self.dma("sp", kh[0:96, 0:nj * 128], self.kscr[h, :, 0:nj * 128], skeys, (kk,))
            st[h] = [kh, kk, vh, vk, None]
            return vkeys

        def load_v(h, vkeys):
            self.dma("sp", st[h][2][:, 0:nj, :], self.vscr[h, :, 0:nj, :], vkeys, (st[h][3],))

        def A(h, j):
            kh, kk = st[h][0], st[h][1]
            c0 = max(0, j - NC * I)
            bs = self.bank()
            self.mm(bs, self.ps[bs][:, c0 * 128:TS], kh[0:96, j * 128:(j + 1) * 128], self.qT[0:96, h, c0 * 128:TS],
                    True, True, (kk, "qT"))
            pT = self.pT[self.pt_i % 2]
            pk = "pT%d" % (self.pt_i % 2)
            self.pt_i += 1
            self.act(pT[:, c0 * 128:TS], self.ps[bs][:, c0 * 128:TS], AF.Exp, (psk(bs), "negC"), (pk,),
                     bias=self.negC[:], scale=scale)
            if j >= NC * I:
                self.tt("dve", pT[:, c0 * 128:(c0 + 1) * 128], pT[:, c0 * 128:(c0 + 1) * 128], self.U_b[:], ALU.mult,
                        (pk, "U_b"), (pk,))
            return pT, pk, c0

        def B(h, j, pT, pk, c0):
            vh, vk = st[h][2], st[h][3]
            if j == 0:
                st[h][4] = self.bank()
                self.reserved.add(st[h][4])
            bacc = st[h][4]
            for c in range(c0, NC):
                self.mm(bacc, self.ps[bacc][:, c * 128:c * 128 + 65], pT[:, c * 128:(c + 1) * 128], vh[:, j, :],
                        (j == 0 and c == 0), (j == NC * I + c), (pk, vk), skip=True)
            if j == nj - 1:
                a3 = self.ps[bacc][:].rearrange("p (c d) -> p c d", c=4)
                self.P.add("dve", lambda e, a3=a3: e.reciprocal(out=self.rden[:, 0:NC], in_=a3[:, 0:NC, 64]), (psk(bacc),),
                           ("rden",))
                self.tt("dve", self.ybtm[:, :, h * 64:(h + 1) * 64], a3[:, 0:NC, 0:64],
                        self.rden[:, 0:NC, None].broadcast_to([128, NC, 64]), ALU.mult, (psk(bacc), "rden"), ("ybtm",))
                self.reserved.discard(bacc)

        pending = None
        for h in range(8):
            vkeys = prep(h)
            for j in range(nj):
                cur = A(h, j)
                if pending is not None:
                    B(*pending)
                if j == 0:
                    load_v(h, vkeys)
                pending = (h, j) + cur
                yield
        B(*pending)
        yield
        for c in range(NC):
            b = self.bank()
            pb = self.psb(b)
            for i in range(4):
                self.tr(b, pb[:, i * 128:(i + 1) * 128], self.ybtm[:, c, i * 128:(i + 1) * 128], self.ident_b[:],
                        ("ybtm", "ident_b"))
            self.cp("act", self.ybT[:, :, c * 128:(c + 1) * 128], pb[:, 0:512].rearrange("p (i t) -> p i t", i=4),
                    (psk(b),), ("ybT",))

    def merge(self, l, x, xk):
        ys = ((self.yaT, "yaT"), (self.ybT, "ybT"), (self.ycT, "ycT"))
        for i in range(3):
            if not (self.brmask >> i) & 1:
                self.P.add("dve", lambda e, t=ys[i][0]: e.memset(t[:], 0.0), (), (ys[i][1],))
        for half in range(2):
            for i in range(3):
                wg, wgk = self.load_blk(self.scr_in[l][7 + 2 * i + half], (("sin", l, 7 + 2 * i + half),))
                wb, wbk = self.load_blk(self.scr_br[l][i][:, :, half * 512:(half + 1) * 512], (("sbr", l, i),), nk=4)
                for m in range(4):
                    bg = self.fm_chunk(wg, wgk, m * 128)
                    sg = self.sig[self.sig_i % 2]
                    sgk = "sig%d" % (self.sig_i % 2)
                    self.sig_i += 1
                    self.act(sg[:], self.pt(bg), AF.Sigmoid, (psk(bg),), (sgk,))
                    bb = self.fm_chunk(wb, wbk, m * 128, nk=4, rhs=ys[i][0], rkey=ys[i][1])
                    if i == 0:
                        self.tt("dve", self.mgacc[:, m, :], sg[:], self.pt(bb), ALU.mult, (sgk, psk(bb)), ("mgacc",))
                    else:
                        self.tt("dve", sg[:], sg[:], self.pt(bb), ALU.mult, (sgk, psk(bb)), (sgk,))
                        if i == 1:
                            self.tt("dve", self.mgacc[:, m, :], self.mgacc[:, m, :], sg[:], ALU.add, (sgk, "mgacc"), ("mgacc",))
                        else:
                            self.tt("dve", self.mgT[:, half * 4 + m, :], self.mgacc[:, m, :], sg[:], ALU.add, (sgk, "mgacc"),
                                    ("mgT",))
                    yield
        for half in range(2):
            wo, wok = self.load_blk(self.scr_o[l][half], (("so", l, half),))
            for m in range(4):
                b = self.fm_chunk(wo, wok, m * 128, rhs=self.mgT, rkey="mgT")
                k = half * 4 + m
                self.tt("dve", x[:, k, :], x[:, k, :], self.pt(b), ALU.add, (xk, psk(b)), (xk,))
            yield

    def build(self):
        assert NC == 2
        self.declare()
        self.alloc()
        self.consts()
        finals = []
        L, NT = self.L, self.NT
        convs = [self.conv_list(l) for l in range(L)]
        for fn in convs[0]:
            fn()
        slots = [(l, I) for l in range(L) for I in range(NT)]
        nslot = len(slots)

        def xt(s):
            return self.xTs[s % 2], "xT%d" % (s % 2)

        def fstream(s):
            if s - 1 >= 0:
                l, I = slots[s - 1]
                x, xk = xt(s - 1)
                if self.do_ffn:
                    yield from self.ffn(l, 2, x, xk)
                finals.extend(self.store_x(l, I, x, xk))
                yield
            if s + 1 < nslot:
                l, I = slots[s + 1]
                x, xk = xt(s + 1)
                if I == 0:
                    self.load_gains(l)
                self.load_x(l, I, x, xk)
                yield
                if self.do_ffn:
                    yield from self.ffn(l, 1, x, xk)

        def run(*gens, weights=None):
            gens = [g for g in gens if g is not None]
            alive = [True] * len(gens)
            acc = [0.0] * len(gens)
            w = weights or [1.0] * len(gens)
            while any(alive):
                for i, g in enumerate(gens):
                    if not alive[i]:
                        continue
                    acc[i] += w[i]
                    while acc[i] >= 1.0 and alive[i]:
                        acc[i] -= 1.0
                        try:
                            next(g)
                        except StopIteration:
                            alive[i] = False

        self.load_gains(0)
        self.load_x(0, 0, *xt(0))
        if self.do_ffn:
            run(self.ffn(0, 1, *xt(0)))
        for s, (l, I) in enumerate(slots):
            if I == 0 and self.do_mix:
                self.load_params(l)
            nxt = convs[l + 1] if l + 1 < L else []
            per = (len(nxt) + NT - 1) // NT if nxt else 0
            for fn in nxt[I * per:(I + 1) * per]:
                fn()
            x, xk = xt(s)
            n_f = 2 * (1 + NJ + 2 * NJJ + 2) + 2
            if self.do_mix:
                n_m = 24 + 2 * (8 * (NC * I + NC) + 8) + 30
                run(self.mixer(l, I, x, xk), fstream(s), weights=[1.0, min(1.0, n_f / n_m)])
            else:
                run(fstream(s))
        run(fstream(nslot))
        self.P.emit(finals)
        self.es.close()
        return self.nc


_CACHE = {}


def kernel(**inputs):
    x = np.asarray(inputs["x"])
    B, S, _ = x.shape
    L = int(np.asarray(inputs["ffn1_norm"]).shape[0])
    key = (S, L)
    if key not in _CACHE:
        _CACHE[key] = Builder(S, L).build()
    nc = _CACHE[key]
    shared = {k: np.ascontiguousarray(np.asarray(v)) for k, v in inputs.items() if k not in ("x", "positions")}
    pos = np.asarray(inputs["positions"]).astype(np.int32)
    in_maps = []
    for b in range(B):
        m = dict(shared)
        m["x"] = np.ascontiguousarray(x[b])
        m["positions"] = np.ascontiguousarray(pos[b])
        in_maps.append(m)
    res = run_bass_kernel_spmd(nc, in_maps, core_ids=list(range(B)))
    return np.stack([np.asarray(r["out"]) for r in res.results], axis=0).astype(np.float32)
```

```python
import contextlib
import numpy as np
import concourse.bass as bass
import concourse.mybir as mybir
from concourse.bass_utils import run_bass_kernel_spmd

F32 = mybir.dt.float32
BF16 = mybir.dt.bfloat16
I32 = mybir.dt.int32
AF = mybir.ActivationFunctionType
ALU = mybir.AluOpType
AX = mybir.AxisListType

D = 1024
DFF = 2816
EPS = 1e-6
TS = 256
NC = TS // 128
NK = 8
NJ = DFF // 128
NJJ = NJ // 2
IN_COLS = 6312

ENGS = ("pe", "act", "dve", "pool", "sp")
NDMA_SEM = 12


class Op:
    __slots__ = ("eng", "fn", "dma", "idx", "deps", "signal", "sigval", "dma_i")

    def __init__(self, eng, fn, dma):
        self.eng = eng
        self.fn = fn
        self.dma = dma
        self.deps = None
        self.signal = False
        self.sigval = 0
        self.dma_i = -1


class Prog:
    def __init__(self, nc):
        self.nc = nc
        self.ops = {e: [] for e in ENGS}
        self.last_w = {}
        self.readers = {}
        self.dma_count = {e: 0 for e in ENGS}
        self.dma_ops = {e: [] for e in ENGS}

    @staticmethod
    def _stream(op):
        return op.eng + ":dma" if op.dma else op.eng

    def add(self, eng, fn, reads=(), writes=(), dma=False):
        op = Op(eng, fn, dma)
        op.idx = len(self.ops[eng])
        deps = {}
        if eng != "pe" and not dma:
            pk = tuple(k for k in reads if isinstance(k, str) and k[:2] == "ps" and k[2:].isdigit())
            if pk:
                writes = tuple(writes) + pk

        def dep(o, raw):
            if o is None:
                return
            if (not o.dma) and o.eng == eng and not dma:
                if eng == "pe" or not raw:
                    return
            s = self._stream(o)
            cur = deps.get(s)
            if cur is None or cur.idx < o.idx:
                deps[s] = o

        for k in reads:
            dep(self.last_w.get(k), True)
        for k in writes:
            dep(self.last_w.get(k), False)
            rd = self.readers.get(k)
            if rd:
                for o in rd.values():
                    dep(o, False)
        if dma:
            i = self.dma_count[eng]
            op.dma_i = i
            self.dma_count[eng] = i + 1
            self.dma_ops[eng].append(op)
            lim = NDMA_SEM if eng != "pool" else 4
            if i >= lim:
                o = self.dma_ops[eng][i - lim]
                s = self._stream(o)
                cur = deps.get(s)
                if cur is None or cur.idx < o.idx:
                    deps[s] = o
        op.deps = list(deps.values())
        for o in op.deps:
            o.signal = True
        st = self._stream(op)
        for k in reads:
            self.readers.setdefault(k, {})[st] = op
        for k in writes:
            self.last_w[k] = op
            self.readers[k] = {}
        self.ops[eng].append(op)
        return op

    def emit(self, final_ops):
        nc = self.nc
        for o in final_ops:
            o.signal = True
        with contextlib.ExitStack() as es:
            csem = {}
            for e in ("pe", "act", "dve", "pool"):
                csem[e] = es.enter_context(nc.semaphore("c_" + e))
            dsem = {}
            for e in ENGS:
                if self.dma_count[e]:
                    dsem[e] = [es.enter_context(nc.semaphore("d_%s_%d" % (e, i))) for i in range(NDMA_SEM)]
            for e in ENGS:
                c = 0
                for op in self.ops[e]:
                    if op.dma:
                        op.sigval = 16 * (op.dma_i // NDMA_SEM + 1)
                    elif op.signal:
                        c += 1
                        op.sigval = c
            block = es.enter_context(nc.Block())
            engobj = {"pe": block.tensor, "act": block.scalar, "dve": block.vector, "pool": block.gpsimd,
                      "sp": block.sync}

            def make(e):
                def body(eng):
                    waited = {}
                    for op in self.ops[e]:
                        for d in op.deps:
                            if d.dma:
                                sem = dsem[d.eng][d.dma_i % NDMA_SEM]
                                key = (d.eng, d.dma_i % NDMA_SEM)
                            else:
                                sem = csem[d.eng]
                                key = d.eng
                            if waited.get(key, 0) >= d.sigval:
                                continue
                            waited[key] = d.sigval
                            eng.wait_ge(sem, d.sigval)
                        ins = op.fn(eng)
                        if op.dma:
                            ins.then_inc(dsem[e][op.dma_i % NDMA_SEM], 16)
                        elif op.signal:
                            ins.then_inc(csem[e], 1)
                    if e == "sp":
                        for o in final_ops:
                            if o.dma:
                                eng.wait_ge(dsem[o.eng][o.dma_i % NDMA_SEM], o.sigval)
                            else:
                                eng.wait_ge(csem[o.eng], o.sigval)
                return body

            for e in ENGS:
                if self.ops[e] or e == "sp":
                    engobj[e](make(e))


NWI = 3
FQ = "sp"
INV_FREQ = (1.0 / (np.float32(10000.0) ** (np.arange(0, 32, 2, dtype=np.float32) / np.float32(32)))).astype(np.float32)
PI = float(np.pi)


def psk(b):
    return "ps%d" % b


class Builder:
    def __init__(self, S, L, do_mix=True, do_ffn=True, mix_parts=(1, 1, 1)):
        self.S = S
        self.L = L
        self.NT = S // TS
        self.NCH = S // 128
        self.do_mix = do_mix
        self.do_ffn = do_ffn
        self.mix_parts = mix_parts
        self.nc = bass.Bass("TRN2", target_bir_lowering=False)
        self.P = Prog(self.nc)
        self.es = contextlib.ExitStack()
        self.bank_i = 0
        self.reserved = set()
        self.pending_conv = []
        self.stage = 8
        self.brmask = 7

    def sb(self, name, shape, dt):
        return self.es.enter_context(self.nc.sbuf_tensor(name, shape, dt))

    def dram_in(self, name, shape, dt=F32):
        return self.nc.dram_tensor(name, list(shape), dt, kind="ExternalInput")

    def bank(self):
        while self.bank_i in self.reserved:
            self.bank_i = (self.bank_i + 1) % 8
        b = self.bank_i
        self.bank_i = (self.bank_i + 1) % 8
        return b

    def pt(self, b):
        return self.ps[b][:, 0:TS]

    def psb(self, b):
        return self.ps[b][:].bitcast(BF16)

    def mm(self, b, out, lhsT, rhs, start, stop, reads, skip=False):
        if skip:
            self.P.add("pe", lambda e: e.matmul(out, lhsT, rhs, start=start, stop=stop, skip_group_check=True), reads,
                       (psk(b),))
        else:
            self.P.add("pe", lambda e: e.matmul(out, lhsT, rhs, start=start, stop=stop), reads, (psk(b),))

    def tr(self, b, out, in_, ident, reads):
        self.P.add("pe", lambda e: e.transpose(out, in_, ident), reads, (psk(b),))

    def act(self, out, in_, func, reads, writes, bias=None, scale=None):
        kw = {}
        if bias is not None:
            kw["bias"] = bias
        if scale is not None:
            kw["scale"] = scale
        self.P.add("act", lambda e: e.activation(out=out, in_=in_, func=func, **kw), reads, writes)

    def ts(self, eng, out, in0, s1, s2, op0, op1, reads, writes):
        if op1 is None:
            self.P.add(eng, lambda e: e.tensor_scalar(out=out, in0=in0, scalar1=s1, scalar2=None, op0=op0), reads, writes)
        else:
            self.P.add(eng, lambda e: e.tensor_scalar(out=out, in0=in0, scalar1=s1, scalar2=s2, op0=op0, op1=op1),
                       reads, writes)

    def rsqrt(self, out, in_, scale, reads, writes):
        self.act(out, in_, AF.Sqrt, tuple(reads) + ("eps_t",), writes, bias=self.eps_t[:out.shape[0], :], scale=scale)
        self.P.add("dve", lambda e: e.reciprocal(out=out, in_=out), writes, writes)

    def stt(self, eng, out, in0, scalar, in1, op0, op1, reads, writes):
        self.P.add(eng, lambda e: e.scalar_tensor_tensor(out=out, in0=in0, scalar=scalar, in1=in1, op0=op0, op1=op1),
                   reads, writes)

    def tt(self, eng, out, in0, in1, op, reads, writes):
        self.P.add(eng, lambda e: e.tensor_tensor(out=out, in0=in0, in1=in1, op=op), reads, writes)

    def red(self, out, in_, reads, writes):
        self.P.add("dve", lambda e: e.tensor_reduce(out=out, in_=in_, axis=AX.X, op=ALU.add), reads, writes)

    def cp(self, eng, out, in_, reads, writes):
        if eng == "act":
            self.P.add("act", lambda e: e.copy(out=out, in_=in_), reads, writes)
        else:
            self.P.add(eng, lambda e: e.tensor_copy(out=out, in_=in_), reads, writes)

    def dma(self, q, out, in_, reads, writes, slow=False):
        if slow:
            return self.P.add(q, lambda e: e.dma_start(out=out, in_=in_, allow_slow_non_contiguous=True), reads, writes,
                              dma=True)
        return self.P.add(q, lambda e: e.dma_start(out=out, in_=in_), reads, writes, dma=True)

    def declare(self):
        nc, S, L = self.nc, self.S, self.L
        self.x_in = self.dram_in("x", (S, D))
        self.pos_in = self.dram_in("positions", (S,), I32)
        names = [
            ("ffn1_norm", (L, D)), ("ffn1_w_in", (L, D, 2 * DFF)), ("ffn1_w_out", (L, DFF, D)),
            ("mix_norm", (L, D)), ("w_in", (L, D, IN_COLS)), ("gm_v_norm", (L, 512)),
            ("gm_w_s", (L, 4, 128, 128)), ("gm_b_s", (L, 4, 128)), ("mla_q_norm", (L, 384)),
            ("mla_kv_norm", (L, 256)), ("mla_w_uq", (L, 384, 768)), ("mla_w_ukv", (L, 256, 1024)),
            ("mla_q_gain", (L, 96)), ("mla_k_gain", (L, 96)), ("ssd_conv_w", (L, 4, 1024)),
            ("ssd_conv_b", (L, 1024)), ("ssd_dt_bias", (L, 8)), ("ssd_a_log", (L, 8)), ("ssd_d", (L, 8)),
            ("ssd_norm", (L, 512)), ("w_branch", (L, 3, 512, D)), ("w_out", (L, D, D)),
            ("ffn2_norm", (L, D)), ("ffn2_w_in", (L, D, 2 * DFF)), ("ffn2_w_out", (L, DFF, D)),
        ]
        self.W = {}
        for n, shp in names:
            self.W[n] = self.dram_in(n, shp)
        self.out = nc.dram_tensor("out", [S, D], F32, kind="ExternalOutput")
        self.xscr = nc.dram_tensor("xscr", [128, NK, S], F32, kind="Internal")
        self.kscr = nc.dram_tensor("kscr", [8, 96, S], BF16, kind="Internal")
        self.vscr = nc.dram_tensor("vscr", [8, 128, S // 128, 65], BF16, kind="Internal")
        self.scr_wi, self.scr_wo, self.scr_in, self.scr_br, self.scr_o, self.scr_uq, self.scr_ukv = {}, {}, {}, {}, {}, {}, {}
        for l in range(L):
            for f in (1, 2):
                self.scr_wi[(l, f)] = nc.dram_tensor("swi_%d_%d" % (l, f), [NJ, 128, NK, 256], BF16, kind="Internal")
                self.scr_wo[(l, f)] = nc.dram_tensor("swo_%d_%d" % (l, f), [2, NJJ, 128, 2, 512], BF16, kind="Internal")
            self.scr_in[l] = nc.dram_tensor("sin_%d" % l, [13, 128, NK, 512], BF16, kind="Internal")
            self.scr_br[l] = nc.dram_tensor("sbr_%d" % l, [3, 128, 4, D], BF16, kind="Internal")
            self.scr_o[l] = nc.dram_tensor("so_%d" % l, [2, 128, NK, 512], BF16, kind="Internal")
            self.scr_uq[l] = nc.dram_tensor("suq_%d" % l, [128, 3, 768], BF16, kind="Internal")
            self.scr_ukv[l] = nc.dram_tensor("sukv_%d" % l, [128, 2, 1024], BF16, kind="Internal")

    def alloc(self):
        nc = self.nc
        sb = self.sb
        NCH = self.NCH
        self.ident_f = sb("ident_f", [128, 128], F32)
        self.ident_b = sb("ident_b", [128, 128], BF16)
        self.ones_b = sb("ones_b", [128, 128], BF16)
        self.ones_f = sb("ones_f", [128, 128], F32)
        self.U_f = sb("U_f", [128, 128], F32)
        self.U_b = sb("U_b", [128, 128], BF16)
        self.Lm_f = sb("Lm_f", [128, 128], F32)
        self.eps_t = sb("eps_t", [128, 1], F32)
        self.one_t = sb("one_t", [128, 1], F32)
        self.negpi_t = sb("negpi_t", [128, 1], F32)
        self.xTs = [sb("xT%d" % i, [128, NK, TS], F32) for i in range(2)]
        self.hT = sb("hT", [128, NK, TS], BF16)
        self.hTf = sb("hTf", [128, NK, TS], BF16)
        self.sq = [sb("sq%d" % i, [128, TS], BF16) for i in range(2)]
        self.sqf = [sb("sqf%d" % i, [128, TS], BF16) for i in range(2)]
        self.rstd = sb("rstd", [128, TS], F32)
        self.rstdf = sb("rstdf", [128, TS], F32)
        self.gains = sb("gains", [128, 2, 3, NK], F32)
        self.Rarg = sb("Rarg", [128, 16, 128], F32)
        self.xtm = self.Rarg[:].rearrange("p a b -> p (a b)").rearrange("p (c d) -> p c d", c=NC)
        self.wif = [sb("wif%d" % i, [128, NK, 256], BF16) for i in range(4)]
        self.wif_i = 0
        self.actT = sb("actT", [128, NJ, TS], BF16)
        self.sg = [sb("sg%d" % i, [128, TS], F32) for i in range(2)]
        self.wi = [sb("wi%d" % i, [128, NK, 512], BF16) for i in range(NWI)]
        self.wo = [sb("wo%d" % i, [128, 2, 512], BF16) for i in range(4)]
        self.ps = [self.es.enter_context(nc.psum_tensor("ps%d" % i, [128, 512], F32)) for i in range(8)]
        self.wi_i = self.wo_i = self.sq_i = self.sqf_i = self.sg_i = self.sig_i = self.pt_i = 0
        if not self.do_mix:
            return
        self.pos_i = sb("pos_i", [128, NCH], I32)
        self.pos_f = sb("pos_f", [128, NCH], F32)
        self.invf = sb("invf", [128, 16], F32)
        av = self.actT[:].rearrange("p j t -> p (j t)").bitcast(F32)
        n16 = NCH * 16
        self.ang = av[:, 0:n16].rearrange("p (c j) -> p c j", j=16)
        self.angf = av[:, n16:2 * n16].rearrange("p (c j) -> p c j", j=16)
        self.angi = av[:, 2 * n16:3 * n16].bitcast(I32).rearrange("p (c j) -> p c j", j=16)
        self.cosT = sb("cosT", [128, NCH, 16], F32)
        self.sinT = sb("sinT", [128, NCH, 16], F32)
        self.gvb = sb("gvb", [128, 512], F32)
        self.bsb = sb("bsb", [128, 4, 128], F32)
        self.Wsraw = sb("Wsraw", [128, 4, 128], F32)
        self.Wsm = sb("Wsm", [128, 4, 128], BF16)
        self.WsT = sb("WsT", [128, 4, 128], BF16)
        self.gqn = sb("gqn", [128, 3], F32)
        self.gkvn = sb("gkvn", [128, 2], F32)
        self.gqb = sb("gqb", [128, 96], F32)
        self.gkb = sb("gkb", [128, 96], F32)
        self.gmx = sb("gmx", [128, 2], F32)
        self.negC = sb("negC", [128, 1], F32)
        self.cw = sb("cw", [128, NK, 4], F32)
        self.cb = sb("cb", [128, NK], F32)
        self.dtb = sb("dtb", [128, 8], F32)
        self.aneg = sb("aneg", [128, 8], F32)
        self.Dsk = sb("Dsk", [128, 8], F32)
        self.gsn = sb("gsn", [128, 512], F32)
        self.uT = sb("uT", [128, 4, TS], BF16)
        self.vg = sb("vg", [128, 512], F32)
        self.junk = sb("junk", [128, 768], F32)
        self.s1 = sb("s1", [128, 8], F32)
        self.s2 = sb("s2", [128, 8], F32)
        self.vtm = sb("vtm", [128, NC, 512], BF16)
        self.zs = sb("zs", [128, NC, 512], BF16)
        self.cqnT = sb("cqnT", [128, 3, TS], BF16)
        self.ckvnT = sb("ckvnT", [128, 2, TS], BF16)
        self.sqq = sb("sqq", [128, 5, TS], BF16)
        self.krdt = sb("krdt", [128, NC, 40], F32)
        self.xbcp = [sb("xbcp%d" % i, [128, TS + 3], F32) for i in range(2)]
        self.hist = sb("hist", [128, NK, 3], F32)
        self.cacc = self.vg[:, 0:TS]
        self.xbcT = sb("xbcT", [128, NK, TS], BF16)
        self.spb = sb("spb", [128, 4, 128], F32)
        self.yaT = sb("yaT", [128, 4, TS], BF16)
        self.dt8 = sb("dt8", [128, 8], F32)
        self.da8 = sb("da8", [128, 8], F32)
        self.t8a = sb("t8a", [128, 8], F32)
        self.t8b = sb("t8b", [128, 8], F32)
        self.cs8 = sb("cs8", [128, 8], F32)
        self.tot8 = sb("tot8", [128, 8], F32)
        self.dout8 = sb("dout8", [128, 8], F32)
        self.etot8 = sb("etot8", [128, 8], F32)
        self.R = self.Rarg[:, 0:8, :]
        self.arg = self.Rarg[:, 8:16, :]
        self.ecs = self.R
        self.cbm = sb("cbm", [128, 2, 128], F32)
        self.MT = sb("MT", [128, 8, 128], BF16)
        self.Cs = sb("Cs", [128, 8, 128], BF16)
        self.xdt = sb("xdt", [128, 8, 64], BF16)
        self.xdd = sb("xdd", [128, 8, 64], BF16)
        self.xsd = sb("xsd", [128, 8, 64], F32)
        self.Btm = sb("Btm", [128, 2, 128], BF16)
        self.Sst = sb("Sst", [128, 8, 64], F32)
        self.Sbf = sb("Sbf", [128, 8, 64], BF16)
        self.y1 = self.vg
        self.yctm = sb("yctm", [128, 512], BF16)
        self.ycT = sb("ycT", [128, 4, TS], BF16)
        self.rq2 = sb("rq2", [128, 2], F32)
        self.qtm = sb("qtm", [128, 8, 96], F32)
        self.hss = sb("hss", [128, 8], F32)
        self.r1 = sb("r1", [128, 8, 16], F32)
        self.r2 = sb("r2", [128, 8, 16], F32)
        self.qr = sb("qr", [128, 8, 96], BF16)
        self.qT = sb("qT", [128, 8, TS], BF16)
        self.kst = sb("kst", [128, 8, 128], BF16)
        self.vst = sb("vst", [128, 8, 65], BF16)
        self.kh = [sb("kh%d" % i, [128, self.S], BF16) for i in range(2)]
        self.vh = [sb("vh%d" % i, [128, NCH, 65], BF16) for i in range(1)]
        self.kv_i = 0
        self.pT = [sb("pT%d" % i, [128, TS], BF16) for i in range(4)]
        self.rden = sb("rden", [128, 4], F32)
        self.ybtm = sb("ybtm", [128, NC, 512], BF16)
        self.ybT = sb("ybT", [128, 4, TS], BF16)
        self.sig = [sb("sig%d" % i, [128, TS], F32) for i in range(2)]
        self.mgacc = sb("mgacc", [128, 4, TS], F32)
        self.mgT = sb("mgT", [128, NK, TS], BF16)

    def consts(self):
        P = self.P
        idf, idb, ones_b, ones_f, U_f, U_b, Lm_f = self.ident_f, self.ident_b, self.ones_b, self.ones_f, self.U_f, self.U_b, self.Lm_f
        P.add("pool", lambda e: e.memset(idf[:], 0.0), (), ("ident_f",))
        P.add("pool", lambda e: e.affine_select(out=idf[:], in_=idf[:], pattern=[[-1, 128]], compare_op=ALU.not_equal,
                                                fill=1.0, base=0, channel_multiplier=1), ("ident_f",), ("ident_f",))
        P.add("pool", lambda e: e.tensor_copy(out=idb[:], in_=idf[:]), ("ident_f",), ("ident_b",))
        P.add("pool", lambda e: e.memset(ones_b[:], 1.0), (), ("ones_b",))
        P.add("pool", lambda e: e.memset(ones_f[:], 1.0), (), ("ones_f",))
        P.add("pool", lambda e: e.memset(U_f[:], 1.0), (), ("U_f",))
        P.add("pool", lambda e: e.affine_select(out=U_f[:], in_=U_f[:], pattern=[[1, 128]], compare_op=ALU.is_ge,
                                                fill=0.0, base=0, channel_multiplier=-1), ("U_f",), ("U_f",))
        P.add("pool", lambda e: e.tensor_copy(out=U_b[:], in_=U_f[:]), ("U_f",), ("U_b",))
        P.add("pool", lambda e: e.memset(Lm_f[:], 1.0), (), ("Lm_f",))
        P.add("pool", lambda e: e.affine_select(out=Lm_f[:], in_=Lm_f[:], pattern=[[-1, 128]], compare_op=ALU.is_ge,
                                                fill=0.0, base=0, channel_multiplier=1), ("Lm_f",), ("Lm_f",))
        eps_t, one_t, negpi_t = self.eps_t, self.one_t, self.negpi_t
        P.add("pool", lambda e: e.memset(eps_t[:], EPS), (), ("eps_t",))
        P.add("pool", lambda e: e.memset(one_t[:], 1.0), (), ("one_t",))
        P.add("pool", lambda e: e.memset(negpi_t[:], -PI), (), ("negpi_t",))
        if not self.do_mix:
            return
        invf = self.invf
        for j in range(16):
            P.add("pool", lambda e, j=j: e.memset(invf[:, j:j + 1], float(INV_FREQ[j])), (), ("invf",))
        NCH = self.NCH
        self.dma("sp", self.pos_i[:], self.pos_in.ap().rearrange("(c p) -> p c", p=128), (), ("pos_i",), slow=True)
        self.cp("dve", self.pos_f[:], self.pos_i[:], ("pos_i",), ("pos_f",))
        self.tt("dve", self.ang, self.pos_f[:, :, None].broadcast_to([128, NCH, 16]),
                self.invf[:, None, :].broadcast_to([128, NCH, 16]), ALU.mult, ("pos_f", "invf"), ("ang",))
        for dst, dk, shift in ((self.sinT, "sinT", 0.0), (self.cosT, "cosT", 0.5 * PI)):
            self.ts("dve", dst[:], self.ang, shift, 1.0 / (2 * PI), ALU.add, ALU.mult, ("ang",), (dk,))
            self.cp("dve", self.angi, dst[:], (dk,), ("angi",))
            self.cp("dve", self.angf, self.angi, ("angi",), ("angf",))
            self.ts("dve", dst[:], self.ang, shift, None, ALU.add, None, ("ang",), (dk,))
            self.stt("dve", dst[:], self.angf, -2 * PI, dst[:], ALU.mult, ALU.add, ("angf", dk), (dk,))
            self.ts("dve", self.angf, dst[:], PI, 2 * PI, ALU.is_gt, ALU.mult, (dk,), ("angf",))
            self.tt("dve", dst[:], dst[:], self.angf, ALU.subtract, (dk, "angf"), (dk,))
            self.ts("dve", self.angf, dst[:], -PI, 2 * PI, ALU.is_lt, ALU.mult, (dk,), ("angf",))
            self.tt("dve", dst[:], dst[:], self.angf, ALU.add, (dk, "angf"), (dk,))
            self.act(dst[:], dst[:], AF.Sin, (dk,), (dk,))

    def conv_list(self, l):
        lst = []

        def add(out, in_, key):
            lst.append(lambda: self.dma("pool", out, in_, (), (key,)))

        if self.do_ffn:
            for f in (1, 2):
                w_in = self.W["ffn%d_w_in" % f][l].rearrange("(k p) c -> p k c", p=128)
                w_out = self.W["ffn%d_w_out" % f][l].rearrange("(jj j2 p) c -> jj p j2 c", j2=2, p=128)
                swi, swo = self.scr_wi[(l, f)], self.scr_wo[(l, f)]
                for j in range(NJ):
                    for h in range(2):
                        add(swi[j][:, :, h * 128:(h + 1) * 128],
                            w_in[:, :, h * DFF + j * 128: h * DFF + (j + 1) * 128], ("swi", l, f, j, h))
                for mg in range(2):
                    for jj in range(NJJ):
                        add(swo[mg, jj], w_out[jj][:, :, mg * 512:(mg + 1) * 512], ("swo", l, f, mg, jj))
        if self.do_mix:
            w = self.W["w_in"][l].rearrange("(k p) c -> p k c", p=128)
            sin = self.scr_in[l]
            srcs = {0: (0, 512), 1: (512, 1024), 2: (1024, 1536), 4: (1696, 2208), 5: (2208, 2720), 6: (2720, 3232)}
            for i in range(6):
                srcs[7 + i] = (3240 + 512 * i, 3240 + 512 * (i + 1))
            for bi, (a, b_) in srcs.items():
                add(sin[bi], w[:, :, a:b_], ("sin", l, bi))
            add(sin[3][:, :, 0:160], w[:, :, 1536:1696], ("sin", l, 3, 0))
            add(sin[3][:, :, 160:168], w[:, :, 3232:3240], ("sin", l, 3, 1))
            for i in range(3):
                add(self.scr_br[l][i], self.W["w_branch"][l, i].rearrange("(kk p) c -> p kk c", p=128), ("sbr", l, i))
            wo_ = self.W["w_out"][l].rearrange("(k p) c -> p k c", p=128)
            for hf in range(2):
                add(self.scr_o[l][hf], wo_[:, :, hf * 512:(hf + 1) * 512], ("so", l, hf))
            add(self.scr_uq[l][:], self.W["mla_w_uq"][l].rearrange("(i p) c -> p i c", p=128), ("suq", l))
            add(self.scr_ukv[l][:], self.W["mla_w_ukv"][l].rearrange("(i p) c -> p i c", p=128), ("sukv", l))
        return lst

    def load_gains(self, l):
        W = self.W
        for i, n in enumerate(("ffn1_norm", "mix_norm", "ffn2_norm")):
            self.dma("sp", self.gains[:, l % 2, i, :], W[n][l].rearrange("(k p) -> p k", p=128), (), (("gains", l % 2),),
                     slow=True)

    def load_params(self, l):
        W = self.W

        def bc(ap, shape):
            return ap.broadcast_to(shape)

        self.dma("sp", self.gvb[:], bc(W["gm_v_norm"][l:l + 1, :], [128, 512]), (), ("gvb",), slow=True)
        self.dma("sp", self.bsb[:], bc(W["gm_b_s"][l:l + 1], [128, 4, 128]), (), ("bsb",), slow=True)
        self.dma("sp", self.Wsraw[:], W["gm_w_s"][l].rearrange("g t s -> t g s"), (), ("Wsraw",))
        self.dma("sp", self.gqn[:], W["mla_q_norm"][l].rearrange("(i p) -> p i", p=128), (), ("gqn",), slow=True)
        self.dma("sp", self.gkvn[:], W["mla_kv_norm"][l].rearrange("(i p) -> p i", p=128), (), ("gkvn",), slow=True)
        self.dma("sp", self.gqb[:], bc(W["mla_q_gain"][l:l + 1, :], [128, 96]), (), ("gqb",), slow=True)
        self.dma("sp", self.gkb[:], bc(W["mla_k_gain"][l:l + 1, :], [128, 96]), (), ("gkb",), slow=True)
        for k in range(4):
            self.dma("sp", self.cw[:, :, k], W["ssd_conv_w"][l, k].rearrange("(c p) -> p c", p=128), (), ("cw",), slow=True)
        self.dma("sp", self.cb[:], W["ssd_conv_b"][l].rearrange("(c p) -> p c", p=128), (), ("cb",), slow=True)
        self.dma("sp", self.dtb[:], bc(W["ssd_dt_bias"][l:l + 1, :], [128, 8]), (), ("dtb",), slow=True)
        self.dma("sp", self.aneg[:], bc(W["ssd_a_log"][l:l + 1, :], [128, 8]), (), ("aneg",), slow=True)
        self.dma("sp", self.Dsk[:], bc(W["ssd_d"][l:l + 1, :], [128, 8]), (), ("Dsk",), slow=True)
        self.dma("sp", self.gsn[:], bc(W["ssd_norm"][l:l + 1, :], [128, 512]), (), ("gsn",), slow=True)
        self.act(self.aneg[:], self.aneg[:], AF.Exp, ("aneg",), ("aneg",))
        self.ts("dve", self.aneg[:], self.aneg[:], -1.0, None, ALU.mult, None, ("aneg",), ("aneg",))
        self.tt("dve", self.Wsm[:], self.Wsraw[:], self.Lm_f[:, None, :].broadcast_to([128, 4, 128]), ALU.mult,
                ("Wsraw", "Lm_f"), ("Wsm",))
        b = self.bank()
        for g in range(4):
            self.tr(b, self.psb(b)[:, g * 128:(g + 1) * 128], self.Wsm[:, g, :], self.ident_b[:], ("Wsm", "ident_b"))
        self.cp("dve", self.WsT[:], self.psb(b)[:, 0:512].rearrange("p (g t) -> p g t", g=4), (psk(b),), ("WsT",))
        self.P.add("dve", lambda e: e.tensor_reduce(out=self.gmx[:, 0:1], in_=self.gqb[:], axis=AX.X, op=ALU.max,
                                                    apply_absolute_value=True), ("gqb",), ("gmx",))
        self.P.add("dve", lambda e: e.tensor_reduce(out=self.gmx[:, 1:2], in_=self.gkb[:], axis=AX.X, op=ALU.max,
                                                    apply_absolute_value=True), ("gkb",), ("gmx",))
        self.stt("dve", self.negC[:], self.gmx[:, 0:1], -float(np.sqrt(96.0)), self.gmx[:, 1:2], ALU.mult, ALU.mult,
                 ("gmx",), ("negC",))
        self.P.add("dve", lambda e: e.memset(self.Sst[:], 0.0), (), ("Sst",))
        self.P.add("dve", lambda e: e.memset(self.Sbf[:], 0.0), (), ("Sbf",))
        self.P.add("dve", lambda e: e.memset(self.hist[:], 0.0), (), ("hist",))
        if l == 0:
            self.P.add("dve", lambda e: e.memset(self.vst[:, :, 64:65], 1.0), (), ("vst1",))

    def load_x(self, l, I, x, xk):
        if l > 0:
            self.dma("sp", x[:], self.xscr[:, :, I * TS:(I + 1) * TS], (("xscr", I),), (xk,))
            return
        XK = ("R", "arg")
        src = self.x_in[I * TS:(I + 1) * TS, :].rearrange("(c p) d -> p c d", p=128)
        self.dma("sp", self.xtm, src, (), XK)
        for k in range(NK):
            b = self.bank()
            for c in range(NC):
                self.tr(b, self.ps[b][:, c * 128:(c + 1) * 128], self.xtm[:, c, k * 128:(k + 1) * 128], self.ident_f[:],
                        XK + ("ident_f",))
            eng = "dve" if k % 2 == 0 else "act"
            self.cp(eng, x[:, k, :], self.pt(b), (psk(b),), (xk,))

    def store_x(self, l, I, x, xk):
        if l < self.L - 1:
            return [self.dma("sp", self.xscr[:, :, I * TS:(I + 1) * TS], x[:], (xk,), (("xscr", I),))]
        XK = ("R", "arg")
        for c in range(NC):
            for hlf in range(2):
                b = self.bank()
                for kk in range(4):
                    k = hlf * 4 + kk
                    self.tr(b, self.ps[b][:, kk * 128:(kk + 1) * 128], x[:, k, c * 128:(c + 1) * 128],
                            self.ident_f[:], (xk, "ident_f"))
                eng = "dve" if hlf == 0 else "act"
                self.cp(eng, self.xtm[:, c, hlf * 512:(hlf + 1) * 512], self.ps[b][:], (psk(b),), XK)
        dst = self.out[I * TS:(I + 1) * TS, :].rearrange("(c p) d -> p c d", p=128)
        return [self.dma("sp", dst, self.xtm, XK, ())]

    def rmsnorm_T(self, l, gi, x, xk, ffn):
        hT, hname = (self.hTf, "hTf") if ffn else (self.hT, "hT")
        rstd, rk = (self.rstdf, "rstdf") if ffn else (self.rstd, "rstd")
        b = self.bank()
        for k in range(NK):
            if ffn:
                sq, sqk = self.sqf[self.sqf_i % 2], "sqf%d" % (self.sqf_i % 2)
                self.sqf_i += 1
            else:
                sq, sqk = self.sq[self.sq_i % 2], "sq%d" % (self.sq_i % 2)
                self.sq_i += 1
            self.act(sq[:], x[:, k, :], AF.Square, (xk,), (sqk,))
            self.mm(b, self.pt(b), self.ones_b[:], sq[:], k == 0, k == NK - 1, (sqk, "ones_b"))
        self.rsqrt(rstd[:], self.pt(b), 1.0 / D, (psk(b),), (rk,))
        for k in range(NK):
            self.stt("dve", hT[:, k, :], x[:, k, :], self.gains[:, l % 2, gi, k:k + 1], rstd[:], ALU.mult, ALU.mult,
                     (xk, ("gains", l % 2), rk), ((hname, k),))

    def ffn(self, l, f, x, xk):
        gi = 0 if f == 1 else 2
        self.rmsnorm_T(l, gi, x, xk, True)
        yield
        swi, swo = self.scr_wi[(l, f)], self.scr_wo[(l, f)]
        for j in range(NJ):
            wi = self.wif[self.wif_i % 4]
            wik = "wif%d" % (self.wif_i % 4)
            self.wif_i += 1
            self.dma(FQ, wi[:], swi[j], (("swi", l, f, j, 0), ("swi", l, f, j, 1)), (wik,))
            bg, bu = self.bank(), self.bank()
            for (b, off) in ((bg, 0), (bu, 128)):
                for k in range(NK):
                    self.mm(b, self.pt(b), wi[:, k, off:off + 128], self.hTf[:, k, :],
                            k == 0, k == NK - 1, (wik, ("hTf", k)))
            sg = self.sg[self.sg_i % 2]
            sgk = "sg%d" % (self.sg_i % 2)
            self.sg_i += 1
            self.act(sg[:], self.pt(bg), AF.Silu, (psk(bg),), (sgk,))
            self.tt("dve", self.actT[:, j, :], sg[:], self.pt(bu), ALU.mult, (sgk, psk(bu)), (("actT", j),))
            yield
        for mg in range(2):
            banks = [self.bank() for _ in range(2)]
            self.reserved.update(banks)
            for jj in range(NJJ):
                wo = self.wo[self.wo_i % 4]
                wok = "wo%d" % (self.wo_i % 4)
                self.wo_i += 1
                self.dma(FQ, wo[:], swo[mg, jj], (("swo", l, f, mg, jj),), (wok,))
                for j2 in range(2):
                    j = jj * 2 + j2
                    for m in range(4):
                        b = banks[m // 2]
                        o = self.ps[b][:, (m % 2) * TS:(m % 2 + 1) * TS]
                        self.mm(b, o, wo[:, j2, m * 128:(m + 1) * 128], self.actT[:, j, :],
                                j == 0 and m % 2 == 0, j == NJ - 1, (wok, ("actT", j)), skip=True)
                yield
            for m in range(4):
                b = banks[m // 2]
                k = mg * 4 + m
                o = self.ps[b][:, (m % 2) * TS:(m % 2 + 1) * TS]
                self.stt("dve", x[:, k, :], o, 0.5, x[:, k, :], ALU.mult, ALU.add, (psk(b), xk), (xk,))
            self.reserved.difference_update(banks)
            yield
    def plan_loads(self, l):
        sin_ = self.scr_in[l]
        L_ = []
        for bi in (0, 1, 2):
            L_.append(("blk", sin_[bi], (("sin", l, bi),), 512, NK))
        L_.append(("blk", sin_[3][:, :, 0:168], (("sin", l, 3, 0), ("sin", l, 3, 1)), 168, NK))
        for bi in (4, 5, 6):
            L_.append(("blk", sin_[bi], (("sin", l, bi),), 512, NK))
        L_.append(("flat", self.scr_uq[l][:], (("suq", l),), 3, 768))
        L_.append(("flat", self.scr_ukv[l][:], (("sukv", l),), 2, 1024))
        for half in range(2):
            for i in range(3):
                L_.append(("blk", sin_[7 + 2 * i + half], (("sin", l, 7 + 2 * i + half),), 512, NK))
                L_.append(("blk", self.scr_br[l][i][:, :, half * 512:(half + 1) * 512], (("sbr", l, i),), 512, 4))
        for half in range(2):
            L_.append(("blk", self.scr_o[l][half], (("so", l, half),), 512, NK))
        self.loads = L_
        self.ld_issued = []
        self.ld_next = 0
        self.PREF = getattr(self, "PREF", 0)

    def _issue(self):
        kind, src, key, a, b_ = self.loads[len(self.ld_issued)]
        wi = self.wi[self.wi_i % NWI]
        wik = "wi%d" % (self.wi_i % NWI)
        self.wi_i += 1
        if kind == "blk":
            v = wi
            self.dma("sp", wi[:, 0:b_, 0:a], src, key, (wik,))
        else:
            v = wi[:].rearrange("p k c -> p (k c)")[:, 0:a * b_].rearrange("p (i c) -> p i c", i=a)
            self.dma("sp", v, src, key, (wik,))
        self.ld_issued.append((v, wik))

    def get_blk(self):
        i = self.ld_next
        self.ld_next += 1
        while len(self.ld_issued) < min(i + 1 + self.PREF, len(self.loads)):
            self._issue()
        return self.ld_issued[i]

    def load_blk(self, src, key, ncols=512, nk=NK):
        return self.get_blk()

    def load_flat(self, src, key, n_i, n_c):
        return self.get_blk()

    def fm_chunk(self, wi, wik, col0, nk=NK, rhs=None, rkey=None):
        b = self.bank()
        for k in range(nk):
            r = self.hT[:, k, :] if rhs is None else rhs[:, k, :]
            rk = ("hT", k) if rhs is None else rkey
            self.mm(b, self.pt(b), wi[:, k, col0:col0 + 128], r, k == 0, k == nk - 1, (wik, rk))
        return b

    def tm_chunk(self, wi, wik, c, col0, ncols):
        b = self.bank()
        for k in range(NK):
            self.mm(b, self.ps[b][:, 0:ncols], self.hT[:, k, c * 128:(c + 1) * 128], wi[:, k, col0:col0 + ncols],
                    k == 0, k == NK - 1, (wik, ("hT", k)))
        return b

    def head_norm_rope(self, cg, gb, gbk, dst_T, dst_key, col0):
        q3 = self.qtm[:]
        self.tt("dve", self.junk[:].rearrange("p (h d) -> p h d", h=8), q3, q3, ALU.mult, ("qtm",), ("junk",))
        self.red(self.hss[:], self.junk[:].rearrange("p (h d) -> p h d", h=8), ("junk",), ("hss",))
        self.rsqrt(self.hss[:], self.hss[:], 1.0 / 96, ("hss",), ("hss",))
        self.tt("dve", q3, q3, self.hss[:, :, None].broadcast_to([128, 8, 96]), ALU.mult, ("qtm", "hss"), ("qtm",))
        self.tt("dve", q3, q3, gb[:, None, :].broadcast_to([128, 8, 96]), ALU.mult, ("qtm", gbk), ("qtm",))
        cos = self.cosT[:, cg, :][:, None, :].broadcast_to([128, 8, 16])
        sin = self.sinT[:, cg, :][:, None, :].broadcast_to([128, 8, 16])
        x1, x2 = self.qtm[:, :, 64:80], self.qtm[:, :, 80:96]
        self.cp("act", self.qr[:, :, 0:64], self.qtm[:, :, 0:64], ("qtm",), ("qr",))
        self.tt("dve", self.r1[:], x1, cos, ALU.mult, ("qtm", "cosT"), ("r1",))
        self.tt("dve", self.r2[:], x2, sin, ALU.mult, ("qtm", "sinT"), ("r2",))
        self.tt("dve", self.qr[:, :, 64:80], self.r1[:], self.r2[:], ALU.subtract, ("r1", "r2"), ("qr",))
        self.tt("dve", self.r1[:], x2, cos, ALU.mult, ("qtm", "cosT"), ("r1",))
        self.tt("dve", self.r2[:], x1, sin, ALU.mult, ("qtm", "sinT"), ("r2",))
        self.tt("dve", self.qr[:, :, 80:96], self.r1[:], self.r2[:], ALU.add, ("r1", "r2"), ("qr",))
        yield
        for g in range(2):
            b = self.bank()
            pb = self.psb(b)
            for hh in range(4):
                self.tr(b, pb[0:96, hh * 128:(hh + 1) * 128], self.qr[:, g * 4 + hh, :], self.ident_b[:], ("qr", "ident_b"))
            self.cp("act" if g else "dve", dst_T[0:96, g * 4:g * 4 + 4, col0:col0 + 128],
                    pb[0:96, 0:512].rearrange("p (h t) -> p h t", h=4), (psk(b),), (dst_key,))

    def mixer(self, l, I, x, xk):
        sin_ = self.scr_in[l]
        self.plan_loads(l)
        self.rmsnorm_T(l, 1, x, xk, False)
        yield
        wi, wik = self.load_blk(sin_[0], (("sin", l, 0),))
        for ch in range(4):
            b = self.fm_chunk(wi, wik, ch * 128)
            self.act(self.uT[:, ch, :], self.pt(b), AF.Gelu, (psk(b),), ("uT",))
        yield
        wi, wik = self.load_blk(sin_[1], (("sin", l, 1),))
        for c in range(NC):
            b = self.tm_chunk(wi, wik, c, 0, 512)
            self.act(self.vg[:], self.ps[b][:], AF.Gelu, (psk(b),), ("vg",))
            self.act(self.junk[:, 0:512], self.vg[:], AF.Square, ("vg",), ("junk",))
            self.red(self.s1[:, 0:1], self.junk[:, 0:512], ("junk",), ("s1",))
            self.rsqrt(self.s1[:, 0:1], self.s1[:, 0:1], 1.0 / 512, ("s1",), ("s1",))
            self.stt("dve", self.vtm[:, c, :], self.vg[:], self.s1[:, 0:1], self.gvb[:], ALU.mult, ALU.mult,
                     ("vg", "s1", "gvb"), (("vtm", c),))
        yield
        wi2, wi2k = self.load_blk(sin_[2], (("sin", l, 2),))
        wi3, wi3k = self.load_blk(sin_[3][:, :, 0:168], (("sin", l, 3, 0), ("sin", l, 3, 1)), ncols=168)
        for i in range(5):
            if i < 4:
                b = self.fm_chunk(wi2, wi2k, i * 128)
            else:
                b = self.fm_chunk(wi3, wi3k, 0)
            self.act(self.sqq[:, i, :], self.pt(b), AF.Square, (psk(b),), ("sqq",))
            if i < 3:
                self.ts("dve", self.cqnT[:, i, :], self.pt(b), self.gqn[:, i:i + 1], None, ALU.mult, None,
                        (psk(b), "gqn"), ("cqnT",))
            else:
                self.ts("dve", self.ckvnT[:, i - 3, :], self.pt(b), self.gkvn[:, i - 3:i - 2], None, ALU.mult, None,
                        (psk(b), "gkvn"), ("ckvnT",))
        for c in range(NC):
            b = self.tm_chunk(wi3, wi3k, c, 128, 40)
            self.cp("dve", self.krdt[:, c, :], self.ps[b][:, 0:40], (psk(b),), ("krdt",))
        yield
        wi, wik = self.load_blk(sin_[4], (("sin", l, 4),))
        for c in range(NC):
            b = self.tm_chunk(wi, wik, c, 0, 512)
            self.act(self.zs[:, c, :], self.ps[b][:], AF.Silu, (psk(b),), ("zs",))
        for blk in range(2):
            yield
            wi, wik = self.load_blk(sin_[5 + blk], (("sin", l, 5 + blk),))
            for cc in range(4):
                ch = blk * 4 + cc
                b = self.fm_chunk(wi, wik, cc * 128)
                xp = self.xbcp[ch % 2]
                xk = "xbcp%d" % (ch % 2)
                self.cp("act", xp[:, 3:TS + 3], self.pt(b), (psk(b),), (xk,))
                self.cp("dve", xp[:, 0:3], self.hist[:, ch, :], ("hist",), (xk,))
                self.ts("dve", self.cacc, xp[:, 3:TS + 3], self.cw[:, ch, 3:4], self.cb[:, ch:ch + 1], ALU.mult, ALU.add,
                        (xk, "cw", "cb"), ("vg",))
                for t in range(3):
                    self.stt("dve", self.cacc, xp[:, t:TS + t], self.cw[:, ch, t:t + 1], self.cacc, ALU.mult, ALU.add,
                             (xk, "cw", "vg"), ("vg",))
                self.cp("dve", self.hist[:, ch, :], xp[:, TS:TS + 3], (xk,), ("hist",))
                self.act(self.xbcT[:, ch, :], self.cacc, AF.Silu, ("vg",), ("xbcT",))
                yield
        wq = self.load_flat(self.scr_uq[l][:], (("suq", l),), 3, 768)
        wkv = self.load_flat(self.scr_ukv[l][:], (("sukv", l),), 2, 1024)
        for c in range(NC):
            yield from self.mla_chunk(c, I * NC + c, slice(c * 128, (c + 1) * 128), wq, wkv)
            yield

        def side():
            for c in range(NC):
                cg = I * NC + c
                ck = slice(c * 128, (c + 1) * 128)
                b = self.bank()
                for g in range(4):
                    self.mm(b, self.ps[b][:, g * 128:(g + 1) * 128], self.vtm[:, c, g * 128:(g + 1) * 128], self.WsT[:, g, :],
                            True, True, (("vtm", c), "WsT"))
                self.tt("dve", self.spb[:], self.ps[b][:].rearrange("p (g t) -> p g t", g=4), self.bsb[:], ALU.add,
                        (psk(b), "bsb"), ("spb",))
                self.tt("dve", self.yaT[:, :, ck], self.spb[:], self.uT[:, :, ck], ALU.mult, ("spb", "uT"), ("yaT",))
                yield from self.ssd_chunk(c, cg, ck)

        att = self.attention(I)
        sd = side()
        n_att = 8 * (NC * I + NC) + 8
        n_side = NC * 12
        acc = 0.0
        att_done = side_done = False
        while not (att_done and side_done):
            if not att_done:
                try:
                    next(att)
                except StopIteration:
                    att_done = True
            yield
            acc += n_side / n_att
            while (acc >= 1.0 or att_done) and not side_done:
                acc -= 1.0
                try:
                    next(sd)
                except StopIteration:
                    side_done = True
        yield from self.merge(l, x, xk)

    def ssd_chunk(self, c, cg, ck):
        xs = self.xbcT
        self.tt("dve", self.t8a[:], self.krdt[:, c, 32:40], self.dtb[:], ALU.add, ("krdt", "dtb"), ("t8a",))
        self.ts("dve", self.t8b[:], self.t8a[:], -1.0, None, ALU.mult, None, ("t8a",), ("t8b",))
        self.tt("dve", self.t8b[:], self.t8b[:], self.t8a[:], ALU.max, ("t8a", "t8b"), ("t8b",))
        self.act(self.t8b[:], self.t8b[:], AF.Exp, ("t8b",), ("t8b",), scale=-1.0)
        self.act(self.t8b[:], self.t8b[:], AF.Ln, ("t8b", "one_t"), ("t8b",), bias=self.one_t[:])
        self.stt("dve", self.dt8[:], self.t8a[:], 0.0, self.t8b[:], ALU.max, ALU.add, ("t8a", "t8b"), ("dt8",))
        self.tt("dve", self.da8[:], self.dt8[:], self.aneg[:], ALU.mult, ("dt8", "aneg"), ("da8",))
        yield
        b = self.bank()
        self.mm(b, self.ps[b][:, 0:8], self.U_f[:], self.da8[:], True, True, ("U_f", "da8"))
        self.cp("dve", self.cs8[:], self.ps[b][:, 0:8], (psk(b),), ("cs8",))
        self.tt("dve", self.R[:], self.U_f[:, None, :].broadcast_to([128, 8, 128]),
                self.da8[:, :, None].broadcast_to([128, 8, 128]), ALU.mult, ("U_f", "da8"), ("R",))
        yield
        bb = [self.bank(), self.bank()]
        for hh in range(2):
            self.mm(bb[hh], self.ps[bb[hh]][:], self.ones_f[:], self.R[:, hh * 4:hh * 4 + 4, :].rearrange("p h l -> p (h l)"),
                    True, True, ("ones_f", "R"))
        for hh in range(2):
            p3 = self.ps[bb[hh]][:].rearrange("p (h l) -> p h l", h=4)
            hs = slice(hh * 4, hh * 4 + 4)
            self.tt("dve", self.arg[:, hs, :], p3, self.cs8[:, hs, None].broadcast_to([128, 4, 128]), ALU.subtract,
                    (psk(bb[hh]), "cs8"), ("arg",))
            self.act(self.ecs[:, hs, :], p3, AF.Exp, (psk(bb[hh]),), ("R",))
            self.cp("dve", self.tot8[:, hs], p3[:, :, 127], (psk(bb[hh]),), ("tot8",))
        self.act(self.arg[:], self.arg[:], AF.Relu, ("arg",), ("arg",), scale=-1.0)
        self.act(self.arg[:], self.arg[:], AF.Exp, ("arg",), ("arg",), scale=-1.0)
        self.tt("dve", self.t8a[:], self.tot8[:], self.cs8[:], ALU.subtract, ("tot8", "cs8"), ("t8a",))
        self.act(self.dout8[:], self.t8a[:], AF.Exp, ("t8a",), ("dout8",))
        self.act(self.etot8[:], self.tot8[:], AF.Exp, ("tot8",), ("etot8",))
        yield
        b = self.bank()
        for g in range(2):
            self.mm(b, self.ps[b][:, g * 128:(g + 1) * 128], xs[:, 4 + g, ck], xs[:, 6 + g, ck], True, True, ("xbcT",))
        self.tt("dve", self.cbm[:], self.ps[b][:, 0:256].rearrange("p (g l) -> p g l", g=2),
                self.U_f[:, None, :].broadcast_to([128, 2, 128]), ALU.mult, (psk(b), "U_f"), ("cbm",))
        self.tt("dve", self.MT[:].rearrange("p (g r) l -> p g r l", g=2), self.arg[:].rearrange("p (g r) l -> p g r l", g=2),
                self.cbm[:, :, None, :].broadcast_to([128, 2, 4, 128]), ALU.mult, ("arg", "cbm"), ("MT",))
        self.tt("dve", self.Cs[:].rearrange("p (g r) l -> p g r l", g=2), self.ecs[:].rearrange("p (g r) l -> p g r l", g=2),
                xs[:, 6:8, ck][:, :, None, :].broadcast_to([128, 2, 4, 128]), ALU.mult, ("R", "xbcT"), ("Cs",))
        yield
        b = self.bank()
        pb = self.psb(b)
        for i in range(4):
            self.tr(b, pb[:, i * 128:(i + 1) * 128], xs[:, i, ck], self.ident_b[:], ("xbcT", "ident_b"))
        x3 = pb[:, 0:512].rearrange("p (h d) -> p h d", h=8)
        self.tt("dve", self.xdt[:], x3, self.dt8[:, :, None].broadcast_to([128, 8, 64]), ALU.mult, (psk(b), "dt8"), ("xdt",))
        self.tt("dve", self.xsd[:], x3, self.Dsk[:, :, None].broadcast_to([128, 8, 64]), ALU.mult, (psk(b), "Dsk"), ("xsd",))
        self.tt("dve", self.xdd[:], self.xdt[:], self.dout8[:, :, None].broadcast_to([128, 8, 64]), ALU.mult,
                ("xdt", "dout8"), ("xdd",))
        yield
        b = self.bank()
        pb = self.psb(b)
        for g in range(2):
            self.tr(b, pb[:, g * 128:(g + 1) * 128], xs[:, 4 + g, ck], self.ident_b[:], ("xbcT", "ident_b"))
        self.cp("act", self.Btm[:], pb[:, 0:256].rearrange("p (g n) -> p g n", g=2), (psk(b),), ("Btm",))
        yield
        b = self.bank()
        for h in range(8):
            o = self.ps[b][:, h * 64:(h + 1) * 64]
            self.mm(b, o, self.MT[:, h, :], self.xdt[:, h, :], True, False, ("MT", "xdt"))
            self.mm(b, o, self.Cs[:, h, :], self.Sbf[:, h, :], False, True, ("Cs", "Sbf"))
        self.tt("dve", self.y1[:], self.ps[b][:], self.xsd[:].rearrange("p h d -> p (h d)"), ALU.add, (psk(b), "xsd"), ("vg",))
        self.tt("dve", self.y1[:], self.y1[:], self.zs[:, c, :], ALU.mult, ("vg", "zs"), ("vg",))
        self.act(self.junk[:, 0:512], self.y1[:], AF.Square, ("vg",), ("junk",))
        self.red(self.s2[:, 0:2], self.junk[:, 0:512].rearrange("p (g d) -> p g d", g=2), ("junk",), ("s2",))
        self.rsqrt(self.s2[:, 0:2], self.s2[:, 0:2], 1.0 / 256, ("s2",), ("s2",))
        for g in range(2):
            gs = slice(g * 256, (g + 1) * 256)
            self.stt("dve", self.yctm[:, gs], self.y1[:, gs], self.s2[:, g:g + 1], self.gsn[:, gs], ALU.mult, ALU.mult,
                     ("vg", "s2", "gsn"), ("yctm",))
        yield
        b = self.bank()
        pb = self.psb(b)
        for i in range(4):
            self.tr(b, pb[:, i * 128:(i + 1) * 128], self.yctm[:, i * 128:(i + 1) * 128], self.ident_b[:], ("yctm", "ident_b"))
        self.cp("act", self.ycT[:, :, ck], pb[:, 0:512].rearrange("p (i t) -> p i t", i=4), (psk(b),), ("ycT",))
        yield
        b = self.bank()
        for h in range(8):
            self.mm(b, self.ps[b][:, h * 64:(h + 1) * 64], self.Btm[:, h // 4, :], self.xdd[:, h, :], True, True, ("Btm", "xdd"))
        self.tt("dve", self.Sst[:], self.Sst[:], self.etot8[:, :, None].broadcast_to([128, 8, 64]), ALU.mult,
                ("Sst", "etot8"), ("Sst",))
        self.tt("dve", self.Sst[:], self.Sst[:], self.ps[b][:].rearrange("p (h d) -> p h d", h=8), ALU.add,
                ("Sst", psk(b)), ("Sst",))
        self.cp("dve", self.Sbf[:], self.Sst[:], ("Sst",), ("Sbf",))
        yield

    def mla_chunk(self, c, cg, ck, wq, wkv):
        Wuq, Wuqk = wq
        Wukv, Wukvk = wkv
        b = self.bank()
        for i in range(3):
            self.mm(b, self.ps[b][:, 0:1], self.sqq[:, i, ck], self.ones_b[:, 0:1], i == 0, i == 2, ("sqq", "ones_b"))
        for i in range(2):
            self.mm(b, self.ps[b][:, 1:2], self.sqq[:, 3 + i, ck], self.ones_b[:, 0:1], i == 0, i == 1, ("sqq", "ones_b"))
        self.rsqrt(self.rq2[:, 0:1], self.ps[b][:, 0:1], 1.0 / 384, (psk(b),), ("rq2",))
        self.rsqrt(self.rq2[:, 1:2], self.ps[b][:, 1:2], 1.0 / 256, (psk(b),), ("rq2",))
        for half in range(2):
            b = self.bank()
            for i in range(3):
                self.mm(b, self.ps[b][:, 0:384], self.cqnT[:, i, ck], Wuq[:, i, half * 384:(half + 1) * 384], i == 0,
                        i == 2, ("cqnT", Wuqk))
            self.ts("dve", self.qtm[:, half * 4:half * 4 + 4, :], self.ps[b][:, 0:384].rearrange("p (h d) -> p h d", h=4),
                    self.rq2[:, 0:1], None, ALU.mult, None, (psk(b), "rq2"), ("qtm",))
        yield from self.head_norm_rope(cg, self.gqb, "gqb", self.qT, "qT", c * 128)
        yield
        for half in range(2):
            b = self.bank()
            for i in range(2):
                self.mm(b, self.ps[b][:], self.ckvnT[:, i, ck], Wukv[:, i, half * 512:(half + 1) * 512], i == 0, i == 1,
                        ("ckvnT", Wukvk))
            p3 = self.ps[b][:].rearrange("p (h d) -> p h d", h=4)
            hs = slice(half * 4, half * 4 + 4)
            self.ts("dve", self.vst[:, hs, 0:64], p3[:, :, 64:128], self.rq2[:, 1:2], None, ALU.mult, None,
                    (psk(b), "rq2"), ("vst",))
            self.ts("dve", self.qtm[:, hs, 0:64], p3[:, :, 0:64], self.rq2[:, 1:2], None, ALU.mult, None,
                    (psk(b), "rq2"), ("qtm",))
        self.cp("dve", self.qtm[:, :, 64:96], self.krdt[:, c, 0:32][:, None, :].broadcast_to([128, 8, 32]), ("krdt",), ("qtm",))
        self.dma("sp", self.vscr[:, :, cg, :].rearrange("h p d -> p h d"), self.vst[:], ("vst", "vst1"), (("vscr", cg),))
        yield from self.head_norm_rope(cg, self.gkb, "gkb", self.kst, "kst", 0)
        self.dma("sp", self.kscr[:, :, cg * 128:(cg + 1) * 128].rearrange("h d t -> d h t"), self.kst[0:96, :, :], ("kst",),
                 (("kscr", cg),))

    def attention(self, I):
        scale = 96.0 ** -0.5
        nj = NC * I + NC
        st = {}

        def prep(h):
            kh = self.kh[self.kv_i % 2]
            kk = "kh%d" % (self.kv_i % 2)
            self.kv_i += 1
            vh, vk = self.vh[0], "vh0"
            skeys = tuple(("kscr", j) for j in range(nj))
            vkeys = tuple(("vscr", j) for j in range(nj))
            self.dma("sp", kh[0:96, 0:nj * 128], self.kscr[h, :, 0:nj * 128], skeys, (kk,))
            st[h] = [kh, kk, vh, vk, None]
            return vkeys

        def load_v(h, vkeys):
            self.dma("sp", st[h][2][:, 0:nj, :], self.vscr[h, :, 0:nj, :], vkeys, (st[h][3],))

        def A(h, j):
            kh, kk = st[h][0], st[h][1]
            c0 = max(0, j - NC * I)
            bs = self.bank()
            self.mm(bs, self.ps[bs][:, c0 * 128:TS], kh[0:96, j * 128:(j + 1) * 128], self.qT[0:96, h, c0 * 128:TS],
                    True, True, (kk, "qT"))
            pT = self.pT[self.pt_i % 4]
            pk = "pT%d" % (self.pt_i % 4)
            self.pt_i += 1
            self.act(pT[:, c0 * 128:TS], self.ps[bs][:, c0 * 128:TS], AF.Exp, (psk(bs), "negC"), (pk,),
                     bias=self.negC[:], scale=scale)
            if j >= NC * I:
                self.tt("dve", pT[:, c0 * 128:(c0 + 1) * 128], pT[:, c0 * 128:(c0 + 1) * 128], self.U_b[:], ALU.mult,
                        (pk, "U_b"), (pk,))
            return pT, pk, c0

        def B(h, j, pT, pk, c0):
            vh, vk = st[h][2], st[h][3]
            if j == 0:
                st[h][4] = self.bank()
                self.reserved.add(st[h][4])
            bacc = st[h][4]
            for c in range(c0, NC):
                self.mm(bacc, self.ps[bacc][:, c * 128:c * 128 + 65], pT[:, c * 128:(c + 1) * 128], vh[:, j, :],
                        (j == 0 and c == 0), (j == NC * I + c), (pk, vk), skip=True)
            if j == nj - 1:
                a3 = self.ps[bacc][:].rearrange("p (c d) -> p c d", c=4)
                self.P.add("dve", lambda e, a3=a3: e.reciprocal(out=self.rden[:, 0:NC], in_=a3[:, 0:NC, 64]), (psk(bacc),),
                           ("rden",))
                self.tt("dve", self.ybtm[:, :, h * 64:(h + 1) * 64], a3[:, 0:NC, 0:64],
                        self.rden[:, 0:NC, None].broadcast_to([128, NC, 64]), ALU.mult, (psk(bacc), "rden"), ("ybtm",))
                self.reserved.discard(bacc)

        LA = 3
        pend = []
        vload = {}
        for h in range(8):
            vkeys = prep(h)
            for j in range(nj):
                cur = A(h, j)
                pend.append((h, j) + cur)
                if j == 0:
                    vload[h] = vkeys
                while len(pend) > LA:
                    p = pend.pop(0)
                    if p[1] == 0:
                        load_v(p[0], vload.pop(p[0]))
                    B(*p)
                yield
        while pend:
            p = pend.pop(0)
            if p[1] == 0:
                load_v(p[0], vload.pop(p[0]))
            B(*p)
        yield
        for c in range(NC):
            b = self.bank()
            pb = self.psb(b)
            for i in range(4):
                self.tr(b, pb[:, i * 128:(i + 1) * 128], self.ybtm[:, c, i * 128:(i + 1) * 128], self.ident_b[:],
                        ("ybtm", "ident_b"))
            self.cp("act", self.ybT[:, :, c * 128:(c + 1) * 128], pb[:, 0:512].rearrange("p (i t) -> p i t", i=4),
                    (psk(b),), ("ybT",))

    def merge(self, l, x, xk):
        ys = ((self.yaT, "yaT"), (self.ybT, "ybT"), (self.ycT, "ycT"))
        for i in range(3):
            if not (self.brmask >> i) & 1:
                self.P.add("dve", lambda e, t=ys[i][0]: e.memset(t[:], 0.0), (), (ys[i][1],))
        for half in range(2):
            for i in range(3):
                wg, wgk = self.load_blk(self.scr_in[l][7 + 2 * i + half], (("sin", l, 7 + 2 * i + half),))
                wb, wbk = self.load_blk(self.scr_br[l][i][:, :, half * 512:(half + 1) * 512], (("sbr", l, i),), nk=4)
                for m in range(4):
                    bg = self.fm_chunk(wg, wgk, m * 128)
                    sg = self.sig[self.sig_i % 2]
                    sgk = "sig%d" % (self.sig_i % 2)
                    self.sig_i += 1
                    self.act(sg[:], self.pt(bg), AF.Sigmoid, (psk(bg),), (sgk,))
                    bb = self.fm_chunk(wb, wbk, m * 128, nk=4, rhs=ys[i][0], rkey=ys[i][1])
                    if i == 0:
                        self.tt("dve", self.mgacc[:, m, :], sg[:], self.pt(bb), ALU.mult, (sgk, psk(bb)), ("mgacc",))
                    else:
                        self.tt("dve", sg[:], sg[:], self.pt(bb), ALU.mult, (sgk, psk(bb)), (sgk,))
                        if i == 1:
                            self.tt("dve", self.mgacc[:, m, :], self.mgacc[:, m, :], sg[:], ALU.add, (sgk, "mgacc"), ("mgacc",))
                        else:
                            self.tt("dve", self.mgT[:, half * 4 + m, :], self.mgacc[:, m, :], sg[:], ALU.add, (sgk, "mgacc"),
                                    ("mgT",))
                    yield
        for half in range(2):
            wo, wok = self.load_blk(self.scr_o[l][half], (("so", l, half),))
            for m in range(4):
                b = self.fm_chunk(wo, wok, m * 128, rhs=self.mgT, rkey="mgT")
                k = half * 4 + m
                self.tt("dve", x[:, k, :], x[:, k, :], self.pt(b), ALU.add, (xk, psk(b)), (xk,))
            yield

    def build(self):
        assert NC == 2
        self.declare()
        self.alloc()
        self.consts()
        finals = []
        L, NT = self.L, self.NT
        convs = [self.conv_list(l) for l in range(L)]
        for fn in convs[0]:
            fn()
        slots = [(l, I) for l in range(L) for I in range(NT)]
        nslot = len(slots)

        def xt(s):
            return self.xTs[s % 2], "xT%d" % (s % 2)

        def fstream(s):
            if s - 1 >= 0:
                l, I = slots[s - 1]
                x, xk = xt(s - 1)
                if self.do_ffn:
                    yield from self.ffn(l, 2, x, xk)
                finals.extend(self.store_x(l, I, x, xk))
                yield
            if s + 1 < nslot:
                l, I = slots[s + 1]
                x, xk = xt(s + 1)
                if I == 0:
                    self.load_gains(l)
                self.load_x(l, I, x, xk)
                yield
                if self.do_ffn:
                    yield from self.ffn(l, 1, x, xk)

        def run(*gens, weights=None):
            gens = [g for g in gens if g is not None]
            alive = [True] * len(gens)
            acc = [0.0] * len(gens)
            w = weights or [1.0] * len(gens)
            while any(alive):
                for i, g in enumerate(gens):
                    if not alive[i]:
                        continue
                    acc[i] += w[i]
                    while acc[i] >= 1.0 and alive[i]:
                        acc[i] -= 1.0
                        try:
                            k = next(g)
                            if k:
                                acc[i] -= float(k)
                        except StopIteration:
                            alive[i] = False

        self.load_gains(0)
        self.load_x(0, 0, *xt(0))
        if self.do_ffn:
            run(self.ffn(0, 1, *xt(0)))
        for s, (l, I) in enumerate(slots):
            if I == 0 and self.do_mix:
                self.load_params(l)
            nxt = convs[l + 1] if l + 1 < L else []
            per = (len(nxt) + NT - 1) // NT if nxt else 0
            for fn in nxt[I * per:(I + 1) * per]:
                fn()
            x, xk = xt(s)
            n_f = 2 * (1 + NJ + 2 * NJJ + 2) + 2
            if self.do_mix:
                n_m = 24 + 2 * (8 * (NC * I + NC) + 8) + 30
                run(self.mixer(l, I, x, xk), fstream(s), weights=[1.0, min(1.0, n_f / n_m)])
            else:
                run(fstream(s))
        run(fstream(nslot))
        self.P.emit(finals)
        self.es.close()
        return self.nc


_CACHE = {}


def kernel(**inputs):
    x = np.asarray(inputs["x"])
    B, S, _ = x.shape
    L = int(np.asarray(inputs["ffn1_norm"]).shape[0])
    key = (S, L)
    if key not in _CACHE:
        _CACHE[key] = Builder(S, L).build()
    nc = _CACHE[key]
    shared = {k: np.ascontiguousarray(np.asarray(v)) for k, v in inputs.items() if k not in ("x", "positions")}
    pos = np.asarray(inputs["positions"]).astype(np.int32)
    in_maps = []
    for b in range(B):
        m = dict(shared)
        m["x"] = np.ascontiguousarray(x[b])
        m["positions"] = np.ascontiguousarray(pos[b])
        in_maps.append(m)
    res = run_bass_kernel_spmd(nc, in_maps, core_ids=list(range(B)))
    return np.stack([np.asarray(r["out"]) for r in res.results], axis=0).astype(np.float32)
```

```python
import contextlib
import numpy as np
import concourse.bass as bass
import concourse.mybir as mybir
from concourse.bass_utils import run_bass_kernel_spmd

F32 = mybir.dt.float32
BF16 = mybir.dt.bfloat16
I32 = mybir.dt.int32
AF = mybir.ActivationFunctionType
ALU = mybir.AluOpType
AX = mybir.AxisListType

D = 1024
DFF = 2816
EPS = 1e-6
TS = 256
NC = TS // 128
NK = 8
NJ = DFF // 128
NJJ = NJ // 2
IN_COLS = 6312

ENGS = ("pe", "act", "dve", "pool", "sp")
NDMA_SEM = 12


class Op:
    __slots__ = ("eng", "fn", "dma", "idx", "deps", "signal", "sigval", "dma_i")

    def __init__(self, eng, fn, dma):
        self.eng = eng
        self.fn = fn
        self.dma = dma
        self.deps = None
        self.signal = False
        self.sigval = 0
        self.dma_i = -1


class Prog:
    def __init__(self, nc):
        self.nc = nc
        self.ops = {e: [] for e in ENGS}
        self.last_w = {}
        self.readers = {}
        self.dma_count = {e: 0 for e in ENGS}
        self.dma_ops = {e: [] for e in ENGS}

    @staticmethod
    def _stream(op):
        return op.eng + ":dma" if op.dma else op.eng

    def add(self, eng, fn, reads=(), writes=(), dma=False):
        op = Op(eng, fn, dma)
        op.idx = len(self.ops[eng])
        deps = {}
        if eng != "pe" and not dma:
            pk = tuple(k for k in reads if isinstance(k, str) and k[:2] == "ps" and k[2:].isdigit())
            if pk:
                writes = tuple(writes) + pk

        def dep(o, raw):
            if o is None:
                return
            if (not o.dma) and o.eng == eng and not dma:
                if eng == "pe" or not raw:
                    return
            s = self._stream(o)
            cur = deps.get(s)
            if cur is None or cur.idx < o.idx:
                deps[s] = o

        for k in reads:
            dep(self.last_w.get(k), True)
        for k in writes:
            dep(self.last_w.get(k), False)
            rd = self.readers.get(k)
            if rd:
                for o in rd.values():
                    dep(o, False)
        if dma:
            i = self.dma_count[eng]
            op.dma_i = i
            self.dma_count[eng] = i + 1
            self.dma_ops[eng].append(op)
            lim = NDMA_SEM if eng != "pool" else 4
            if i >= lim:
                o = self.dma_ops[eng][i - lim]
                s = self._stream(o)
                cur = deps.get(s)
                if cur is None or cur.idx < o.idx:
                    deps[s] = o
        op.deps = list(deps.values())
        for o in op.deps:
            o.signal = True
        st = self._stream(op)
        for k in reads:
            self.readers.setdefault(k, {})[st] = op
        for k in writes:
            self.last_w[k] = op
            self.readers[k] = {}
        self.ops[eng].append(op)
        return op

    def emit(self, final_ops):
        nc = self.nc
        for o in final_ops:
            o.signal = True
        with contextlib.ExitStack() as es:
            csem = {}
            for e in ("pe", "act", "dve", "pool"):
                csem[e] = es.enter_context(nc.semaphore("c_" + e))
            dsem = {}
            for e in ENGS:
                if self.dma_count[e]:
                    dsem[e] = [es.enter_context(nc.semaphore("d_%s_%d" % (e, i))) for i in range(NDMA_SEM)]
            for e in ENGS:
                c = 0
                for op in self.ops[e]:
                    if op.dma:
                        op.sigval = 16 * (op.dma_i // NDMA_SEM + 1)
                    elif op.signal:
                        c += 1
                        op.sigval = c
            block = es.enter_context(nc.Block())
            engobj = {"pe": block.tensor, "act": block.scalar, "dve": block.vector, "pool": block.gpsimd,
                      "sp": block.sync}

            def make(e):
                def body(eng):
                    waited = {}
                    for op in self.ops[e]:
                        for d in op.deps:
                            if d.dma:
                                sem = dsem[d.eng][d.dma_i % NDMA_SEM]
                                key = (d.eng, d.dma_i % NDMA_SEM)
                            else:
                                sem = csem[d.eng]
                                key = d.eng
                            if waited.get(key, 0) >= d.sigval:
                                continue
                            waited[key] = d.sigval
                            eng.wait_ge(sem, d.sigval)
                        ins = op.fn(eng)
                        if op.dma:
                            ins.then_inc(dsem[e][op.dma_i % NDMA_SEM], 16)
                        elif op.signal:
                            ins.then_inc(csem[e], 1)
                    if e == "sp":
                        for o in final_ops:
                            if o.dma:
                                eng.wait_ge(dsem[o.eng][o.dma_i % NDMA_SEM], o.sigval)
                            else:
                                eng.wait_ge(csem[o.eng], o.sigval)
                return body

            for e in ENGS:
                if self.ops[e] or e == "sp":
                    engobj[e](make(e))


NWI = 3
FQ = "sp"
INV_FREQ = (1.0 / (np.float32(10000.0) ** (np.arange(0, 32, 2, dtype=np.float32) / np.float32(32)))).astype(np.float32)
PI = float(np.pi)


def psk(b):
    return "ps%d" % b


class Builder:
    def __init__(self, S, L, do_mix=True, do_ffn=True, mix_parts=(1, 1, 1)):
        self.S = S
        self.L = L
        self.NT = S // TS
        self.NCH = S // 128
        self.do_mix = do_mix
        self.do_ffn = do_ffn
        self.mix_parts = mix_parts
        self.nc = bass.Bass("TRN2", target_bir_lowering=False)
        self.P = Prog(self.nc)
        self.es = contextlib.ExitStack()
        self.bank_i = 0
        self.reserved = set()
        self.pending_conv = []
        self.stage = 8
        self.brmask = 7

    def sb(self, name, shape, dt):
        return self.es.enter_context(self.nc.sbuf_tensor(name, shape, dt))

    def dram_in(self, name, shape, dt=F32):
        return self.nc.dram_tensor(name, list(shape), dt, kind="ExternalInput")

    def bank(self):
        while self.bank_i in self.reserved:
            self.bank_i = (self.bank_i + 1) % 8
        b = self.bank_i
        self.bank_i = (self.bank_i + 1) % 8
        return b

    def pt(self, b):
        return self.ps[b][:, 0:TS]

    def psb(self, b):
        return self.ps[b][:].bitcast(BF16)

    def mm(self, b, out, lhsT, rhs, start, stop, reads, skip=False):
        if skip:
            self.P.add("pe", lambda e: e.matmul(out, lhsT, rhs, start=start, stop=stop, skip_group_check=True), reads,
                       (psk(b),))
        else:
            self.P.add("pe", lambda e: e.matmul(out, lhsT, rhs, start=start, stop=stop), reads, (psk(b),))

    def tr(self, b, out, in_, ident, reads):
        self.P.add("pe", lambda e: e.transpose(out, in_, ident), reads, (psk(b),))

    def act(self, out, in_, func, reads, writes, bias=None, scale=None):
        kw = {}
        if bias is not None:
            kw["bias"] = bias
        if scale is not None:
            kw["scale"] = scale
        self.P.add("act", lambda e: e.activation(out=out, in_=in_, func=func, **kw), reads, writes)

    def ts(self, eng, out, in0, s1, s2, op0, op1, reads, writes):
        if op1 is None:
            self.P.add(eng, lambda e: e.tensor_scalar(out=out, in0=in0, scalar1=s1, scalar2=None, op0=op0), reads, writes)
        else:
            self.P.add(eng, lambda e: e.tensor_scalar(out=out, in0=in0, scalar1=s1, scalar2=s2, op0=op0, op1=op1),
                       reads, writes)

    def rsqrt(self, out, in_, scale, reads, writes):
        self.act(out, in_, AF.Sqrt, tuple(reads) + ("eps_t",), writes, bias=self.eps_t[:out.shape[0], :], scale=scale)
        self.P.add("dve", lambda e: e.reciprocal(out=out, in_=out), writes, writes)

    def stt(self, eng, out, in0, scalar, in1, op0, op1, reads, writes):
        self.P.add(eng, lambda e: e.scalar_tensor_tensor(out=out, in0=in0, scalar=scalar, in1=in1, op0=op0, op1=op1),
                   reads, writes)

    def tt(self, eng, out, in0, in1, op, reads, writes):
        self.P.add(eng, lambda e: e.tensor_tensor(out=out, in0=in0, in1=in1, op=op), reads, writes)

    def red(self, out, in_, reads, writes):
        self.P.add("dve", lambda e: e.tensor_reduce(out=out, in_=in_, axis=AX.X, op=ALU.add), reads, writes)

    def cp(self, eng, out, in_, reads, writes):
        if eng == "act":
            self.P.add("act", lambda e: e.copy(out=out, in_=in_), reads, writes)
        else:
            self.P.add(eng, lambda e: e.tensor_copy(out=out, in_=in_), reads, writes)

    def dma(self, q, out, in_, reads, writes, slow=False):
        if slow:
            return self.P.add(q, lambda e: e.dma_start(out=out, in_=in_, allow_slow_non_contiguous=True), reads, writes,
                              dma=True)
        return self.P.add(q, lambda e: e.dma_start(out=out, in_=in_), reads, writes, dma=True)

    def declare(self):
        nc, S, L = self.nc, self.S, self.L
        self.x_in = self.dram_in("x", (S, D))
        self.pos_in = self.dram_in("positions", (S,), I32)
        names = [
            ("ffn1_norm", (L, D)), ("ffn1_w_in", (L, D, 2 * DFF)), ("ffn1_w_out", (L, DFF, D)),
            ("mix_norm", (L, D)), ("w_in", (L, D, IN_COLS)), ("gm_v_norm", (L, 512)),
            ("gm_w_s", (L, 4, 128, 128)), ("gm_b_s", (L, 4, 128)), ("mla_q_norm", (L, 384)),
            ("mla_kv_norm", (L, 256)), ("mla_w_uq", (L, 384, 768)), ("mla_w_ukv", (L, 256, 1024)),
            ("mla_q_gain", (L, 96)), ("mla_k_gain", (L, 96)), ("ssd_conv_w", (L, 4, 1024)),
            ("ssd_conv_b", (L, 1024)), ("ssd_dt_bias", (L, 8)), ("ssd_a_log", (L, 8)), ("ssd_d", (L, 8)),
            ("ssd_norm", (L, 512)), ("w_branch", (L, 3, 512, D)), ("w_out", (L, D, D)),
            ("ffn2_norm", (L, D)), ("ffn2_w_in", (L, D, 2 * DFF)), ("ffn2_w_out", (L, DFF, D)),
        ]
        self.W = {}
        for n, shp in names:
            self.W[n] = self.dram_in(n, shp)
        self.out = nc.dram_tensor("out", [S, D], F32, kind="ExternalOutput")
        self.xscr = nc.dram_tensor("xscr", [128, NK, S], F32, kind="Internal")
        self.kscr = nc.dram_tensor("kscr", [8, 96, S], BF16, kind="Internal")
        self.vscr = nc.dram_tensor("vscr", [8, 128, S // 128, 65], BF16, kind="Internal")
        self.scr_wi, self.scr_wo, self.scr_in, self.scr_br, self.scr_o, self.scr_uq, self.scr_ukv = {}, {}, {}, {}, {}, {}, {}
        for l in range(L):
            for f in (1, 2):
                self.scr_wi[(l, f)] = nc.dram_tensor("swi_%d_%d" % (l, f), [NJ, 128, NK, 256], BF16, kind="Internal")
                self.scr_wo[(l, f)] = nc.dram_tensor("swo_%d_%d" % (l, f), [2, NJJ, 128, 2, 512], BF16, kind="Internal")
            self.scr_in[l] = nc.dram_tensor("sin_%d" % l, [13, 128, NK, 512], BF16, kind="Internal")
            self.scr_br[l] = nc.dram_tensor("sbr_%d" % l, [3, 128, 4, D], BF16, kind="Internal")
            self.scr_o[l] = nc.dram_tensor("so_%d" % l, [2, 128, NK, 512], BF16, kind="Internal")
            self.scr_uq[l] = nc.dram_tensor("suq_%d" % l, [128, 3, 768], BF16, kind="Internal")
            self.scr_ukv[l] = nc.dram_tensor("sukv_%d" % l, [128, 2, 1024], BF16, kind="Internal")

    def alloc(self):
        nc = self.nc
        sb = self.sb
        NCH = self.NCH
        self.ident_f = sb("ident_f", [128, 128], F32)
        self.ident_b = sb("ident_b", [128, 128], BF16)
        self.ones_b = sb("ones_b", [128, 128], BF16)
        self.ones_f = sb("ones_f", [128, 128], F32)
        self.U_f = sb("U_f", [128, 128], F32)
        self.U_b = sb("U_b", [128, 128], BF16)
        self.Lm_f = sb("Lm_f", [128, 128], F32)
        self.eps_t = sb("eps_t", [128, 1], F32)
        self.one_t = sb("one_t", [128, 1], F32)
        self.negpi_t = sb("negpi_t", [128, 1], F32)
        self.xTs = [sb("xT%d" % i, [128, NK, TS], F32) for i in range(2)]
        self.hT = sb("hT", [128, NK, TS], BF16)
        self.hTf = sb("hTf", [128, NK, TS], BF16)
        self.sq = [sb("sq%d" % i, [128, TS], BF16) for i in range(2)]
        self.sqf = [sb("sqf%d" % i, [128, TS], BF16) for i in range(2)]
        self.rstd = sb("rstd", [128, TS], F32)
        self.rstdf = sb("rstdf", [128, TS], F32)
        self.gains = sb("gains", [128, 2, 3, NK], F32)
        self.Rarg = sb("Rarg", [128, 16, 128], F32)
        self.wif = [sb("wif%d" % i, [128, NK, 256], BF16) for i in range(4)]
        self.wif_i = 0
        self.actT = sb("actT", [128, NJ, TS], BF16)
        self.sg = [sb("sg%d" % i, [128, TS], F32) for i in range(2)]
        self.wi = [sb("wi%d" % i, [128, NK, 512], BF16) for i in range(NWI)]
        self.wo = [sb("wo%d" % i, [128, 2, 512], BF16) for i in range(4)]
        self.ps = [self.es.enter_context(nc.psum_tensor("ps%d" % i, [128, 512], F32)) for i in range(8)]
        self.wi_i = self.wo_i = self.sq_i = self.sqf_i = self.sg_i = self.sig_i = self.pt_i = 0
        if not self.do_mix:
            return
        self.pos_i = sb("pos_i", [128, NCH], I32)
        self.pos_f = sb("pos_f", [128, NCH], F32)
        self.invf = sb("invf", [128, 16], F32)
        av = self.actT[:].rearrange("p j t -> p (j t)").bitcast(F32)
        self.xtm = av[:, 0:NC * D].rearrange("p (c d) -> p c d", c=NC)
        xv = self.xTs[1][:].rearrange("p k t -> p (k t)")
        n16 = NCH * 16
        self.ang = xv[:, 0:n16].rearrange("p (c j) -> p c j", j=16)
        self.angf = xv[:, n16:2 * n16].rearrange("p (c j) -> p c j", j=16)
        self.angi = xv[:, 2 * n16:3 * n16].bitcast(I32).rearrange("p (c j) -> p c j", j=16)
        self.cosT = sb("cosT", [128, NCH, 16], F32)
        self.sinT = sb("sinT", [128, NCH, 16], F32)
        self.gvb = sb("gvb", [128, 512], F32)
        self.bsb = sb("bsb", [128, 4, 128], F32)
        self.Wsraw = sb("Wsraw", [128, 4, 128], F32)
        self.Wsm = sb("Wsm", [128, 4, 128], BF16)
        self.WsT = sb("WsT", [128, 4, 128], BF16)
        self.gqn = sb("gqn", [128, 3], F32)
        self.gkvn = sb("gkvn", [128, 2], F32)
        self.gqb = sb("gqb", [128, 96], F32)
        self.gkb = sb("gkb", [128, 96], F32)
        self.gmx = sb("gmx", [128, 2], F32)
        self.negC = sb("negC", [128, 1], F32)
        self.cw = sb("cw", [128, NK, 4], F32)
        self.cb = sb("cb", [128, NK], F32)
        self.dtb = sb("dtb", [128, 8], F32)
        self.aneg = sb("aneg", [128, 8], F32)
        self.Dsk = sb("Dsk", [128, 8], F32)
        self.gsn = sb("gsn", [128, 512], F32)
        self.uT = sb("uT", [128, 4, TS], BF16)
        self.vg = sb("vg", [128, 512], F32)
        self.junk = sb("junk", [128, 768], F32)
        self.s1 = sb("s1", [128, 8], F32)
        self.s2 = sb("s2", [128, 8], F32)
        self.vtm = sb("vtm", [128, NC, 512], BF16)
        self.zs = sb("zs", [128, NC, 512], BF16)
        self.cqnT = sb("cqnT", [128, 3, TS], BF16)
        self.ckvnT = sb("ckvnT", [128, 2, TS], BF16)
        self.sqq = sb("sqq", [128, 5, TS], BF16)
        self.krdt = sb("krdt", [128, NC, 40], F32)
        self.xbcp = [sb("xbcp%d" % i, [128, TS + 3], F32) for i in range(2)]
        self.hist = sb("hist", [128, NK, 3], F32)
        self.cacc = self.vg[:, 0:TS]
        self.xbcT = sb("xbcT", [128, NK, TS], BF16)
        self.spb = sb("spb", [128, 4, 128], F32)
        self.yaT = sb("yaT", [128, 4, TS], BF16)
        self.dt8 = sb("dt8", [128, 8], F32)
        self.da8 = sb("da8", [128, 8], F32)
        self.t8a = sb("t8a", [128, 8], F32)
        self.t8b = sb("t8b", [128, 8], F32)
        self.cs8 = sb("cs8", [128, 8], F32)
        self.tot8 = sb("tot8", [128, 8], F32)
        self.dout8 = sb("dout8", [128, 8], F32)
        self.etot8 = sb("etot8", [128, 8], F32)
        self.R = self.Rarg[:, 0:8, :]
        self.arg = self.Rarg[:, 8:16, :]
        self.ecs = self.R
        self.cbm = sb("cbm", [128, 2, 128], F32)
        self.MT = sb("MT", [128, 8, 128], BF16)
        self.Cs = sb("Cs", [128, 8, 128], BF16)
        self.xdt = sb("xdt", [128, 8, 64], BF16)
        self.xdd = sb("xdd", [128, 8, 64], BF16)
        self.xsd = sb("xsd", [128, 8, 64], F32)
        self.Btm = sb("Btm", [128, 2, 128], BF16)
        self.Sst = sb("Sst", [128, 8, 64], F32)
        self.Sbf = sb("Sbf", [128, 8, 64], BF16)
        self.y1 = self.vg
        self.yctm = sb("yctm", [128, 512], BF16)
        self.ycT = sb("ycT", [128, 4, TS], BF16)
        self.rq2 = sb("rq2", [128, 2], F32)
        self.qtm = sb("qtm", [128, 8, 96], F32)
        self.hss = sb("hss", [128, 8], F32)
        self.r1 = sb("r1", [128, 8, 16], F32)
        self.r2 = sb("r2", [128, 8, 16], F32)
        self.qr = sb("qr", [128, 8, 96], BF16)
        self.qT = sb("qT", [128, 8, TS], BF16)
        self.kst = sb("kst", [128, 8, 128], BF16)
        self.vst = sb("vst", [128, 8, 65], BF16)
        self.kh = [sb("kh%d" % i, [128, self.S], BF16) for i in range(2)]
        self.vh = [sb("vh%d" % i, [128, NCH, 65], BF16) for i in range(1)]
        self.kv_i = 0
        self.pT = [sb("pT%d" % i, [128, TS], BF16) for i in range(4)]
        self.rden = sb("rden", [128, 4], F32)
        self.ybtm = sb("ybtm", [128, NC, 512], BF16)
        self.ybT = sb("ybT", [128, 4, TS], BF16)
        self.sig = [sb("sig%d" % i, [128, TS], F32) for i in range(2)]
        self.mgacc = sb("mgacc", [128, 4, TS], F32)
        self.mgT = sb("mgT", [128, NK, TS], BF16)

    def consts(self):
        P = self.P
        idf, idb, ones_b, ones_f, U_f, U_b, Lm_f = self.ident_f, self.ident_b, self.ones_b, self.ones_f, self.U_f, self.U_b, self.Lm_f
        P.add("pool", lambda e: e.memset(idf[:], 0.0), (), ("ident_f",))
        P.add("pool", lambda e: e.affine_select(out=idf[:], in_=idf[:], pattern=[[-1, 128]], compare_op=ALU.not_equal,
                                                fill=1.0, base=0, channel_multiplier=1), ("ident_f",), ("ident_f",))
        P.add("pool", lambda e: e.tensor_copy(out=idb[:], in_=idf[:]), ("ident_f",), ("ident_b",))
        P.add("pool", lambda e: e.memset(ones_b[:], 1.0), (), ("ones_b",))
        P.add("pool", lambda e: e.memset(ones_f[:], 1.0), (), ("ones_f",))
        P.add("pool", lambda e: e.memset(U_f[:], 1.0), (), ("U_f",))
        P.add("pool", lambda e: e.affine_select(out=U_f[:], in_=U_f[:], pattern=[[1, 128]], compare_op=ALU.is_ge,
                                                fill=0.0, base=0, channel_multiplier=-1), ("U_f",), ("U_f",))
        P.add("pool", lambda e: e.tensor_copy(out=U_b[:], in_=U_f[:]), ("U_f",), ("U_b",))
        P.add("pool", lambda e: e.memset(Lm_f[:], 1.0), (), ("Lm_f",))
        P.add("pool", lambda e: e.affine_select(out=Lm_f[:], in_=Lm_f[:], pattern=[[-1, 128]], compare_op=ALU.is_ge,
                                                fill=0.0, base=0, channel_multiplier=1), ("Lm_f",), ("Lm_f",))
        eps_t, one_t, negpi_t = self.eps_t, self.one_t, self.negpi_t
        P.add("pool", lambda e: e.memset(eps_t[:], EPS), (), ("eps_t",))
        P.add("pool", lambda e: e.memset(one_t[:], 1.0), (), ("one_t",))
        P.add("pool", lambda e: e.memset(negpi_t[:], -PI), (), ("negpi_t",))
        if not self.do_mix:
            return
        invf = self.invf
        for j in range(16):
            P.add("pool", lambda e, j=j: e.memset(invf[:, j:j + 1], float(INV_FREQ[j])), (), ("invf",))
        NCH = self.NCH
        self.dma("sp", self.pos_i[:], self.pos_in.ap().rearrange("(c p) -> p c", p=128), (), ("pos_i",), slow=True)
        self.cp("dve", self.pos_f[:], self.pos_i[:], ("pos_i",), ("pos_f",))
        self.tt("dve", self.ang, self.pos_f[:, :, None].broadcast_to([128, NCH, 16]),
                self.invf[:, None, :].broadcast_to([128, NCH, 16]), ALU.mult, ("pos_f", "invf"), ("ang", "xT1"))
        for dst, dk, shift in ((self.sinT, "sinT", 0.0), (self.cosT, "cosT", 0.5 * PI)):
            self.ts("dve", dst[:], self.ang, shift, 1.0 / (2 * PI), ALU.add, ALU.mult, ("ang", "xT1"), (dk,))
            self.cp("dve", self.angi, dst[:], (dk,), ("angi", "xT1"))
            self.cp("dve", self.angf, self.angi, ("angi", "xT1"), ("angf", "xT1"))
            self.ts("dve", dst[:], self.ang, shift, None, ALU.add, None, ("ang", "xT1"), (dk,))
            self.stt("dve", dst[:], self.angf, -2 * PI, dst[:], ALU.mult, ALU.add, ("angf", dk, "xT1"), (dk,))
            self.ts("dve", self.angf, dst[:], PI, 2 * PI, ALU.is_gt, ALU.mult, (dk,), ("angf", "xT1"))
            self.tt("dve", dst[:], dst[:], self.angf, ALU.subtract, (dk, "angf", "xT1"), (dk,))
            self.ts("dve", self.angf, dst[:], -PI, 2 * PI, ALU.is_lt, ALU.mult, (dk,), ("angf", "xT1"))
            self.tt("dve", dst[:], dst[:], self.angf, ALU.add, (dk, "angf", "xT1"), (dk,))
            self.act(dst[:], dst[:], AF.Sin, (dk,), (dk,))

    def conv_list(self, l):
        lst = []

        def add(out, in_, key):
            lst.append(lambda: self.dma("pool", out, in_, (), (key,)))

        if self.do_ffn:
            for f in (1, 2):
                w_in = self.W["ffn%d_w_in" % f][l].rearrange("(k p) c -> p k c", p=128)
                w_out = self.W["ffn%d_w_out" % f][l].rearrange("(jj j2 p) c -> jj p j2 c", j2=2, p=128)
                swi, swo = self.scr_wi[(l, f)], self.scr_wo[(l, f)]
                for j in range(NJ):
                    for h in range(2):
                        add(swi[j][:, :, h * 128:(h + 1) * 128],
                            w_in[:, :, h * DFF + j * 128: h * DFF + (j + 1) * 128], ("swi", l, f, j, h))
                for mg in range(2):
                    for jj in range(NJJ):
                        add(swo[mg, jj], w_out[jj][:, :, mg * 512:(mg + 1) * 512], ("swo", l, f, mg, jj))
        if self.do_mix:
            w = self.W["w_in"][l].rearrange("(k p) c -> p k c", p=128)
            sin = self.scr_in[l]
            srcs = {0: (0, 512), 1: (512, 1024), 2: (1024, 1536), 4: (1696, 2208), 5: (2208, 2720), 6: (2720, 3232)}
            for i in range(6):
                srcs[7 + i] = (3240 + 512 * i, 3240 + 512 * (i + 1))
            for bi, (a, b_) in srcs.items():
                add(sin[bi], w[:, :, a:b_], ("sin", l, bi))
            add(sin[3][:, :, 0:160], w[:, :, 1536:1696], ("sin", l, 3, 0))
            add(sin[3][:, :, 160:168], w[:, :, 3232:3240], ("sin", l, 3, 1))
            for i in range(3):
                add(self.scr_br[l][i], self.W["w_branch"][l, i].rearrange("(kk p) c -> p kk c", p=128), ("sbr", l, i))
            wo_ = self.W["w_out"][l].rearrange("(k p) c -> p k c", p=128)
            for hf in range(2):
                add(self.scr_o[l][hf], wo_[:, :, hf * 512:(hf + 1) * 512], ("so", l, hf))
            add(self.scr_uq[l][:], self.W["mla_w_uq"][l].rearrange("(i p) c -> p i c", p=128), ("suq", l))
            add(self.scr_ukv[l][:], self.W["mla_w_ukv"][l].rearrange("(i p) c -> p i c", p=128), ("sukv", l))
        return lst

    def load_gains(self, l):
        W = self.W
        for i, n in enumerate(("ffn1_norm", "mix_norm", "ffn2_norm")):
            self.dma("sp", self.gains[:, l % 2, i, :], W[n][l].rearrange("(k p) -> p k", p=128), (), (("gains", l % 2),),
                     slow=True)

    def load_params(self, l):
        W = self.W

        def bc(ap, shape):
            return ap.broadcast_to(shape)

        self.dma("sp", self.gvb[:], bc(W["gm_v_norm"][l:l + 1, :], [128, 512]), (), ("gvb",), slow=True)
        self.dma("sp", self.bsb[:], bc(W["gm_b_s"][l:l + 1], [128, 4, 128]), (), ("bsb",), slow=True)
        self.dma("sp", self.Wsraw[:], W["gm_w_s"][l].rearrange("g t s -> t g s"), (), ("Wsraw",))
        self.dma("sp", self.gqn[:], W["mla_q_norm"][l].rearrange("(i p) -> p i", p=128), (), ("gqn",), slow=True)
        self.dma("sp", self.gkvn[:], W["mla_kv_norm"][l].rearrange("(i p) -> p i", p=128), (), ("gkvn",), slow=True)
        self.dma("sp", self.gqb[:], bc(W["mla_q_gain"][l:l + 1, :], [128, 96]), (), ("gqb",), slow=True)
        self.dma("sp", self.gkb[:], bc(W["mla_k_gain"][l:l + 1, :], [128, 96]), (), ("gkb",), slow=True)
        for k in range(4):
            self.dma("sp", self.cw[:, :, k], W["ssd_conv_w"][l, k].rearrange("(c p) -> p c", p=128), (), ("cw",), slow=True)
        self.dma("sp", self.cb[:], W["ssd_conv_b"][l].rearrange("(c p) -> p c", p=128), (), ("cb",), slow=True)
        self.dma("sp", self.dtb[:], bc(W["ssd_dt_bias"][l:l + 1, :], [128, 8]), (), ("dtb",), slow=True)
        self.dma("sp", self.aneg[:], bc(W["ssd_a_log"][l:l + 1, :], [128, 8]), (), ("aneg",), slow=True)
        self.dma("sp", self.Dsk[:], bc(W["ssd_d"][l:l + 1, :], [128, 8]), (), ("Dsk",), slow=True)
        self.dma("sp", self.gsn[:], bc(W["ssd_norm"][l:l + 1, :], [128, 512]), (), ("gsn",), slow=True)
        self.act(self.aneg[:], self.aneg[:], AF.Exp, ("aneg",), ("aneg",))
        self.ts("dve", self.aneg[:], self.aneg[:], -1.0, None, ALU.mult, None, ("aneg",), ("aneg",))
        self.tt("dve", self.Wsm[:], self.Wsraw[:], self.Lm_f[:, None, :].broadcast_to([128, 4, 128]), ALU.mult,
                ("Wsraw", "Lm_f"), ("Wsm",))
        b = self.bank()
        for g in range(4):
            self.tr(b, self.psb(b)[:, g * 128:(g + 1) * 128], self.Wsm[:, g, :], self.ident_b[:], ("Wsm", "ident_b"))
        self.cp("dve", self.WsT[:], self.psb(b)[:, 0:512].rearrange("p (g t) -> p g t", g=4), (psk(b),), ("WsT",))
        self.P.add("dve", lambda e: e.tensor_reduce(out=self.gmx[:, 0:1], in_=self.gqb[:], axis=AX.X, op=ALU.max,
                                                    apply_absolute_value=True), ("gqb",), ("gmx",))
        self.P.add("dve", lambda e: e.tensor_reduce(out=self.gmx[:, 1:2], in_=self.gkb[:], axis=AX.X, op=ALU.max,
                                                    apply_absolute_value=True), ("gkb",), ("gmx",))
        self.stt("dve", self.negC[:], self.gmx[:, 0:1], -float(np.sqrt(96.0)), self.gmx[:, 1:2], ALU.mult, ALU.mult,
                 ("gmx",), ("negC",))
        self.P.add("dve", lambda e: e.memset(self.Sst[:], 0.0), (), ("Sst",))
        self.P.add("dve", lambda e: e.memset(self.Sbf[:], 0.0), (), ("Sbf",))
        self.P.add("dve", lambda e: e.memset(self.hist[:], 0.0), (), ("hist",))
        if l == 0:
            self.P.add("dve", lambda e: e.memset(self.vst[:, :, 64:65], 1.0), (), ("vst1",))

    def load_x(self, l, I, x, xk):
        if l > 0:
            self.dma("sp", x[:], self.xscr[:, :, I * TS:(I + 1) * TS], (("xscr", I),), (xk,))
            return
        XK = tuple(("actT", j) for j in range(NJ))
        src = self.x_in[I * TS:(I + 1) * TS, :].rearrange("(c p) d -> p c d", p=128)
        self.dma("sp", self.xtm, src, (), XK)
        for k in range(NK):
            b = self.bank()
            for c in range(NC):
                self.tr(b, self.ps[b][:, c * 128:(c + 1) * 128], self.xtm[:, c, k * 128:(k + 1) * 128], self.ident_f[:],
                        XK + ("ident_f",))
            eng = "dve" if k % 2 == 0 else "act"
            self.cp(eng, x[:, k, :], self.pt(b), (psk(b),), (xk,))

    def store_x(self, l, I, x, xk):
        if l < self.L - 1:
            return [self.dma("sp", self.xscr[:, :, I * TS:(I + 1) * TS], x[:], (xk,), (("xscr", I),))]
        XK = tuple(("actT", j) for j in range(NJ))
        for c in range(NC):
            for hlf in range(2):
                b = self.bank()
                for kk in range(4):
                    k = hlf * 4 + kk
                    self.tr(b, self.ps[b][:, kk * 128:(kk + 1) * 128], x[:, k, c * 128:(c + 1) * 128],
                            self.ident_f[:], (xk, "ident_f"))
                eng = "dve" if hlf == 0 else "act"
                self.cp(eng, self.xtm[:, c, hlf * 512:(hlf + 1) * 512], self.ps[b][:], (psk(b),), XK)
        dst = self.out[I * TS:(I + 1) * TS, :].rearrange("(c p) d -> p c d", p=128)
        return [self.dma("sp", dst, self.xtm, XK, ())]

    def rmsnorm_T(self, l, gi, x, xk, ffn):
        hT, hname = (self.hTf, "hTf") if ffn else (self.hT, "hT")
        rstd, rk = (self.rstdf, "rstdf") if ffn else (self.rstd, "rstd")
        b = self.bank()
        for k in range(NK):
            if ffn:
                sq, sqk = self.sqf[self.sqf_i % 2], "sqf%d" % (self.sqf_i % 2)
                self.sqf_i += 1
            else:
                sq, sqk = self.sq[self.sq_i % 2], "sq%d" % (self.sq_i % 2)
                self.sq_i += 1
            self.act(sq[:], x[:, k, :], AF.Square, (xk,), (sqk,))
            self.mm(b, self.pt(b), self.ones_b[:], sq[:], k == 0, k == NK - 1, (sqk, "ones_b"))
        self.rsqrt(rstd[:], self.pt(b), 1.0 / D, (psk(b),), (rk,))
        for k in range(NK):
            self.stt("dve", hT[:, k, :], x[:, k, :], self.gains[:, l % 2, gi, k:k + 1], rstd[:], ALU.mult, ALU.mult,
                     (xk, ("gains", l % 2), rk), ((hname, k),))

    def ffn(self, l, f, x, xk):
        gi = 0 if f == 1 else 2
        self.rmsnorm_T(l, gi, x, xk, True)
        yield
        swi, swo = self.scr_wi[(l, f)], self.scr_wo[(l, f)]
        for j in range(NJ):
            wi = self.wif[self.wif_i % 4]
            wik = "wif%d" % (self.wif_i % 4)
            self.wif_i += 1
            self.dma(FQ, wi[:], swi[j], (("swi", l, f, j, 0), ("swi", l, f, j, 1)), (wik,))
            bg, bu = self.bank(), self.bank()
            for (b, off) in ((bg, 0), (bu, 128)):
                for k in range(NK):
                    self.mm(b, self.pt(b), wi[:, k, off:off + 128], self.hTf[:, k, :],
                            k == 0, k == NK - 1, (wik, ("hTf", k)))
            sg = self.sg[self.sg_i % 2]
            sgk = "sg%d" % (self.sg_i % 2)
            self.sg_i += 1
            self.act(sg[:], self.pt(bg), AF.Silu, (psk(bg),), (sgk,))
            self.tt("dve", self.actT[:, j, :], sg[:], self.pt(bu), ALU.mult, (sgk, psk(bu)), (("actT", j),))
            yield
        for mg in range(2):
            banks = [self.bank() for _ in range(2)]
            self.reserved.update(banks)
            for jj in range(NJJ):
                wo = self.wo[self.wo_i % 4]
                wok = "wo%d" % (self.wo_i % 4)
                self.wo_i += 1
                self.dma(FQ, wo[:], swo[mg, jj], (("swo", l, f, mg, jj),), (wok,))
                for j2 in range(2):
                    j = jj * 2 + j2
                    for m in range(4):
                        b = banks[m // 2]
                        o = self.ps[b][:, (m % 2) * TS:(m % 2 + 1) * TS]
                        self.mm(b, o, wo[:, j2, m * 128:(m + 1) * 128], self.actT[:, j, :],
                                j == 0 and m % 2 == 0, j == NJ - 1, (wok, ("actT", j)), skip=True)
                yield
            for m in range(4):
                b = banks[m // 2]
                k = mg * 4 + m
                o = self.ps[b][:, (m % 2) * TS:(m % 2 + 1) * TS]
                self.stt("dve", x[:, k, :], o, 0.5, x[:, k, :], ALU.mult, ALU.add, (psk(b), xk), (xk,))
            self.reserved.difference_update(banks)
            yield
    def plan_loads(self, l):
        sin_ = self.scr_in[l]
        L_ = []
        for bi in (0, 1, 2):
            L_.append(("blk", sin_[bi], (("sin", l, bi),), 512, NK))
        L_.append(("blk", sin_[3][:, :, 0:168], (("sin", l, 3, 0), ("sin", l, 3, 1)), 168, NK))
        for bi in (4, 5, 6):
            L_.append(("blk", sin_[bi], (("sin", l, bi),), 512, NK))
        L_.append(("flat", self.scr_uq[l][:], (("suq", l),), 3, 768))
        L_.append(("flat", self.scr_ukv[l][:], (("sukv", l),), 2, 1024))
        for half in range(2):
            for i in range(3):
                L_.append(("blk", sin_[7 + 2 * i + half], (("sin", l, 7 + 2 * i + half),), 512, NK))
                L_.append(("blk", self.scr_br[l][i][:, :, half * 512:(half + 1) * 512], (("sbr", l, i),), 512, 4))
        for half in range(2):
            L_.append(("blk", self.scr_o[l][half], (("so", l, half),), 512, NK))
        self.loads = L_
        self.ld_issued = []
        self.ld_next = 0
        self.PREF = getattr(self, "PREF", 0)

    def _issue(self):
        kind, src, key, a, b_ = self.loads[len(self.ld_issued)]
        wi = self.wi[self.wi_i % NWI]
        wik = "wi%d" % (self.wi_i % NWI)
        self.wi_i += 1
        if kind == "blk":
            v = wi
            self.dma("sp", wi[:, 0:b_, 0:a], src, key, (wik,))
        else:
            v = wi[:].rearrange("p k c -> p (k c)")[:, 0:a * b_].rearrange("p (i c) -> p i c", i=a)
            self.dma("sp", v, src, key, (wik,))
        self.ld_issued.append((v, wik))

    def get_blk(self):
        i = self.ld_next
        self.ld_next += 1
        while len(self.ld_issued) < min(i + 1 + self.PREF, len(self.loads)):
            self._issue()
        return self.ld_issued[i]

    def load_blk(self, src, key, ncols=512, nk=NK):
        return self.get_blk()

    def load_flat(self, src, key, n_i, n_c):
        return self.get_blk()

    def fm_chunk(self, wi, wik, col0, nk=NK, rhs=None, rkey=None):
        b = self.bank()
        for k in range(nk):
            r = self.hT[:, k, :] if rhs is None else rhs[:, k, :]
            rk = ("hT", k) if rhs is None else rkey
            self.mm(b, self.pt(b), wi[:, k, col0:col0 + 128], r, k == 0, k == nk - 1, (wik, rk))
        return b

    def tm_chunk(self, wi, wik, c, col0, ncols):
        b = self.bank()
        for k in range(NK):
            self.mm(b, self.ps[b][:, 0:ncols], self.hT[:, k, c * 128:(c + 1) * 128], wi[:, k, col0:col0 + ncols],
                    k == 0, k == NK - 1, (wik, ("hT", k)))
        return b

    def head_norm_rope(self, cg, gb, gbk, dst_T, dst_key, col0):
        q3 = self.qtm[:]
        self.tt("dve", self.junk[:].rearrange("p (h d) -> p h d", h=8), q3, q3, ALU.mult, ("qtm",), ("junk",))
        self.red(self.hss[:], self.junk[:].rearrange("p (h d) -> p h d", h=8), ("junk",), ("hss",))
        self.rsqrt(self.hss[:], self.hss[:], 1.0 / 96, ("hss",), ("hss",))
        self.tt("dve", q3, q3, self.hss[:, :, None].broadcast_to([128, 8, 96]), ALU.mult, ("qtm", "hss"), ("qtm",))
        self.tt("dve", q3, q3, gb[:, None, :].broadcast_to([128, 8, 96]), ALU.mult, ("qtm", gbk), ("qtm",))
        cos = self.cosT[:, cg, :][:, None, :].broadcast_to([128, 8, 16])
        sin = self.sinT[:, cg, :][:, None, :].broadcast_to([128, 8, 16])
        x1, x2 = self.qtm[:, :, 64:80], self.qtm[:, :, 80:96]
        self.cp("act", self.qr[:, :, 0:64], self.qtm[:, :, 0:64], ("qtm",), ("qr",))
        self.tt("dve", self.r1[:], x1, cos, ALU.mult, ("qtm", "cosT"), ("r1",))
        self.tt("dve", self.r2[:], x2, sin, ALU.mult, ("qtm", "sinT"), ("r2",))
        self.tt("dve", self.qr[:, :, 64:80], self.r1[:], self.r2[:], ALU.subtract, ("r1", "r2"), ("qr",))
        self.tt("dve", self.r1[:], x2, cos, ALU.mult, ("qtm", "cosT"), ("r1",))
        self.tt("dve", self.r2[:], x1, sin, ALU.mult, ("qtm", "sinT"), ("r2",))
        self.tt("dve", self.qr[:, :, 80:96], self.r1[:], self.r2[:], ALU.add, ("r1", "r2"), ("qr",))
        yield
        for g in range(2):
            b = self.bank()
            pb = self.psb(b)
            for hh in range(4):
                self.tr(b, pb[0:96, hh * 128:(hh + 1) * 128], self.qr[:, g * 4 + hh, :], self.ident_b[:], ("qr", "ident_b"))
            self.cp("act" if g else "dve", dst_T[0:96, g * 4:g * 4 + 4, col0:col0 + 128],
                    pb[0:96, 0:512].rearrange("p (h t) -> p h t", h=4), (psk(b),), (dst_key,))

    def mixer(self, l, I, x, xk):
        sin_ = self.scr_in[l]
        self.plan_loads(l)
        self.rmsnorm_T(l, 1, x, xk, False)
        yield
        wi, wik = self.load_blk(sin_[0], (("sin", l, 0),))
        for ch in range(4):
            b = self.fm_chunk(wi, wik, ch * 128)
            self.act(self.uT[:, ch, :], self.pt(b), AF.Gelu, (psk(b),), ("uT",))
        yield
        wi, wik = self.load_blk(sin_[1], (("sin", l, 1),))
        for c in range(NC):
            b = self.tm_chunk(wi, wik, c, 0, 512)
            self.act(self.vg[:], self.ps[b][:], AF.Gelu, (psk(b),), ("vg",))
            self.act(self.junk[:, 0:512], self.vg[:], AF.Square, ("vg",), ("junk",))
            self.red(self.s1[:, 0:1], self.junk[:, 0:512], ("junk",), ("s1",))
            self.rsqrt(self.s1[:, 0:1], self.s1[:, 0:1], 1.0 / 512, ("s1",), ("s1",))
            self.stt("dve", self.vtm[:, c, :], self.vg[:], self.s1[:, 0:1], self.gvb[:], ALU.mult, ALU.mult,
                     ("vg", "s1", "gvb"), (("vtm", c),))
        yield
        wi2, wi2k = self.load_blk(sin_[2], (("sin", l, 2),))
        wi3, wi3k = self.load_blk(sin_[3][:, :, 0:168], (("sin", l, 3, 0), ("sin", l, 3, 1)), ncols=168)
        for i in range(5):
            if i < 4:
                b = self.fm_chunk(wi2, wi2k, i * 128)
            else:
                b = self.fm_chunk(wi3, wi3k, 0)
            self.act(self.sqq[:, i, :], self.pt(b), AF.Square, (psk(b),), ("sqq",))
            if i < 3:
                self.ts("dve", self.cqnT[:, i, :], self.pt(b), self.gqn[:, i:i + 1], None, ALU.mult, None,
                        (psk(b), "gqn"), ("cqnT",))
            else:
                self.ts("dve", self.ckvnT[:, i - 3, :], self.pt(b), self.gkvn[:, i - 3:i - 2], None, ALU.mult, None,
                        (psk(b), "gkvn"), ("ckvnT",))
        for c in range(NC):
            b = self.tm_chunk(wi3, wi3k, c, 128, 40)
            self.cp("dve", self.krdt[:, c, :], self.ps[b][:, 0:40], (psk(b),), ("krdt",))
        yield
        wi, wik = self.load_blk(sin_[4], (("sin", l, 4),))
        for c in range(NC):
            b = self.tm_chunk(wi, wik, c, 0, 512)
            self.act(self.zs[:, c, :], self.ps[b][:], AF.Silu, (psk(b),), ("zs",))
        for blk in range(2):
            yield
            wi, wik = self.load_blk(sin_[5 + blk], (("sin", l, 5 + blk),))
            for cc in range(4):
                ch = blk * 4 + cc
                b = self.fm_chunk(wi, wik, cc * 128)
                xp = self.xbcp[ch % 2]
                xk = "xbcp%d" % (ch % 2)
                self.cp("act", xp[:, 3:TS + 3], self.pt(b), (psk(b),), (xk,))
                self.cp("dve", xp[:, 0:3], self.hist[:, ch, :], ("hist",), (xk,))
                self.ts("dve", self.cacc, xp[:, 3:TS + 3], self.cw[:, ch, 3:4], self.cb[:, ch:ch + 1], ALU.mult, ALU.add,
                        (xk, "cw", "cb"), ("vg",))
                for t in range(3):
                    self.stt("dve", self.cacc, xp[:, t:TS + t], self.cw[:, ch, t:t + 1], self.cacc, ALU.mult, ALU.add,
                             (xk, "cw", "vg"), ("vg",))
                self.cp("dve", self.hist[:, ch, :], xp[:, TS:TS + 3], (xk,), ("hist",))
                self.act(self.xbcT[:, ch, :], self.cacc, AF.Silu, ("vg",), ("xbcT",))
                yield
        wq = self.load_flat(self.scr_uq[l][:], (("suq", l),), 3, 768)
        wkv = self.load_flat(self.scr_ukv[l][:], (("sukv", l),), 2, 1024)
        for c in range(NC):
            yield from self.mla_chunk(c, I * NC + c, slice(c * 128, (c + 1) * 128), wq, wkv)
            yield

        def side():
            for c in range(NC):
                cg = I * NC + c
                ck = slice(c * 128, (c + 1) * 128)
                b = self.bank()
                for g in range(4):
                    self.mm(b, self.ps[b][:, g * 128:(g + 1) * 128], self.vtm[:, c, g * 128:(g + 1) * 128], self.WsT[:, g, :],
                            True, True, (("vtm", c), "WsT"))
                self.tt("dve", self.spb[:], self.ps[b][:].rearrange("p (g t) -> p g t", g=4), self.bsb[:], ALU.add,
                        (psk(b), "bsb"), ("spb",))
                self.tt("dve", self.yaT[:, :, ck], self.spb[:], self.uT[:, :, ck], ALU.mult, ("spb", "uT"), ("yaT",))
                yield from self.ssd_chunk(c, cg, ck)

        att = self.attention(I)
        sd = side()
        n_att = 8 * (NC * I + NC) + 8
        n_side = NC * 12
        acc = 0.0
        att_done = side_done = False
        while not (att_done and side_done):
            if not att_done:
                try:
                    next(att)
                except StopIteration:
                    att_done = True
            yield
            acc += n_side / n_att
            while (acc >= 1.0 or att_done) and not side_done:
                acc -= 1.0
                try:
                    next(sd)
                except StopIteration:
                    side_done = True
        yield from self.merge(l, x, xk)

    def ssd_chunk(self, c, cg, ck):
        xs = self.xbcT
        self.tt("dve", self.t8a[:], self.krdt[:, c, 32:40], self.dtb[:], ALU.add, ("krdt", "dtb"), ("t8a",))
        self.ts("dve", self.t8b[:], self.t8a[:], -1.0, None, ALU.mult, None, ("t8a",), ("t8b",))
        self.tt("dve", self.t8b[:], self.t8b[:], self.t8a[:], ALU.max, ("t8a", "t8b"), ("t8b",))
        self.act(self.t8b[:], self.t8b[:], AF.Exp, ("t8b",), ("t8b",), scale=-1.0)
        self.act(self.t8b[:], self.t8b[:], AF.Ln, ("t8b", "one_t"), ("t8b",), bias=self.one_t[:])
        self.stt("dve", self.dt8[:], self.t8a[:], 0.0, self.t8b[:], ALU.max, ALU.add, ("t8a", "t8b"), ("dt8",))
        self.tt("dve", self.da8[:], self.dt8[:], self.aneg[:], ALU.mult, ("dt8", "aneg"), ("da8",))
        yield
        b = self.bank()
        self.mm(b, self.ps[b][:, 0:8], self.U_f[:], self.da8[:], True, True, ("U_f", "da8"))
        self.cp("dve", self.cs8[:], self.ps[b][:, 0:8], (psk(b),), ("cs8",))
        self.tt("dve", self.R[:], self.U_f[:, None, :].broadcast_to([128, 8, 128]),
                self.da8[:, :, None].broadcast_to([128, 8, 128]), ALU.mult, ("U_f", "da8"), ("R",))
        yield
        bb = [self.bank(), self.bank()]
        for hh in range(2):
            self.mm(bb[hh], self.ps[bb[hh]][:], self.ones_f[:], self.R[:, hh * 4:hh * 4 + 4, :].rearrange("p h l -> p (h l)"),
                    True, True, ("ones_f", "R"))
        for hh in range(2):
            p3 = self.ps[bb[hh]][:].rearrange("p (h l) -> p h l", h=4)
            hs = slice(hh * 4, hh * 4 + 4)
            self.tt("dve", self.arg[:, hs, :], p3, self.cs8[:, hs, None].broadcast_to([128, 4, 128]), ALU.subtract,
                    (psk(bb[hh]), "cs8"), ("arg",))
            self.act(self.ecs[:, hs, :], p3, AF.Exp, (psk(bb[hh]),), ("R",))
            self.cp("dve", self.tot8[:, hs], p3[:, :, 127], (psk(bb[hh]),), ("tot8",))
        self.act(self.arg[:], self.arg[:], AF.Relu, ("arg",), ("arg",), scale=-1.0)
        self.act(self.arg[:], self.arg[:], AF.Exp, ("arg",), ("arg",), scale=-1.0)
        self.tt("dve", self.t8a[:], self.tot8[:], self.cs8[:], ALU.subtract, ("tot8", "cs8"), ("t8a",))
        self.act(self.dout8[:], self.t8a[:], AF.Exp, ("t8a",), ("dout8",))
        self.act(self.etot8[:], self.tot8[:], AF.Exp, ("tot8",), ("etot8",))
        yield
        b = self.bank()
        for g in range(2):
            self.mm(b, self.ps[b][:, g * 128:(g + 1) * 128], xs[:, 4 + g, ck], xs[:, 6 + g, ck], True, True, ("xbcT",))
        self.tt("dve", self.cbm[:], self.ps[b][:, 0:256].rearrange("p (g l) -> p g l", g=2),
                self.U_f[:, None, :].broadcast_to([128, 2, 128]), ALU.mult, (psk(b), "U_f"), ("cbm",))
        self.tt("dve", self.MT[:].rearrange("p (g r) l -> p g r l", g=2), self.arg[:].rearrange("p (g r) l -> p g r l", g=2),
                self.cbm[:, :, None, :].broadcast_to([128, 2, 4, 128]), ALU.mult, ("arg", "cbm"), ("MT",))
        self.tt("dve", self.Cs[:].rearrange("p (g r) l -> p g r l", g=2), self.ecs[:].rearrange("p (g r) l -> p g r l", g=2),
                xs[:, 6:8, ck][:, :, None, :].broadcast_to([128, 2, 4, 128]), ALU.mult, ("R", "xbcT"), ("Cs",))
        yield
        b = self.bank()
        pb = self.psb(b)
        for i in range(4):
            self.tr(b, pb[:, i * 128:(i + 1) * 128], xs[:, i, ck], self.ident_b[:], ("xbcT", "ident_b"))
        x3 = pb[:, 0:512].rearrange("p (h d) -> p h d", h=8)
        self.tt("dve", self.xdt[:], x3, self.dt8[:, :, None].broadcast_to([128, 8, 64]), ALU.mult, (psk(b), "dt8"), ("xdt",))
        self.tt("dve", self.xsd[:], x3, self.Dsk[:, :, None].broadcast_to([128, 8, 64]), ALU.mult, (psk(b), "Dsk"), ("xsd",))
        self.tt("dve", self.xdd[:], self.xdt[:], self.dout8[:, :, None].broadcast_to([128, 8, 64]), ALU.mult,
                ("xdt", "dout8"), ("xdd",))
        yield
        b = self.bank()
        pb = self.psb(b)
        for g in range(2):
            self.tr(b, pb[:, g * 128:(g + 1) * 128], xs[:, 4 + g, ck], self.ident_b[:], ("xbcT", "ident_b"))
        self.cp("act", self.Btm[:], pb[:, 0:256].rearrange("p (g n) -> p g n", g=2), (psk(b),), ("Btm",))
        yield
        b = self.bank()
        for h in range(8):
            o = self.ps[b][:, h * 64:(h + 1) * 64]
            self.mm(b, o, self.MT[:, h, :], self.xdt[:, h, :], True, False, ("MT", "xdt"))
            self.mm(b, o, self.Cs[:, h, :], self.Sbf[:, h, :], False, True, ("Cs", "Sbf"))
        self.tt("dve", self.y1[:], self.ps[b][:], self.xsd[:].rearrange("p h d -> p (h d)"), ALU.add, (psk(b), "xsd"), ("vg",))
        self.tt("dve", self.y1[:], self.y1[:], self.zs[:, c, :], ALU.mult, ("vg", "zs"), ("vg",))
        self.act(self.junk[:, 0:512], self.y1[:], AF.Square, ("vg",), ("junk",))
        self.red(self.s2[:, 0:2], self.junk[:, 0:512].rearrange("p (g d) -> p g d", g=2), ("junk",), ("s2",))
        self.rsqrt(self.s2[:, 0:2], self.s2[:, 0:2], 1.0 / 256, ("s2",), ("s2",))
        for g in range(2):
            gs = slice(g * 256, (g + 1) * 256)
            self.stt("dve", self.yctm[:, gs], self.y1[:, gs], self.s2[:, g:g + 1], self.gsn[:, gs], ALU.mult, ALU.mult,
                     ("vg", "s2", "gsn"), ("yctm",))
        yield
        b = self.bank()
        pb = self.psb(b)
        for i in range(4):
            self.tr(b, pb[:, i * 128:(i + 1) * 128], self.yctm[:, i * 128:(i + 1) * 128], self.ident_b[:], ("yctm", "ident_b"))
        self.cp("act", self.ycT[:, :, ck], pb[:, 0:512].rearrange("p (i t) -> p i t", i=4), (psk(b),), ("ycT",))
        yield
        b = self.bank()
        for h in range(8):
            self.mm(b, self.ps[b][:, h * 64:(h + 1) * 64], self.Btm[:, h // 4, :], self.xdd[:, h, :], True, True, ("Btm", "xdd"))
        self.tt("dve", self.Sst[:], self.Sst[:], self.etot8[:, :, None].broadcast_to([128, 8, 64]), ALU.mult,
                ("Sst", "etot8"), ("Sst",))
        self.tt("dve", self.Sst[:], self.Sst[:], self.ps[b][:].rearrange("p (h d) -> p h d", h=8), ALU.add,
                ("Sst", psk(b)), ("Sst",))
        self.cp("dve", self.Sbf[:], self.Sst[:], ("Sst",), ("Sbf",))
        yield

    def mla_chunk(self, c, cg, ck, wq, wkv):
        Wuq, Wuqk = wq
        Wukv, Wukvk = wkv
        b = self.bank()
        for i in range(3):
            self.mm(b, self.ps[b][:, 0:1], self.sqq[:, i, ck], self.ones_b[:, 0:1], i == 0, i == 2, ("sqq", "ones_b"))
        for i in range(2):
            self.mm(b, self.ps[b][:, 1:2], self.sqq[:, 3 + i, ck], self.ones_b[:, 0:1], i == 0, i == 1, ("sqq", "ones_b"))
        self.rsqrt(self.rq2[:, 0:1], self.ps[b][:, 0:1], 1.0 / 384, (psk(b),), ("rq2",))
        self.rsqrt(self.rq2[:, 1:2], self.ps[b][:, 1:2], 1.0 / 256, (psk(b),), ("rq2",))
        for half in range(2):
            b = self.bank()
            for i in range(3):
                self.mm(b, self.ps[b][:, 0:384], self.cqnT[:, i, ck], Wuq[:, i, half * 384:(half + 1) * 384], i == 0,
                        i == 2, ("cqnT", Wuqk))
            self.ts("dve", self.qtm[:, half * 4:half * 4 + 4, :], self.ps[b][:, 0:384].rearrange("p (h d) -> p h d", h=4),
                    self.rq2[:, 0:1], None, ALU.mult, None, (psk(b), "rq2"), ("qtm",))
        yield from self.head_norm_rope(cg, self.gqb, "gqb", self.qT, "qT", c * 128)
        yield
        for half in range(2):
            b = self.bank()
            for i in range(2):
                self.mm(b, self.ps[b][:], self.ckvnT[:, i, ck], Wukv[:, i, half * 512:(half + 1) * 512], i == 0, i == 1,
                        ("ckvnT", Wukvk))
            p3 = self.ps[b][:].rearrange("p (h d) -> p h d", h=4)
            hs = slice(half * 4, half * 4 + 4)
            self.ts("dve", self.vst[:, hs, 0:64], p3[:, :, 64:128], self.rq2[:, 1:2], None, ALU.mult, None,
                    (psk(b), "rq2"), ("vst",))
            self.ts("dve", self.qtm[:, hs, 0:64], p3[:, :, 0:64], self.rq2[:, 1:2], None, ALU.mult, None,
                    (psk(b), "rq2"), ("qtm",))
        self.cp("dve", self.qtm[:, :, 64:96], self.krdt[:, c, 0:32][:, None, :].broadcast_to([128, 8, 32]), ("krdt",), ("qtm",))
        self.dma("sp", self.vscr[:, :, cg, :].rearrange("h p d -> p h d"), self.vst[:], ("vst", "vst1"), (("vscr", cg),))
        yield from self.head_norm_rope(cg, self.gkb, "gkb", self.kst, "kst", 0)
        self.dma("sp", self.kscr[:, :, cg * 128:(cg + 1) * 128].rearrange("h d t -> d h t"), self.kst[0:96, :, :], ("kst",),
                 (("kscr", cg),))

    def attention(self, I):
        scale = 96.0 ** -0.5
        nj = NC * I + NC
        st = {}

        def prep(h):
            kh = self.kh[self.kv_i % 2]
            kk = "kh%d" % (self.kv_i % 2)
            self.kv_i += 1
            vh, vk = self.vh[0], "vh0"
            skeys = tuple(("kscr", j) for j in range(nj))
            vkeys = tuple(("vscr", j) for j in range(nj))
            self.dma("sp", kh[0:96, 0:nj * 128], self.kscr[h, :, 0:nj * 128], skeys, (kk,))
            st[h] = [kh, kk, vh, vk, None]
            return vkeys

        def load_v(h, vkeys):
            self.dma("sp", st[h][2][:, 0:nj, :], self.vscr[h, :, 0:nj, :], vkeys, (st[h][3],))

        def A(h, j):
            kh, kk = st[h][0], st[h][1]
            c0 = max(0, j - NC * I)
            bs = self.bank()
            self.mm(bs, self.ps[bs][:, c0 * 128:TS], kh[0:96, j * 128:(j + 1) * 128], self.qT[0:96, h, c0 * 128:TS],
                    True, True, (kk, "qT"))
            pT = self.pT[self.pt_i % 4]
            pk = "pT%d" % (self.pt_i % 4)
            self.pt_i += 1
            self.act(pT[:, c0 * 128:TS], self.ps[bs][:, c0 * 128:TS], AF.Exp, (psk(bs), "negC"), (pk,),
                     bias=self.negC[:], scale=scale)
            if j >= NC * I:
                self.tt("dve", pT[:, c0 * 128:(c0 + 1) * 128], pT[:, c0 * 128:(c0 + 1) * 128], self.U_b[:], ALU.mult,
                        (pk, "U_b"), (pk,))
            return pT, pk, c0

        def B(h, j, pT, pk, c0):
            vh, vk = st[h][2], st[h][3]
            if j == 0:
                st[h][4] = self.bank()
                self.reserved.add(st[h][4])
            bacc = st[h][4]
            for c in range(c0, NC):
                self.mm(bacc, self.ps[bacc][:, c * 128:c * 128 + 65], pT[:, c * 128:(c + 1) * 128], vh[:, j, :],
                        (j == 0 and c == 0), (j == NC * I + c), (pk, vk), skip=True)
            if j == nj - 1:
                a3 = self.ps[bacc][:].rearrange("p (c d) -> p c d", c=4)
                self.P.add("dve", lambda e, a3=a3: e.reciprocal(out=self.rden[:, 0:NC], in_=a3[:, 0:NC, 64]), (psk(bacc),),
                           ("rden",))
                self.tt("dve", self.ybtm[:, :, h * 64:(h + 1) * 64], a3[:, 0:NC, 0:64],
                        self.rden[:, 0:NC, None].broadcast_to([128, NC, 64]), ALU.mult, (psk(bacc), "rden"), ("ybtm",))
                self.reserved.discard(bacc)

        LA = 3
        pend = []
        vload = {}
        for h in range(8):
            vkeys = prep(h)
            for j in range(nj):
                cur = A(h, j)
                pend.append((h, j) + cur)
                if j == 0:
                    vload[h] = vkeys
                while len(pend) > LA:
                    p = pend.pop(0)
                    if p[1] == 0:
                        load_v(p[0], vload.pop(p[0]))
                    B(*p)
                yield
        while pend:
            p = pend.pop(0)
            if p[1] == 0:
                load_v(p[0], vload.pop(p[0]))
            B(*p)
        yield
        for c in range(NC):
            b = self.bank()
            pb = self.psb(b)
            for i in range(4):
                self.tr(b, pb[:, i * 128:(i + 1) * 128], self.ybtm[:, c, i * 128:(i + 1) * 128], self.ident_b[:],
                        ("ybtm", "ident_b"))
            self.cp("act", self.ybT[:, :, c * 128:(c + 1) * 128], pb[:, 0:512].rearrange("p (i t) -> p i t", i=4),
                    (psk(b),), ("ybT",))

    def merge(self, l, x, xk):
        ys = ((self.yaT, "yaT"), (self.ybT, "ybT"), (self.ycT, "ycT"))
        for i in range(3):
            if not (self.brmask >> i) & 1:
                self.P.add("dve", lambda e, t=ys[i][0]: e.memset(t[:], 0.0), (), (ys[i][1],))
        for half in range(2):
            for i in range(3):
                wg, wgk = self.load_blk(self.scr_in[l][7 + 2 * i + half], (("sin", l, 7 + 2 * i + half),))
                wb, wbk = self.load_blk(self.scr_br[l][i][:, :, half * 512:(half + 1) * 512], (("sbr", l, i),), nk=4)
                for m in range(4):
                    bg = self.fm_chunk(wg, wgk, m * 128)
                    sg = self.sig[self.sig_i % 2]
                    sgk = "sig%d" % (self.sig_i % 2)
                    self.sig_i += 1
                    self.act(sg[:], self.pt(bg), AF.Sigmoid, (psk(bg),), (sgk,))
                    bb = self.fm_chunk(wb, wbk, m * 128, nk=4, rhs=ys[i][0], rkey=ys[i][1])
                    if i == 0:
                        self.tt("dve", self.mgacc[:, m, :], sg[:], self.pt(bb), ALU.mult, (sgk, psk(bb)), ("mgacc",))
                    else:
                        self.tt("dve", sg[:], sg[:], self.pt(bb), ALU.mult, (sgk, psk(bb)), (sgk,))
                        if i == 1:
                            self.tt("dve", self.mgacc[:, m, :], self.mgacc[:, m, :], sg[:], ALU.add, (sgk, "mgacc"), ("mgacc",))
                        else:
                            self.tt("dve", self.mgT[:, half * 4 + m, :], self.mgacc[:, m, :], sg[:], ALU.add, (sgk, "mgacc"),
                                    ("mgT",))
                    yield
        for half in range(2):
            wo, wok = self.load_blk(self.scr_o[l][half], (("so", l, half),))
            for m in range(4):
                b = self.fm_chunk(wo, wok, m * 128, rhs=self.mgT, rkey="mgT")
                k = half * 4 + m
                self.tt("dve", x[:, k, :], x[:, k, :], self.pt(b), ALU.add, (xk, psk(b)), (xk,))
            yield

    def build(self):
        assert NC == 2
        self.FW = getattr(self, "FW", 1.0)
        self.declare()
        self.alloc()
        self.consts()
        finals = []
        L, NT = self.L, self.NT
        convs = [self.conv_list(l) for l in range(L)]
        for fn in convs[0]:
            fn()
        slots = [(l, I) for l in range(L) for I in range(NT)]
        nslot = len(slots)

        def xt(s):
            return self.xTs[s % 2], "xT%d" % (s % 2)

        def fstream(s):
            if s - 1 >= 0:
                l, I = slots[s - 1]
                x, xk = xt(s - 1)
                if self.do_ffn:
                    yield from self.ffn(l, 2, x, xk)
                finals.extend(self.store_x(l, I, x, xk))
                yield
            if s + 1 < nslot:
                l, I = slots[s + 1]
                x, xk = xt(s + 1)
                if I == 0:
                    self.load_gains(l)
                self.load_x(l, I, x, xk)
                yield
                if self.do_ffn:
                    yield from self.ffn(l, 1, x, xk)

        def run(*gens, weights=None):
            gens = [g for g in gens if g is not None]
            alive = [True] * len(gens)
            acc = [0.0] * len(gens)
            w = weights or [1.0] * len(gens)
            while any(alive):
                for i, g in enumerate(gens):
                    if not alive[i]:
                        continue
                    acc[i] += w[i]
                    while acc[i] >= 1.0 and alive[i]:
                        acc[i] -= 1.0
                        try:
                            k = next(g)
                            if k:
                                acc[i] -= float(k)
                        except StopIteration:
                            alive[i] = False

        self.load_gains(0)
        self.load_x(0, 0, *xt(0))
        if self.do_ffn:
            run(self.ffn(0, 1, *xt(0)))
        for s, (l, I) in enumerate(slots):
            if I == 0 and self.do_mix:
                self.load_params(l)
            nxt = convs[l + 1] if l + 1 < L else []
            per = (len(nxt) + NT - 1) // NT if nxt else 0
            for fn in nxt[I * per:(I + 1) * per]:
                fn()
            x, xk = xt(s)
            n_f = 2 * (1 + NJ + 2 * NJJ + 2) + 2
            if self.do_mix:
                n_m = 24 + 2 * (8 * (NC * I + NC) + 8) + 30
                run(self.mixer(l, I, x, xk), fstream(s), weights=[1.0, min(1.0, self.FW * n_f / n_m)])
            else:
                run(fstream(s))
        run(fstream(nslot))
        self.P.emit(finals)
        self.es.close()
        return self.nc


_CACHE = {}


def kernel(**inputs):
    x = np.asarray(inputs["x"])
    B, S, _ = x.shape
    L = int(np.asarray(inputs["ffn1_norm"]).shape[0])
    key = (S, L)
    if key not in _CACHE:
        _CACHE[key] = Builder(S, L).build()
    nc = _CACHE[key]
    shared = {k: np.ascontiguousarray(np.asarray(v)) for k, v in inputs.items() if k not in ("x", "positions")}
    pos = np.asarray(inputs["positions"]).astype(np.int32)
    in_maps = []
    for b in range(B):
        m = dict(shared)
        m["x"] = np.ascontiguousarray(x[b])
        m["positions"] = np.ascontiguousarray(pos[b])
        in_maps.append(m)
    res = run_bass_kernel_spmd(nc, in_maps, core_ids=list(range(B)))
    return np.stack([np.asarray(r["out"]) for r in res.results], axis=0).astype(np.float32)
```

```python
import contextlib
import numpy as np
import concourse.bass as bass
import concourse.mybir as mybir
from concourse.bass_utils import run_bass_kernel_spmd

F32 = mybir.dt.float32
BF16 = mybir.dt.bfloat16
I32 = mybir.dt.int32
AF = mybir.ActivationFunctionType
ALU = mybir.AluOpType
AX = mybir.AxisListType

D = 1024
DFF = 2816
EPS = 1e-6
TS = 256
NC = TS // 128
NK = 8
NJ = DFF // 128
NJJ = NJ // 2
IN_COLS = 6312

ENGS = ("pe", "act", "dve", "pool", "sp")
NDMA_SEM = 12


class Op:
    __slots__ = ("eng", "fn", "dma", "idx", "deps", "signal", "sigval", "dma_i")

    def __init__(self, eng, fn, dma):
        self.eng = eng
        self.fn = fn
        self.dma = dma
        self.deps = None
        self.signal = False
        self.sigval = 0
        self.dma_i = -1


class Prog:
    def __init__(self, nc):
        self.nc = nc
        self.ops = {e: [] for e in ENGS}
        self.last_w = {}
        self.readers = {}
        self.dma_count = {e: 0 for e in ENGS}
        self.dma_ops = {e: [] for e in ENGS}

    @staticmethod
    def _stream(op):
        return op.eng + ":dma" if op.dma else op.eng

    def add(self, eng, fn, reads=(), writes=(), dma=False):
        op = Op(eng, fn, dma)
        op.idx = len(self.ops[eng])
        deps = {}
        if eng != "pe" and not dma:
            pk = tuple(k for k in reads if isinstance(k, str) and k[:2] == "ps" and k[2:].isdigit())
            if pk:
                writes = tuple(writes) + pk

        def dep(o, raw):
            if o is None:
                return
            if (not o.dma) and o.eng == eng and not dma:
                if eng == "pe" or not raw:
                    return
            s = self._stream(o)
            cur = deps.get(s)
            if cur is None or cur.idx < o.idx:
                deps[s] = o

        for k in reads:
            dep(self.last_w.get(k), True)
        for k in writes:
            dep(self.last_w.get(k), False)
            rd = self.readers.get(k)
            if rd:
                for o in rd.values():
                    dep(o, False)
        if dma:
            i = self.dma_count[eng]
            op.dma_i = i
            self.dma_count[eng] = i + 1
            self.dma_ops[eng].append(op)
            lim = NDMA_SEM if eng != "pool" else 4
            if i >= lim:
                o = self.dma_ops[eng][i - lim]
                s = self._stream(o)
                cur = deps.get(s)
                if cur is None or cur.idx < o.idx:
                    deps[s] = o
        op.deps = list(deps.values())
        for o in op.deps:
            o.signal = True
        st = self._stream(op)
        for k in reads:
            self.readers.setdefault(k, {})[st] = op
        for k in writes:
            self.last_w[k] = op
            self.readers[k] = {}
        self.ops[eng].append(op)
        return op

    def emit(self, final_ops):
        nc = self.nc
        for o in final_ops:
            o.signal = True
        with contextlib.ExitStack() as es:
            csem = {}
            for e in ("pe", "act", "dve", "pool"):
                csem[e] = es.enter_context(nc.semaphore("c_" + e))
            dsem = {}
            for e in ENGS:
                if self.dma_count[e]:
                    dsem[e] = [es.enter_context(nc.semaphore("d_%s_%d" % (e, i))) for i in range(NDMA_SEM)]
            for e in ENGS:
                c = 0
                for op in self.ops[e]:
                    if op.dma:
                        op.sigval = 16 * (op.dma_i // NDMA_SEM + 1)
                    elif op.signal:
                        c += 1
                        op.sigval = c
            block = es.enter_context(nc.Block())
            engobj = {"pe": block.tensor, "act": block.scalar, "dve": block.vector, "pool": block.gpsimd,
                      "sp": block.sync}

            def make(e):
                def body(eng):
                    waited = {}
                    for op in self.ops[e]:
                        for d in op.deps:
                            if d.dma:
                                sem = dsem[d.eng][d.dma_i % NDMA_SEM]
                                key = (d.eng, d.dma_i % NDMA_SEM)
                            else:
                                sem = csem[d.eng]
                                key = d.eng
                            if waited.get(key, 0) >= d.sigval:
                                continue
                            waited[key] = d.sigval
                            eng.wait_ge(sem, d.sigval)
                        ins = op.fn(eng)
                        if op.dma:
                            ins.then_inc(dsem[e][op.dma_i % NDMA_SEM], 16)
                        elif op.signal:
                            ins.then_inc(csem[e], 1)
                    if e == "sp":
                        for o in final_ops:
                            if o.dma:
                                eng.wait_ge(dsem[o.eng][o.dma_i % NDMA_SEM], o.sigval)
                            else:
                                eng.wait_ge(csem[o.eng], o.sigval)
                return body

            for e in ENGS:
                if self.ops[e] or e == "sp":
                    engobj[e](make(e))


NWI = 3
FQ = "sp"
INV_FREQ = (1.0 / (np.float32(10000.0) ** (np.arange(0, 32, 2, dtype=np.float32) / np.float32(32)))).astype(np.float32)
PI = float(np.pi)


def psk(b):
    return "ps%d" % b


class Builder:
    def __init__(self, S, L, do_mix=True, do_ffn=True, mix_parts=(1, 1, 1)):
        self.S = S
        self.L = L
        self.NT = S // TS
        self.NCH = S // 128
        self.do_mix = do_mix
        self.do_ffn = do_ffn
        self.mix_parts = mix_parts
        self.nc = bass.Bass("TRN2", target_bir_lowering=False)
        self.P = Prog(self.nc)
        self.es = contextlib.ExitStack()
        self.bank_i = 0
        self.reserved = set()
        self.pending_conv = []
        self.stage = 8
        self.brmask = 7

    def sb(self, name, shape, dt):
        return self.es.enter_context(self.nc.sbuf_tensor(name, shape, dt))

    def dram_in(self, name, shape, dt=F32):
        return self.nc.dram_tensor(name, list(shape), dt, kind="ExternalInput")

    def bank(self):
        while self.bank_i in self.reserved:
            self.bank_i = (self.bank_i + 1) % 8
        b = self.bank_i
        self.bank_i = (self.bank_i + 1) % 8
        return b

    def pt(self, b):
        return self.ps[b][:, 0:TS]

    def psb(self, b):
        return self.ps[b][:].bitcast(BF16)

    def mm(self, b, out, lhsT, rhs, start, stop, reads, skip=False):
        if skip:
            self.P.add("pe", lambda e: e.matmul(out, lhsT, rhs, start=start, stop=stop, skip_group_check=True), reads,
                       (psk(b),))
        else:
            self.P.add("pe", lambda e: e.matmul(out, lhsT, rhs, start=start, stop=stop), reads, (psk(b),))

    def tr(self, b, out, in_, ident, reads):
        self.P.add("pe", lambda e: e.transpose(out, in_, ident), reads, (psk(b),))

    def act(self, out, in_, func, reads, writes, bias=None, scale=None):
        kw = {}
        if bias is not None:
            kw["bias"] = bias
        if scale is not None:
            kw["scale"] = scale
        self.P.add("act", lambda e: e.activation(out=out, in_=in_, func=func, **kw), reads, writes)

    def ts(self, eng, out, in0, s1, s2, op0, op1, reads, writes):
        if op1 is None:
            self.P.add(eng, lambda e: e.tensor_scalar(out=out, in0=in0, scalar1=s1, scalar2=None, op0=op0), reads, writes)
        else:
            self.P.add(eng, lambda e: e.tensor_scalar(out=out, in0=in0, scalar1=s1, scalar2=s2, op0=op0, op1=op1),
                       reads, writes)

    def rsqrt(self, out, in_, scale, reads, writes):
        self.act(out, in_, AF.Sqrt, tuple(reads) + ("eps_t",), writes, bias=self.eps_t[:out.shape[0], :], scale=scale)
        self.P.add("dve", lambda e: e.reciprocal(out=out, in_=out), writes, writes)

    def stt(self, eng, out, in0, scalar, in1, op0, op1, reads, writes):
        self.P.add(eng, lambda e: e.scalar_tensor_tensor(out=out, in0=in0, scalar=scalar, in1=in1, op0=op0, op1=op1),
                   reads, writes)

    def tt(self, eng, out, in0, in1, op, reads, writes):
        self.P.add(eng, lambda e: e.tensor_tensor(out=out, in0=in0, in1=in1, op=op), reads, writes)

    def red(self, out, in_, reads, writes):
        self.P.add("dve", lambda e: e.tensor_reduce(out=out, in_=in_, axis=AX.X, op=ALU.add), reads, writes)

    def cp(self, eng, out, in_, reads, writes):
        if eng == "act":
            self.P.add("act", lambda e: e.copy(out=out, in_=in_), reads, writes)
        else:
            self.P.add(eng, lambda e: e.tensor_copy(out=out, in_=in_), reads, writes)

    def dma(self, q, out, in_, reads, writes, slow=False):
        if slow:
            return self.P.add(q, lambda e: e.dma_start(out=out, in_=in_, allow_slow_non_contiguous=True), reads, writes,
                              dma=True)
        return self.P.add(q, lambda e: e.dma_start(out=out, in_=in_), reads, writes, dma=True)

    def declare(self):
        nc, S, L = self.nc, self.S, self.L
        self.x_in = self.dram_in("x", (S, D))
        self.pos_in = self.dram_in("positions", (S,), I32)
        names = [
            ("ffn1_norm", (L, D)), ("ffn1_w_in", (L, D, 2 * DFF)), ("ffn1_w_out", (L, DFF, D)),
            ("mix_norm", (L, D)), ("w_in", (L, D, IN_COLS)), ("gm_v_norm", (L, 512)),
            ("gm_w_s", (L, 4, 128, 128)), ("gm_b_s", (L, 4, 128)), ("mla_q_norm", (L, 384)),
            ("mla_kv_norm", (L, 256)), ("mla_w_uq", (L, 384, 768)), ("mla_w_ukv", (L, 256, 1024)),
            ("mla_q_gain", (L, 96)), ("mla_k_gain", (L, 96)), ("ssd_conv_w", (L, 4, 1024)),
            ("ssd_conv_b", (L, 1024)), ("ssd_dt_bias", (L, 8)), ("ssd_a_log", (L, 8)), ("ssd_d", (L, 8)),
            ("ssd_norm", (L, 512)), ("w_branch", (L, 3, 512, D)), ("w_out", (L, D, D)),
            ("ffn2_norm", (L, D)), ("ffn2_w_in", (L, D, 2 * DFF)), ("ffn2_w_out", (L, DFF, D)),
        ]
        self.W = {}
        for n, shp in names:
            self.W[n] = self.dram_in(n, shp)
        self.out = nc.dram_tensor("out", [S, D], F32, kind="ExternalOutput")
        self.xscr = nc.dram_tensor("xscr", [128, NK, S], F32, kind="Internal")
        self.kscr = nc.dram_tensor("kscr", [8, 96, S], BF16, kind="Internal")
        self.vscr = nc.dram_tensor("vscr", [8, 128, S // 128, 65], BF16, kind="Internal")
        self.scr_wi, self.scr_wo, self.scr_in, self.scr_br, self.scr_o, self.scr_uq, self.scr_ukv = {}, {}, {}, {}, {}, {}, {}
        for l in range(L):
            for f in (1, 2):
                self.scr_wi[(l, f)] = nc.dram_tensor("swi_%d_%d" % (l, f), [NJ, 128, NK, 256], BF16, kind="Internal")
                self.scr_wo[(l, f)] = nc.dram_tensor("swo_%d_%d" % (l, f), [2, NJJ, 128, 2, 512], BF16, kind="Internal")
            self.scr_in[l] = nc.dram_tensor("sin_%d" % l, [13, 128, NK, 512], BF16, kind="Internal")
            self.scr_br[l] = nc.dram_tensor("sbr_%d" % l, [3, 128, 4, D], BF16, kind="Internal")
            self.scr_o[l] = nc.dram_tensor("so_%d" % l, [2, 128, NK, 512], BF16, kind="Internal")
            self.scr_uq[l] = nc.dram_tensor("suq_%d" % l, [128, 3, 768], BF16, kind="Internal")
            self.scr_ukv[l] = nc.dram_tensor("sukv_%d" % l, [128, 2, 1024], BF16, kind="Internal")

    def alloc(self):
        nc = self.nc
        sb = self.sb
        NCH = self.NCH
        self.ident_f = sb("ident_f", [128, 128], F32)
        self.ident_b = sb("ident_b", [128, 128], BF16)
        self.ones_b = sb("ones_b", [128, 128], BF16)
        self.ones_f = sb("ones_f", [128, 128], F32)
        self.U_f = sb("U_f", [128, 128], F32)
        self.U_b = sb("U_b", [128, 128], BF16)
        self.Lm_f = sb("Lm_f", [128, 128], F32)
        self.eps_t = sb("eps_t", [128, 1], F32)
        self.one_t = sb("one_t", [128, 1], F32)
        self.negpi_t = sb("negpi_t", [128, 1], F32)
        self.xTs = [sb("xT%d" % i, [128, NK, TS], F32) for i in range(2)]
        self.hT = sb("hT", [128, NK, TS], BF16)
        self.hTf = sb("hTf", [128, NK, TS], BF16)
        self.sq = [sb("sq%d" % i, [128, TS], BF16) for i in range(2)]
        self.sqf = [sb("sqf%d" % i, [128, TS], BF16) for i in range(2)]
        self.rstd = sb("rstd", [128, TS], F32)
        self.rstdf = sb("rstdf", [128, TS], F32)
        self.gains = sb("gains", [128, 2, 3, NK], F32)
        self.Rarg = sb("Rarg", [128, 16, 128], F32)
        self.wif = [sb("wif%d" % i, [128, NK, 256], BF16) for i in range(4)]
        self.wif_i = 0
        self.actT = sb("actT", [128, NJ, TS], BF16)
        self.sg = [sb("sg%d" % i, [128, TS], F32) for i in range(2)]
        self.wi = [sb("wi%d" % i, [128, NK, 512], BF16) for i in range(NWI)]
        self.wo = [sb("wo%d" % i, [128, 2, 512], BF16) for i in range(4)]
        self.ps = [self.es.enter_context(nc.psum_tensor("ps%d" % i, [128, 512], F32)) for i in range(8)]
        self.wi_i = self.wo_i = self.sq_i = self.sqf_i = self.sg_i = self.sig_i = self.pt_i = 0
        if not self.do_mix:
            return
        self.pos_i = sb("pos_i", [128, NCH], I32)
        self.pos_f = sb("pos_f", [128, NCH], F32)
        self.invf = sb("invf", [128, 16], F32)
        av = self.actT[:].rearrange("p j t -> p (j t)").bitcast(F32)
        self.xtm = av[:, 0:NC * D].rearrange("p (c d) -> p c d", c=NC)
        xv = self.xTs[1][:].rearrange("p k t -> p (k t)")
        n16 = NCH * 16
        self.ang = xv[:, 0:n16].rearrange("p (c j) -> p c j", j=16)
        self.angf = xv[:, n16:2 * n16].rearrange("p (c j) -> p c j", j=16)
        self.angi = xv[:, 2 * n16:3 * n16].bitcast(I32).rearrange("p (c j) -> p c j", j=16)
        self.cosT = sb("cosT", [128, NCH, 16], F32)
        self.sinT = sb("sinT", [128, NCH, 16], F32)
        self.gvb = sb("gvb", [128, 512], F32)
        self.bsb = sb("bsb", [128, 4, 128], F32)
        self.Wsraw = sb("Wsraw", [128, 4, 128], F32)
        self.Wsm = sb("Wsm", [128, 4, 128], BF16)
        self.WsT = sb("WsT", [128, 4, 128], BF16)
        self.gqn = sb("gqn", [128, 3], F32)
        self.gkvn = sb("gkvn", [128, 2], F32)
        self.gqb = sb("gqb", [128, 96], F32)
        self.gkb = sb("gkb", [128, 96], F32)
        self.gmx = sb("gmx", [128, 2], F32)
        self.negC = sb("negC", [128, 1], F32)
        self.cw = sb("cw", [128, NK, 4], F32)
        self.cb = sb("cb", [128, NK], F32)
        self.dtb = sb("dtb", [128, 8], F32)
        self.aneg = sb("aneg", [128, 8], F32)
        self.Dsk = sb("Dsk", [128, 8], F32)
        self.gsn = sb("gsn", [128, 512], F32)
        self.uT = sb("uT", [128, 4, TS], BF16)
        self.vg = sb("vg", [128, 512], F32)
        self.junk = sb("junk", [128, 768], F32)
        self.s1 = sb("s1", [128, 8], F32)
        self.s2 = sb("s2", [128, 8], F32)
        self.vtm = sb("vtm", [128, NC, 512], BF16)
        self.zs = sb("zs", [128, NC, 512], BF16)
        self.cqnT = sb("cqnT", [128, 3, TS], BF16)
        self.ckvnT = sb("ckvnT", [128, 2, TS], BF16)
        self.sqq = sb("sqq", [128, 5, TS], BF16)
        self.krdt = sb("krdt", [128, NC, 40], F32)
        self.xbcp = [sb("xbcp%d" % i, [128, TS + 3], F32) for i in range(2)]
        self.hist = sb("hist", [128, NK, 3], F32)
        self.cacc = self.vg[:, 0:TS]
        self.xbcT = sb("xbcT", [128, NK, TS], BF16)
        self.spb = sb("spb", [128, 4, 128], F32)
        self.yaT = sb("yaT", [128, 4, TS], BF16)
        self.dt8 = sb("dt8", [128, 8], F32)
        self.da8 = sb("da8", [128, 8], F32)
        self.t8a = sb("t8a", [128, 8], F32)
        self.t8b = sb("t8b", [128, 8], F32)
        self.cs8 = sb("cs8", [128, 8], F32)
        self.tot8 = sb("tot8", [128, 8], F32)
        self.dout8 = sb("dout8", [128, 8], F32)
        self.etot8 = sb("etot8", [128, 8], F32)
        self.R = self.Rarg[:, 0:8, :]
        self.arg = self.Rarg[:, 8:16, :]
        self.ecs = self.R
        self.cbm = sb("cbm", [128, 2, 128], F32)
        self.MT = sb("MT", [128, 8, 128], BF16)
        self.Cs = sb("Cs", [128, 8, 128], BF16)
        self.xdt = sb("xdt", [128, 8, 64], BF16)
        self.xdd = sb("xdd", [128, 8, 64], BF16)
        self.xsd = sb("xsd", [128, 8, 64], F32)
        self.Btm = sb("Btm", [128, 2, 128], BF16)
        self.Sst = sb("Sst", [128, 8, 64], F32)
        self.Sbf = sb("Sbf", [128, 8, 64], BF16)
        self.y1 = self.vg
        self.yctm = sb("yctm", [128, 512], BF16)
        self.ycT = sb("ycT", [128, 4, TS], BF16)
        self.rq2 = sb("rq2", [128, 2], F32)
        self.qtm = sb("qtm", [128, 8, 96], F32)
        self.hss = sb("hss", [128, 8], F32)
        self.r1 = sb("r1", [128, 8, 16], F32)
        self.r2 = sb("r2", [128, 8, 16], F32)
        self.qr = sb("qr", [128, 8, 96], BF16)
        self.qT = sb("qT", [128, 8, TS], BF16)
        self.kst = sb("kst", [128, 8, 128], BF16)
        self.vst = sb("vst", [128, 8, 65], BF16)
        self.kh = [sb("kh%d" % i, [128, self.S], BF16) for i in range(2)]
        self.vh = [sb("vh%d" % i, [128, NCH, 65], BF16) for i in range(1)]
        self.kv_i = 0
        self.pT = [sb("pT%d" % i, [128, TS], BF16) for i in range(4)]
        self.rden = sb("rden", [128, 4], F32)
        self.ybtm = sb("ybtm", [128, NC, 512], BF16)
        self.ybT = sb("ybT", [128, 4, TS], BF16)
        self.sig = [sb("sig%d" % i, [128, TS], F32) for i in range(2)]
        self.mgacc = sb("mgacc", [128, 4, TS], F32)
        self.mgT = sb("mgT", [128, NK, TS], BF16)

    def consts(self):
        P = self.P
        idf, idb, ones_b, ones_f, U_f, U_b, Lm_f = self.ident_f, self.ident_b, self.ones_b, self.ones_f, self.U_f, self.U_b, self.Lm_f
        P.add("pool", lambda e: e.memset(idf[:], 0.0), (), ("ident_f",))
        P.add("pool", lambda e: e.affine_select(out=idf[:], in_=idf[:], pattern=[[-1, 128]], compare_op=ALU.not_equal,
                                                fill=1.0, base=0, channel_multiplier=1), ("ident_f",), ("ident_f",))
        P.add("pool", lambda e: e.tensor_copy(out=idb[:], in_=idf[:]), ("ident_f",), ("ident_b",))
        P.add("pool", lambda e: e.memset(ones_b[:], 1.0), (), ("ones_b",))
        P.add("pool", lambda e: e.memset(ones_f[:], 1.0), (), ("ones_f",))
        P.add("pool", lambda e: e.memset(U_f[:], 1.0), (), ("U_f",))
        P.add("pool", lambda e: e.affine_select(out=U_f[:], in_=U_f[:], pattern=[[1, 128]], compare_op=ALU.is_ge,
                                                fill=0.0, base=0, channel_multiplier=-1), ("U_f",), ("U_f",))
        P.add("pool", lambda e: e.tensor_copy(out=U_b[:], in_=U_f[:]), ("U_f",), ("U_b",))
        P.add("pool", lambda e: e.memset(Lm_f[:], 1.0), (), ("Lm_f",))
        P.add("pool", lambda e: e.affine_select(out=Lm_f[:], in_=Lm_f[:], pattern=[[-1, 128]], compare_op=ALU.is_ge,
                                                fill=0.0, base=0, channel_multiplier=1), ("Lm_f",), ("Lm_f",))
        eps_t, one_t, negpi_t = self.eps_t, self.one_t, self.negpi_t
        P.add("pool", lambda e: e.memset(eps_t[:], EPS), (), ("eps_t",))
        P.add("pool", lambda e: e.memset(one_t[:], 1.0), (), ("one_t",))
        P.add("pool", lambda e: e.memset(negpi_t[:], -PI), (), ("negpi_t",))
        if not self.do_mix:
            return
        invf = self.invf
        for j in range(16):
            P.add("pool", lambda e, j=j: e.memset(invf[:, j:j + 1], float(INV_FREQ[j])), (), ("invf",))
        NCH = self.NCH
        self.dma("sp", self.pos_i[:], self.pos_in.ap().rearrange("(c p) -> p c", p=128), (), ("pos_i",), slow=True)
        self.cp("dve", self.pos_f[:], self.pos_i[:], ("pos_i",), ("pos_f",))
        self.tt("dve", self.ang, self.pos_f[:, :, None].broadcast_to([128, NCH, 16]),
                self.invf[:, None, :].broadcast_to([128, NCH, 16]), ALU.mult, ("pos_f", "invf"), ("ang", "xT1"))
        for dst, dk, shift in ((self.sinT, "sinT", 0.0), (self.cosT, "cosT", 0.5 * PI)):
            self.ts("dve", dst[:], self.ang, shift, 1.0 / (2 * PI), ALU.add, ALU.mult, ("ang", "xT1"), (dk,))
            self.cp("dve", self.angi, dst[:], (dk,), ("angi", "xT1"))
            self.cp("dve", self.angf, self.angi, ("angi", "xT1"), ("angf", "xT1"))
            self.ts("dve", dst[:], self.ang, shift, None, ALU.add, None, ("ang", "xT1"), (dk,))
            self.stt("dve", dst[:], self.angf, -2 * PI, dst[:], ALU.mult, ALU.add, ("angf", dk, "xT1"), (dk,))
            self.ts("dve", self.angf, dst[:], PI, 2 * PI, ALU.is_gt, ALU.mult, (dk,), ("angf", "xT1"))
            self.tt("dve", dst[:], dst[:], self.angf, ALU.subtract, (dk, "angf", "xT1"), (dk,))
            self.ts("dve", self.angf, dst[:], -PI, 2 * PI, ALU.is_lt, ALU.mult, (dk,), ("angf", "xT1"))
            self.tt("dve", dst[:], dst[:], self.angf, ALU.add, (dk, "angf", "xT1"), (dk,))
            self.act(dst[:], dst[:], AF.Sin, (dk,), (dk,))

    def conv_list(self, l):
        lst = []

        def add(out, in_, key):
            lst.append(lambda: self.dma("pool", out, in_, (), (key,)))

        if self.do_ffn:
            for f in (1, 2):
                w_in = self.W["ffn%d_w_in" % f][l].rearrange("(k p) c -> p k c", p=128)
                w_out = self.W["ffn%d_w_out" % f][l].rearrange("(jj j2 p) c -> jj p j2 c", j2=2, p=128)
                swi, swo = self.scr_wi[(l, f)], self.scr_wo[(l, f)]
                for j in range(NJ):
                    for h in range(2):
                        add(swi[j][:, :, h * 128:(h + 1) * 128],
                            w_in[:, :, h * DFF + j * 128: h * DFF + (j + 1) * 128], ("swi", l, f, j, h))
                for mg in range(2):
                    for jj in range(NJJ):
                        add(swo[mg, jj], w_out[jj][:, :, mg * 512:(mg + 1) * 512], ("swo", l, f, mg, jj))
        if self.do_mix:
            w = self.W["w_in"][l].rearrange("(k p) c -> p k c", p=128)
            sin = self.scr_in[l]
            srcs = {0: (0, 512), 1: (512, 1024), 2: (1024, 1536), 4: (1696, 2208), 5: (2208, 2720), 6: (2720, 3232)}
            for i in range(6):
                srcs[7 + i] = (3240 + 512 * i, 3240 + 512 * (i + 1))
            for bi, (a, b_) in srcs.items():
                add(sin[bi], w[:, :, a:b_], ("sin", l, bi))
            add(sin[3][:, :, 0:160], w[:, :, 1536:1696], ("sin", l, 3, 0))
            add(sin[3][:, :, 160:168], w[:, :, 3232:3240], ("sin", l, 3, 1))
            for i in range(3):
                add(self.scr_br[l][i], self.W["w_branch"][l, i].rearrange("(kk p) c -> p kk c", p=128), ("sbr", l, i))
            wo_ = self.W["w_out"][l].rearrange("(k p) c -> p k c", p=128)
            for hf in range(2):
                add(self.scr_o[l][hf], wo_[:, :, hf * 512:(hf + 1) * 512], ("so", l, hf))
            add(self.scr_uq[l][:], self.W["mla_w_uq"][l].rearrange("(i p) c -> p i c", p=128), ("suq", l))
            add(self.scr_ukv[l][:], self.W["mla_w_ukv"][l].rearrange("(i p) c -> p i c", p=128), ("sukv", l))
        return lst

    def load_gains(self, l):
        W = self.W
        for i, n in enumerate(("ffn1_norm", "mix_norm", "ffn2_norm")):
            self.dma("sp", self.gains[:, l % 2, i, :], W[n][l].rearrange("(k p) -> p k", p=128), (), (("gains", l % 2),),
                     slow=True)

    def load_params(self, l):
        W = self.W

        def bc(ap, shape):
            return ap.broadcast_to(shape)

        self.dma("sp", self.gvb[:], bc(W["gm_v_norm"][l:l + 1, :], [128, 512]), (), ("gvb",), slow=True)
        self.dma("sp", self.bsb[:], bc(W["gm_b_s"][l:l + 1], [128, 4, 128]), (), ("bsb",), slow=True)
        self.dma("sp", self.Wsraw[:], W["gm_w_s"][l].rearrange("g t s -> t g s"), (), ("Wsraw",))
        self.dma("sp", self.gqn[:], W["mla_q_norm"][l].rearrange("(i p) -> p i", p=128), (), ("gqn",), slow=True)
        self.dma("sp", self.gkvn[:], W["mla_kv_norm"][l].rearrange("(i p) -> p i", p=128), (), ("gkvn",), slow=True)
        self.dma("sp", self.gqb[:], bc(W["mla_q_gain"][l:l + 1, :], [128, 96]), (), ("gqb",), slow=True)
        self.dma("sp", self.gkb[:], bc(W["mla_k_gain"][l:l + 1, :], [128, 96]), (), ("gkb",), slow=True)
        for k in range(4):
            self.dma("sp", self.cw[:, :, k], W["ssd_conv_w"][l, k].rearrange("(c p) -> p c", p=128), (), ("cw",), slow=True)
        self.dma("sp", self.cb[:], W["ssd_conv_b"][l].rearrange("(c p) -> p c", p=128), (), ("cb",), slow=True)
        self.dma("sp", self.dtb[:], bc(W["ssd_dt_bias"][l:l + 1, :], [128, 8]), (), ("dtb",), slow=True)
        self.dma("sp", self.aneg[:], bc(W["ssd_a_log"][l:l + 1, :], [128, 8]), (), ("aneg",), slow=True)
        self.dma("sp", self.Dsk[:], bc(W["ssd_d"][l:l + 1, :], [128, 8]), (), ("Dsk",), slow=True)
        self.dma("sp", self.gsn[:], bc(W["ssd_norm"][l:l + 1, :], [128, 512]), (), ("gsn",), slow=True)
        self.act(self.aneg[:], self.aneg[:], AF.Exp, ("aneg",), ("aneg",))
        self.ts("dve", self.aneg[:], self.aneg[:], -1.0, None, ALU.mult, None, ("aneg",), ("aneg",))
        self.tt("dve", self.Wsm[:], self.Wsraw[:], self.Lm_f[:, None, :].broadcast_to([128, 4, 128]), ALU.mult,
                ("Wsraw", "Lm_f"), ("Wsm",))
        b = self.bank()
        for g in range(4):
            self.tr(b, self.psb(b)[:, g * 128:(g + 1) * 128], self.Wsm[:, g, :], self.ident_b[:], ("Wsm", "ident_b"))
        self.cp("dve", self.WsT[:], self.psb(b)[:, 0:512].rearrange("p (g t) -> p g t", g=4), (psk(b),), ("WsT",))
        self.P.add("dve", lambda e: e.tensor_reduce(out=self.gmx[:, 0:1], in_=self.gqb[:], axis=AX.X, op=ALU.max,
                                                    apply_absolute_value=True), ("gqb",), ("gmx",))
        self.P.add("dve", lambda e: e.tensor_reduce(out=self.gmx[:, 1:2], in_=self.gkb[:], axis=AX.X, op=ALU.max,
                                                    apply_absolute_value=True), ("gkb",), ("gmx",))
        self.stt("dve", self.negC[:], self.gmx[:, 0:1], -float(np.sqrt(96.0)), self.gmx[:, 1:2], ALU.mult, ALU.mult,
                 ("gmx",), ("negC",))
        self.P.add("dve", lambda e: e.memset(self.Sst[:], 0.0), (), ("Sst",))
        self.P.add("dve", lambda e: e.memset(self.Sbf[:], 0.0), (), ("Sbf",))
        self.P.add("dve", lambda e: e.memset(self.hist[:], 0.0), (), ("hist",))
        if l == 0:
            self.P.add("dve", lambda e: e.memset(self.vst[:, :, 64:65], 1.0), (), ("vst1",))

    def load_x(self, l, I, x, xk):
        if l > 0:
            self.dma("sp", x[:], self.xscr[:, :, I * TS:(I + 1) * TS], (("xscr", I),), (xk,))
            return
        XK = tuple(("actT", j) for j in range(NJ))
        src = self.x_in[I * TS:(I + 1) * TS, :].rearrange("(c p) d -> p c d", p=128)
        self.dma("sp", self.xtm, src, (), XK)
        for k in range(NK):
            b = self.bank()
            for c in range(NC):
                self.tr(b, self.ps[b][:, c * 128:(c + 1) * 128], self.xtm[:, c, k * 128:(k + 1) * 128], self.ident_f[:],
                        XK + ("ident_f",))
            eng = "dve" if k % 2 == 0 else "act"
            self.cp(eng, x[:, k, :], self.pt(b), (psk(b),), (xk,))

    def store_x(self, l, I, x, xk):
        if l < self.L - 1:
            return [self.dma("sp", self.xscr[:, :, I * TS:(I + 1) * TS], x[:], (xk,), (("xscr", I),))]
        XK = tuple(("actT", j) for j in range(NJ))
        for c in range(NC):
            for hlf in range(2):
                b = self.bank()
                for kk in range(4):
                    k = hlf * 4 + kk
                    self.tr(b, self.ps[b][:, kk * 128:(kk + 1) * 128], x[:, k, c * 128:(c + 1) * 128],
                            self.ident_f[:], (xk, "ident_f"))
                eng = "dve" if hlf == 0 else "act"
                self.cp(eng, self.xtm[:, c, hlf * 512:(hlf + 1) * 512], self.ps[b][:], (psk(b),), XK)
        dst = self.out[I * TS:(I + 1) * TS, :].rearrange("(c p) d -> p c d", p=128)
        return [self.dma("sp", dst, self.xtm, XK, ())]

    def rmsnorm_T(self, l, gi, x, xk, ffn):
        hT, hname = (self.hTf, "hTf") if ffn else (self.hT, "hT")
        rstd, rk = (self.rstdf, "rstdf") if ffn else (self.rstd, "rstd")
        b = self.bank()
        for k in range(NK):
            if ffn:
                sq, sqk = self.sqf[self.sqf_i % 2], "sqf%d" % (self.sqf_i % 2)
                self.sqf_i += 1
            else:
                sq, sqk = self.sq[self.sq_i % 2], "sq%d" % (self.sq_i % 2)
                self.sq_i += 1
            self.act(sq[:], x[:, k, :], AF.Square, (xk,), (sqk,))
            self.mm(b, self.pt(b), self.ones_b[:], sq[:], k == 0, k == NK - 1, (sqk, "ones_b"))
        self.rsqrt(rstd[:], self.pt(b), 1.0 / D, (psk(b),), (rk,))
        for k in range(NK):
            self.stt("dve", hT[:, k, :], x[:, k, :], self.gains[:, l % 2, gi, k:k + 1], rstd[:], ALU.mult, ALU.mult,
                     (xk, ("gains", l % 2), rk), ((hname, k),))

    def ffn(self, l, f, x, xk):
        gi = 0 if f == 1 else 2
        self.rmsnorm_T(l, gi, x, xk, True)
        yield
        swi, swo = self.scr_wi[(l, f)], self.scr_wo[(l, f)]
        for j in range(NJ):
            wi = self.wif[self.wif_i % 4]
            wik = "wif%d" % (self.wif_i % 4)
            self.wif_i += 1
            self.dma(FQ, wi[:], swi[j], (("swi", l, f, j, 0), ("swi", l, f, j, 1)), (wik,))
            bg, bu = self.bank(), self.bank()
            for (b, off) in ((bg, 0), (bu, 128)):
                for k in range(NK):
                    self.mm(b, self.pt(b), wi[:, k, off:off + 128], self.hTf[:, k, :],
                            k == 0, k == NK - 1, (wik, ("hTf", k)))
            sg = self.sg[self.sg_i % 2]
            sgk = "sg%d" % (self.sg_i % 2)
            self.sg_i += 1
            self.act(sg[:], self.pt(bg), AF.Silu, (psk(bg),), (sgk,))
            self.tt("dve", self.actT[:, j, :], sg[:], self.pt(bu), ALU.mult, (sgk, psk(bu)), (("actT", j),))
            yield
        for mg in range(2):
            banks = [self.bank() for _ in range(2)]
            self.reserved.update(banks)
            for jj in range(NJJ):
                wo = self.wo[self.wo_i % 4]
                wok = "wo%d" % (self.wo_i % 4)
                self.wo_i += 1
                self.dma(FQ, wo[:], swo[mg, jj], (("swo", l, f, mg, jj),), (wok,))
                for j2 in range(2):
                    j = jj * 2 + j2
                    for m in range(4):
                        b = banks[m // 2]
                        o = self.ps[b][:, (m % 2) * TS:(m % 2 + 1) * TS]
                        self.mm(b, o, wo[:, j2, m * 128:(m + 1) * 128], self.actT[:, j, :],
                                j == 0 and m % 2 == 0, j == NJ - 1, (wok, ("actT", j)), skip=True)
                yield
            for m in range(4):
                b = banks[m // 2]
                k = mg * 4 + m
                o = self.ps[b][:, (m % 2) * TS:(m % 2 + 1) * TS]
                self.stt("dve", x[:, k, :], o, 0.5, x[:, k, :], ALU.mult, ALU.add, (psk(b), xk), (xk,))
            self.reserved.difference_update(banks)
            yield
    def plan_loads(self, l):
        sin_ = self.scr_in[l]
        L_ = []
        for bi in (0, 1, 2):
            L_.append(("blk", sin_[bi], (("sin", l, bi),), 512, NK))
        L_.append(("blk", sin_[3][:, :, 0:168], (("sin", l, 3, 0), ("sin", l, 3, 1)), 168, NK))
        for bi in (4, 5, 6):
            L_.append(("blk", sin_[bi], (("sin", l, bi),), 512, NK))
        L_.append(("flat", self.scr_uq[l][:], (("suq", l),), 3, 768))
        L_.append(("flat", self.scr_ukv[l][:], (("sukv", l),), 2, 1024))
        for half in range(2):
            for i in range(3):
                L_.append(("blk", sin_[7 + 2 * i + half], (("sin", l, 7 + 2 * i + half),), 512, NK))
                L_.append(("blk", self.scr_br[l][i][:, :, half * 512:(half + 1) * 512], (("sbr", l, i),), 512, 4))
        for half in range(2):
            L_.append(("blk", self.scr_o[l][half], (("so", l, half),), 512, NK))
        self.loads = L_
        self.ld_issued = []
        self.ld_next = 0
        self.PREF = getattr(self, "PREF", 0)
        self.MQ = getattr(self, "MQ", "sp")

    def _issue(self):
        kind, src, key, a, b_ = self.loads[len(self.ld_issued)]
        wi = self.wi[self.wi_i % NWI]
        wik = "wi%d" % (self.wi_i % NWI)
        self.wi_i += 1
        if kind == "blk":
            v = wi
            self.dma(self.MQ, wi[:, 0:b_, 0:a], src, key, (wik,))
        else:
            v = wi[:].rearrange("p k c -> p (k c)")[:, 0:a * b_].rearrange("p (i c) -> p i c", i=a)
            self.dma(self.MQ, v, src, key, (wik,))
        self.ld_issued.append((v, wik))

    def get_blk(self):
        i = self.ld_next
        self.ld_next += 1
        while len(self.ld_issued) < min(i + 1 + self.PREF, len(self.loads)):
            self._issue()
        return self.ld_issued[i]

    def load_blk(self, src, key, ncols=512, nk=NK):
        return self.get_blk()

    def load_flat(self, src, key, n_i, n_c):
        return self.get_blk()

    def fm_chunk(self, wi, wik, col0, nk=NK, rhs=None, rkey=None):
        b = self.bank()
        for k in range(nk):
            r = self.hT[:, k, :] if rhs is None else rhs[:, k, :]
            rk = ("hT", k) if rhs is None else rkey
            self.mm(b, self.pt(b), wi[:, k, col0:col0 + 128], r, k == 0, k == nk - 1, (wik, rk))
        return b

    def tm_chunk(self, wi, wik, c, col0, ncols):
        b = self.bank()
        for k in range(NK):
            self.mm(b, self.ps[b][:, 0:ncols], self.hT[:, k, c * 128:(c + 1) * 128], wi[:, k, col0:col0 + ncols],
                    k == 0, k == NK - 1, (wik, ("hT", k)))
        return b

    def head_norm_rope(self, cg, gb, gbk, dst_T, dst_key, col0):
        q3 = self.qtm[:]
        self.tt("dve", self.junk[:].rearrange("p (h d) -> p h d", h=8), q3, q3, ALU.mult, ("qtm",), ("junk",))
        self.red(self.hss[:], self.junk[:].rearrange("p (h d) -> p h d", h=8), ("junk",), ("hss",))
        self.rsqrt(self.hss[:], self.hss[:], 1.0 / 96, ("hss",), ("hss",))
        self.tt("dve", q3, q3, self.hss[:, :, None].broadcast_to([128, 8, 96]), ALU.mult, ("qtm", "hss"), ("qtm",))
        self.tt("dve", q3, q3, gb[:, None, :].broadcast_to([128, 8, 96]), ALU.mult, ("qtm", gbk), ("qtm",))
        cos = self.cosT[:, cg, :][:, None, :].broadcast_to([128, 8, 16])
        sin = self.sinT[:, cg, :][:, None, :].broadcast_to([128, 8, 16])
        x1, x2 = self.qtm[:, :, 64:80], self.qtm[:, :, 80:96]
        self.cp("act", self.qr[:, :, 0:64], self.qtm[:, :, 0:64], ("qtm",), ("qr",))
        self.tt("dve", self.r1[:], x1, cos, ALU.mult, ("qtm", "cosT"), ("r1",))
        self.tt("dve", self.r2[:], x2, sin, ALU.mult, ("qtm", "sinT"), ("r2",))
        self.tt("dve", self.qr[:, :, 64:80], self.r1[:], self.r2[:], ALU.subtract, ("r1", "r2"), ("qr",))
        self.tt("dve", self.r1[:], x2, cos, ALU.mult, ("qtm", "cosT"), ("r1",))
        self.tt("dve", self.r2[:], x1, sin, ALU.mult, ("qtm", "sinT"), ("r2",))
        self.tt("dve", self.qr[:, :, 80:96], self.r1[:], self.r2[:], ALU.add, ("r1", "r2"), ("qr",))
        yield 10
        for g in range(2):
            b = self.bank()
            pb = self.psb(b)
            for hh in range(4):
                self.tr(b, pb[0:96, hh * 128:(hh + 1) * 128], self.qr[:, g * 4 + hh, :], self.ident_b[:], ("qr", "ident_b"))
            self.cp("act" if g else "dve", dst_T[0:96, g * 4:g * 4 + 4, col0:col0 + 128],
                    pb[0:96, 0:512].rearrange("p (h t) -> p h t", h=4), (psk(b),), (dst_key,))

    def mixer(self, l, I, x, xk):
        sin_ = self.scr_in[l]
        self.plan_loads(l)
        self.rmsnorm_T(l, 1, x, xk, False)
        yield
        wi, wik = self.load_blk(sin_[0], (("sin", l, 0),))
        for ch in range(4):
            b = self.fm_chunk(wi, wik, ch * 128)
            self.act(self.uT[:, ch, :], self.pt(b), AF.Gelu, (psk(b),), ("uT",))
        yield
        wi, wik = self.load_blk(sin_[1], (("sin", l, 1),))
        for c in range(NC):
            b = self.tm_chunk(wi, wik, c, 0, 512)
            self.act(self.vg[:], self.ps[b][:], AF.Gelu, (psk(b),), ("vg",))
            self.act(self.junk[:, 0:512], self.vg[:], AF.Square, ("vg",), ("junk",))
            self.red(self.s1[:, 0:1], self.junk[:, 0:512], ("junk",), ("s1",))
            self.rsqrt(self.s1[:, 0:1], self.s1[:, 0:1], 1.0 / 512, ("s1",), ("s1",))
            self.stt("dve", self.vtm[:, c, :], self.vg[:], self.s1[:, 0:1], self.gvb[:], ALU.mult, ALU.mult,
                     ("vg", "s1", "gvb"), (("vtm", c),))
        yield
        wi2, wi2k = self.load_blk(sin_[2], (("sin", l, 2),))
        wi3, wi3k = self.load_blk(sin_[3][:, :, 0:168], (("sin", l, 3, 0), ("sin", l, 3, 1)), ncols=168)
        for i in range(5):
            if i < 4:
                b = self.fm_chunk(wi2, wi2k, i * 128)
            else:
                b = self.fm_chunk(wi3, wi3k, 0)
            self.act(self.sqq[:, i, :], self.pt(b), AF.Square, (psk(b),), ("sqq",))
            if i < 3:
                self.ts("dve", self.cqnT[:, i, :], self.pt(b), self.gqn[:, i:i + 1], None, ALU.mult, None,
                        (psk(b), "gqn"), ("cqnT",))
            else:
                self.ts("dve", self.ckvnT[:, i - 3, :], self.pt(b), self.gkvn[:, i - 3:i - 2], None, ALU.mult, None,
                        (psk(b), "gkvn"), ("ckvnT",))
        for c in range(NC):
            b = self.tm_chunk(wi3, wi3k, c, 128, 40)
            self.cp("dve", self.krdt[:, c, :], self.ps[b][:, 0:40], (psk(b),), ("krdt",))
        yield
        wi, wik = self.load_blk(sin_[4], (("sin", l, 4),))
        for c in range(NC):
            b = self.tm_chunk(wi, wik, c, 0, 512)
            self.act(self.zs[:, c, :], self.ps[b][:], AF.Silu, (psk(b),), ("zs",))
        for blk in range(2):
            yield
            wi, wik = self.load_blk(sin_[5 + blk], (("sin", l, 5 + blk),))
            for cc in range(4):
                ch = blk * 4 + cc
                b = self.fm_chunk(wi, wik, cc * 128)
                xp = self.xbcp[ch % 2]
                xk = "xbcp%d" % (ch % 2)
                self.cp("act", xp[:, 3:TS + 3], self.pt(b), (psk(b),), (xk,))
                self.cp("dve", xp[:, 0:3], self.hist[:, ch, :], ("hist",), (xk,))
                self.ts("dve", self.cacc, xp[:, 3:TS + 3], self.cw[:, ch, 3:4], self.cb[:, ch:ch + 1], ALU.mult, ALU.add,
                        (xk, "cw", "cb"), ("vg",))
                for t in range(3):
                    self.stt("dve", self.cacc, xp[:, t:TS + t], self.cw[:, ch, t:t + 1], self.cacc, ALU.mult, ALU.add,
                             (xk, "cw", "vg"), ("vg",))
                self.cp("dve", self.hist[:, ch, :], xp[:, TS:TS + 3], (xk,), ("hist",))
                self.act(self.xbcT[:, ch, :], self.cacc, AF.Silu, ("vg",), ("xbcT",))
                yield
        wq = self.load_flat(self.scr_uq[l][:], (("suq", l),), 3, 768)
        wkv = self.load_flat(self.scr_ukv[l][:], (("sukv", l),), 2, 1024)
        for c in range(NC):
            yield from self.mla_chunk(c, I * NC + c, slice(c * 128, (c + 1) * 128), wq, wkv)
            yield

        def side():
            for c in range(NC):
                cg = I * NC + c
                ck = slice(c * 128, (c + 1) * 128)
                b = self.bank()
                for g in range(4):
                    self.mm(b, self.ps[b][:, g * 128:(g + 1) * 128], self.vtm[:, c, g * 128:(g + 1) * 128], self.WsT[:, g, :],
                            True, True, (("vtm", c), "WsT"))
                self.tt("dve", self.spb[:], self.ps[b][:].rearrange("p (g t) -> p g t", g=4), self.bsb[:], ALU.add,
                        (psk(b), "bsb"), ("spb",))
                self.tt("dve", self.yaT[:, :, ck], self.spb[:], self.uT[:, :, ck], ALU.mult, ("spb", "uT"), ("yaT",))
                yield from self.ssd_chunk(c, cg, ck)

        att = self.attention(I)
        sd = side()
        n_att = 8 * (NC * I + NC) + 8
        n_side = NC * 12
        acc = 0.0
        att_done = side_done = False
        while not (att_done and side_done):
            if not att_done:
                try:
                    next(att)
                except StopIteration:
                    att_done = True
            yield
            acc += n_side / n_att
            while (acc >= 1.0 or att_done) and not side_done:
                acc -= 1.0
                try:
                    next(sd)
                except StopIteration:
                    side_done = True
        yield from self.merge(l, x, xk)

    def ssd_chunk(self, c, cg, ck):
        xs = self.xbcT
        self.tt("dve", self.t8a[:], self.krdt[:, c, 32:40], self.dtb[:], ALU.add, ("krdt", "dtb"), ("t8a",))
        self.ts("dve", self.t8b[:], self.t8a[:], -1.0, None, ALU.mult, None, ("t8a",), ("t8b",))
        self.tt("dve", self.t8b[:], self.t8b[:], self.t8a[:], ALU.max, ("t8a", "t8b"), ("t8b",))
        self.act(self.t8b[:], self.t8b[:], AF.Exp, ("t8b",), ("t8b",), scale=-1.0)
        self.act(self.t8b[:], self.t8b[:], AF.Ln, ("t8b", "one_t"), ("t8b",), bias=self.one_t[:])
        self.stt("dve", self.dt8[:], self.t8a[:], 0.0, self.t8b[:], ALU.max, ALU.add, ("t8a", "t8b"), ("dt8",))
        self.tt("dve", self.da8[:], self.dt8[:], self.aneg[:], ALU.mult, ("dt8", "aneg"), ("da8",))
        yield
        b = self.bank()
        self.mm(b, self.ps[b][:, 0:8], self.U_f[:], self.da8[:], True, True, ("U_f", "da8"))
        self.cp("dve", self.cs8[:], self.ps[b][:, 0:8], (psk(b),), ("cs8",))
        self.tt("dve", self.R[:], self.U_f[:, None, :].broadcast_to([128, 8, 128]),
                self.da8[:, :, None].broadcast_to([128, 8, 128]), ALU.mult, ("U_f", "da8"), ("R",))
        yield
        bb = [self.bank(), self.bank()]
        for hh in range(2):
            self.mm(bb[hh], self.ps[bb[hh]][:], self.ones_f[:], self.R[:, hh * 4:hh * 4 + 4, :].rearrange("p h l -> p (h l)"),
                    True, True, ("ones_f", "R"))
        for hh in range(2):
            p3 = self.ps[bb[hh]][:].rearrange("p (h l) -> p h l", h=4)
            hs = slice(hh * 4, hh * 4 + 4)
            self.tt("dve", self.arg[:, hs, :], p3, self.cs8[:, hs, None].broadcast_to([128, 4, 128]), ALU.subtract,
                    (psk(bb[hh]), "cs8"), ("arg",))
            self.act(self.ecs[:, hs, :], p3, AF.Exp, (psk(bb[hh]),), ("R",))
            self.cp("dve", self.tot8[:, hs], p3[:, :, 127], (psk(bb[hh]),), ("tot8",))
        self.act(self.arg[:], self.arg[:], AF.Relu, ("arg",), ("arg",), scale=-1.0)
        self.act(self.arg[:], self.arg[:], AF.Exp, ("arg",), ("arg",), scale=-1.0)
        self.tt("dve", self.t8a[:], self.tot8[:], self.cs8[:], ALU.subtract, ("tot8", "cs8"), ("t8a",))
        self.act(self.dout8[:], self.t8a[:], AF.Exp, ("t8a",), ("dout8",))
        self.act(self.etot8[:], self.tot8[:], AF.Exp, ("tot8",), ("etot8",))
        yield
        b = self.bank()
        for g in range(2):
            self.mm(b, self.ps[b][:, g * 128:(g + 1) * 128], xs[:, 4 + g, ck], xs[:, 6 + g, ck], True, True, ("xbcT",))
        self.tt("dve", self.cbm[:], self.ps[b][:, 0:256].rearrange("p (g l) -> p g l", g=2),
                self.U_f[:, None, :].broadcast_to([128, 2, 128]), ALU.mult, (psk(b), "U_f"), ("cbm",))
        self.tt("dve", self.MT[:].rearrange("p (g r) l -> p g r l", g=2), self.arg[:].rearrange("p (g r) l -> p g r l", g=2),
                self.cbm[:, :, None, :].broadcast_to([128, 2, 4, 128]), ALU.mult, ("arg", "cbm"), ("MT",))
        self.tt("dve", self.Cs[:].rearrange("p (g r) l -> p g r l", g=2), self.ecs[:].rearrange("p (g r) l -> p g r l", g=2),
                xs[:, 6:8, ck][:, :, None, :].broadcast_to([128, 2, 4, 128]), ALU.mult, ("R", "xbcT"), ("Cs",))
        yield
        b = self.bank()
        pb = self.psb(b)
        for i in range(4):
            self.tr(b, pb[:, i * 128:(i + 1) * 128], xs[:, i, ck], self.ident_b[:], ("xbcT", "ident_b"))
        x3 = pb[:, 0:512].rearrange("p (h d) -> p h d", h=8)
        self.tt("dve", self.xdt[:], x3, self.dt8[:, :, None].broadcast_to([128, 8, 64]), ALU.mult, (psk(b), "dt8"), ("xdt",))
        self.tt("dve", self.xsd[:], x3, self.Dsk[:, :, None].broadcast_to([128, 8, 64]), ALU.mult, (psk(b), "Dsk"), ("xsd",))
        self.tt("dve", self.xdd[:], self.xdt[:], self.dout8[:, :, None].broadcast_to([128, 8, 64]), ALU.mult,
                ("xdt", "dout8"), ("xdd",))
        yield
        b = self.bank()
        pb = self.psb(b)
        for g in range(2):
            self.tr(b, pb[:, g * 128:(g + 1) * 128], xs[:, 4 + g, ck], self.ident_b[:], ("xbcT", "ident_b"))
        self.cp("act", self.Btm[:], pb[:, 0:256].rearrange("p (g n) -> p g n", g=2), (psk(b),), ("Btm",))
        yield
        b = self.bank()
        for h in range(8):
            o = self.ps[b][:, h * 64:(h + 1) * 64]
            self.mm(b, o, self.MT[:, h, :], self.xdt[:, h, :], True, False, ("MT", "xdt"))
            self.mm(b, o, self.Cs[:, h, :], self.Sbf[:, h, :], False, True, ("Cs", "Sbf"))
        self.tt("dve", self.y1[:], self.ps[b][:], self.xsd[:].rearrange("p h d -> p (h d)"), ALU.add, (psk(b), "xsd"), ("vg",))
        self.tt("dve", self.y1[:], self.y1[:], self.zs[:, c, :], ALU.mult, ("vg", "zs"), ("vg",))
        self.act(self.junk[:, 0:512], self.y1[:], AF.Square, ("vg",), ("junk",))
        self.red(self.s2[:, 0:2], self.junk[:, 0:512].rearrange("p (g d) -> p g d", g=2), ("junk",), ("s2",))
        self.rsqrt(self.s2[:, 0:2], self.s2[:, 0:2], 1.0 / 256, ("s2",), ("s2",))
        for g in range(2):
            gs = slice(g * 256, (g + 1) * 256)
            self.stt("dve", self.yctm[:, gs], self.y1[:, gs], self.s2[:, g:g + 1], self.gsn[:, gs], ALU.mult, ALU.mult,
                     ("vg", "s2", "gsn"), ("yctm",))
        yield
        b = self.bank()
        pb = self.psb(b)
        for i in range(4):
            self.tr(b, pb[:, i * 128:(i + 1) * 128], self.yctm[:, i * 128:(i + 1) * 128], self.ident_b[:], ("yctm", "ident_b"))
        self.cp("act", self.ycT[:, :, ck], pb[:, 0:512].rearrange("p (i t) -> p i t", i=4), (psk(b),), ("ycT",))
        yield
        b = self.bank()
        for h in range(8):
            self.mm(b, self.ps[b][:, h * 64:(h + 1) * 64], self.Btm[:, h // 4, :], self.xdd[:, h, :], True, True, ("Btm", "xdd"))
        self.tt("dve", self.Sst[:], self.Sst[:], self.etot8[:, :, None].broadcast_to([128, 8, 64]), ALU.mult,
                ("Sst", "etot8"), ("Sst",))
        self.tt("dve", self.Sst[:], self.Sst[:], self.ps[b][:].rearrange("p (h d) -> p h d", h=8), ALU.add,
                ("Sst", psk(b)), ("Sst",))
        self.cp("dve", self.Sbf[:], self.Sst[:], ("Sst",), ("Sbf",))
        yield

    def mla_chunk(self, c, cg, ck, wq, wkv):
        Wuq, Wuqk = wq
        Wukv, Wukvk = wkv
        b = self.bank()
        for i in range(3):
            self.mm(b, self.ps[b][:, 0:1], self.sqq[:, i, ck], self.ones_b[:, 0:1], i == 0, i == 2, ("sqq", "ones_b"))
        for i in range(2):
            self.mm(b, self.ps[b][:, 1:2], self.sqq[:, 3 + i, ck], self.ones_b[:, 0:1], i == 0, i == 1, ("sqq", "ones_b"))
        self.rsqrt(self.rq2[:, 0:1], self.ps[b][:, 0:1], 1.0 / 384, (psk(b),), ("rq2",))
        self.rsqrt(self.rq2[:, 1:2], self.ps[b][:, 1:2], 1.0 / 256, (psk(b),), ("rq2",))
        for half in range(2):
            b = self.bank()
            for i in range(3):
                self.mm(b, self.ps[b][:, 0:384], self.cqnT[:, i, ck], Wuq[:, i, half * 384:(half + 1) * 384], i == 0,
                        i == 2, ("cqnT", Wuqk))
            self.ts("dve", self.qtm[:, half * 4:half * 4 + 4, :], self.ps[b][:, 0:384].rearrange("p (h d) -> p h d", h=4),
                    self.rq2[:, 0:1], None, ALU.mult, None, (psk(b), "rq2"), ("qtm",))
        yield from self.head_norm_rope(cg, self.gqb, "gqb", self.qT, "qT", c * 128)
        yield
        for half in range(2):
            b = self.bank()
            for i in range(2):
                self.mm(b, self.ps[b][:], self.ckvnT[:, i, ck], Wukv[:, i, half * 512:(half + 1) * 512], i == 0, i == 1,
                        ("ckvnT", Wukvk))
            p3 = self.ps[b][:].rearrange("p (h d) -> p h d", h=4)
            hs = slice(half * 4, half * 4 + 4)
            self.ts("dve", self.vst[:, hs, 0:64], p3[:, :, 64:128], self.rq2[:, 1:2], None, ALU.mult, None,
                    (psk(b), "rq2"), ("vst",))
            self.ts("dve", self.qtm[:, hs, 0:64], p3[:, :, 0:64], self.rq2[:, 1:2], None, ALU.mult, None,
                    (psk(b), "rq2"), ("qtm",))
        self.cp("dve", self.qtm[:, :, 64:96], self.krdt[:, c, 0:32][:, None, :].broadcast_to([128, 8, 32]), ("krdt",), ("qtm",))
        self.dma("sp", self.vscr[:, :, cg, :].rearrange("h p d -> p h d"), self.vst[:], ("vst", "vst1"), (("vscr", cg),))
        yield from self.head_norm_rope(cg, self.gkb, "gkb", self.kst, "kst", 0)
        self.dma("sp", self.kscr[:, :, cg * 128:(cg + 1) * 128].rearrange("h d t -> d h t"), self.kst[0:96, :, :], ("kst",),
                 (("kscr", cg),))

    def attention(self, I):
        scale = 96.0 ** -0.5
        nj = NC * I + NC
        st = {}

        def prep(h):
            kh = self.kh[self.kv_i % 2]
            kk = "kh%d" % (self.kv_i % 2)
            self.kv_i += 1
            vh, vk = self.vh[0], "vh0"
            skeys = tuple(("kscr", j) for j in range(nj))
            vkeys = tuple(("vscr", j) for j in range(nj))
            self.dma("sp", kh[0:96, 0:nj * 128], self.kscr[h, :, 0:nj * 128], skeys, (kk,))
            st[h] = [kh, kk, vh, vk, None]
            return vkeys

        def load_v(h, vkeys):
            self.dma("sp", st[h][2][:, 0:nj, :], self.vscr[h, :, 0:nj, :], vkeys, (st[h][3],))

        def A(h, j):
            kh, kk = st[h][0], st[h][1]
            c0 = max(0, j - NC * I)
            bs = self.bank()
            self.mm(bs, self.ps[bs][:, c0 * 128:TS], kh[0:96, j * 128:(j + 1) * 128], self.qT[0:96, h, c0 * 128:TS],
                    True, True, (kk, "qT"))
            pT = self.pT[self.pt_i % 4]
            pk = "pT%d" % (self.pt_i % 4)
            self.pt_i += 1
            self.act(pT[:, c0 * 128:TS], self.ps[bs][:, c0 * 128:TS], AF.Exp, (psk(bs), "negC"), (pk,),
                     bias=self.negC[:], scale=scale)
            if j >= NC * I:
                self.tt("dve", pT[:, c0 * 128:(c0 + 1) * 128], pT[:, c0 * 128:(c0 + 1) * 128], self.U_b[:], ALU.mult,
                        (pk, "U_b"), (pk,))
            return pT, pk, c0

        def B(h, j, pT, pk, c0):
            vh, vk = st[h][2], st[h][3]
            if j == 0:
                st[h][4] = self.bank()
                self.reserved.add(st[h][4])
            bacc = st[h][4]
            for c in range(c0, NC):
                self.mm(bacc, self.ps[bacc][:, c * 128:c * 128 + 65], pT[:, c * 128:(c + 1) * 128], vh[:, j, :],
                        (j == 0 and c == 0), (j == NC * I + c), (pk, vk), skip=True)
            if j == nj - 1:
                a3 = self.ps[bacc][:].rearrange("p (c d) -> p c d", c=4)
                self.P.add("dve", lambda e, a3=a3: e.reciprocal(out=self.rden[:, 0:NC], in_=a3[:, 0:NC, 64]), (psk(bacc),),
                           ("rden",))
                self.tt("dve", self.ybtm[:, :, h * 64:(h + 1) * 64], a3[:, 0:NC, 0:64],
                        self.rden[:, 0:NC, None].broadcast_to([128, NC, 64]), ALU.mult, (psk(bacc), "rden"), ("ybtm",))
                self.reserved.discard(bacc)

        LA = 3
        pend = []
        vload = {}
        for h in range(8):
            vkeys = prep(h)
            for j in range(nj):
                cur = A(h, j)
                pend.append((h, j) + cur)
                if j == 0:
                    vload[h] = vkeys
                while len(pend) > LA:
                    p = pend.pop(0)
                    if p[1] == 0:
                        load_v(p[0], vload.pop(p[0]))
                    B(*p)
                yield
        while pend:
            p = pend.pop(0)
            if p[1] == 0:
                load_v(p[0], vload.pop(p[0]))
            B(*p)
        yield
        for c in range(NC):
            b = self.bank()
            pb = self.psb(b)
            for i in range(4):
                self.tr(b, pb[:, i * 128:(i + 1) * 128], self.ybtm[:, c, i * 128:(i + 1) * 128], self.ident_b[:],
                        ("ybtm", "ident_b"))
            self.cp("act", self.ybT[:, :, c * 128:(c + 1) * 128], pb[:, 0:512].rearrange("p (i t) -> p i t", i=4),
                    (psk(b),), ("ybT",))

    def merge(self, l, x, xk):
        ys = ((self.yaT, "yaT"), (self.ybT, "ybT"), (self.ycT, "ycT"))
        for i in range(3):
            if not (self.brmask >> i) & 1:
                self.P.add("dve", lambda e, t=ys[i][0]: e.memset(t[:], 0.0), (), (ys[i][1],))
        for half in range(2):
            for i in range(3):
                wg, wgk = self.load_blk(self.scr_in[l][7 + 2 * i + half], (("sin", l, 7 + 2 * i + half),))
                wb, wbk = self.load_blk(self.scr_br[l][i][:, :, half * 512:(half + 1) * 512], (("sbr", l, i),), nk=4)
                for m in range(4):
                    bg = self.fm_chunk(wg, wgk, m * 128)
                    sg = self.sig[self.sig_i % 2]
                    sgk = "sig%d" % (self.sig_i % 2)
                    self.sig_i += 1
                    self.act(sg[:], self.pt(bg), AF.Sigmoid, (psk(bg),), (sgk,))
                    bb = self.fm_chunk(wb, wbk, m * 128, nk=4, rhs=ys[i][0], rkey=ys[i][1])
                    if i == 0:
                        self.tt("dve", self.mgacc[:, m, :], sg[:], self.pt(bb), ALU.mult, (sgk, psk(bb)), ("mgacc",))
                    else:
                        self.tt("dve", sg[:], sg[:], self.pt(bb), ALU.mult, (sgk, psk(bb)), (sgk,))
                        if i == 1:
                            self.tt("dve", self.mgacc[:, m, :], self.mgacc[:, m, :], sg[:], ALU.add, (sgk, "mgacc"), ("mgacc",))
                        else:
                            self.tt("dve", self.mgT[:, half * 4 + m, :], self.mgacc[:, m, :], sg[:], ALU.add, (sgk, "mgacc"),
                                    ("mgT",))
                    yield
        for half in range(2):
            wo, wok = self.load_blk(self.scr_o[l][half], (("so", l, half),))
            for m in range(4):
                b = self.fm_chunk(wo, wok, m * 128, rhs=self.mgT, rkey="mgT")
                k = half * 4 + m
                self.tt("dve", x[:, k, :], x[:, k, :], self.pt(b), ALU.add, (xk, psk(b)), (xk,))
            yield

    def build(self):
        assert NC == 2
        self.FW = getattr(self, "FW", 1.0)
        self.declare()
        self.alloc()
        self.consts()
        finals = []
        L, NT = self.L, self.NT
        convs = [self.conv_list(l) for l in range(L)]
        for fn in convs[0]:
            fn()
        slots = [(l, I) for l in range(L) for I in range(NT)]
        nslot = len(slots)

        def xt(s):
            return self.xTs[s % 2], "xT%d" % (s % 2)

        def fstream(s):
            if s - 1 >= 0:
                l, I = slots[s - 1]
                x, xk = xt(s - 1)
                if self.do_ffn:
                    yield from self.ffn(l, 2, x, xk)
                finals.extend(self.store_x(l, I, x, xk))
                yield
            if s + 1 < nslot:
                l, I = slots[s + 1]
                x, xk = xt(s + 1)
                if I == 0:
                    self.load_gains(l)
                self.load_x(l, I, x, xk)
                yield
                if self.do_ffn:
                    yield from self.ffn(l, 1, x, xk)

        def run(*gens, weights=None):
            gens = [g for g in gens if g is not None]
            alive = [True] * len(gens)
            acc = [0.0] * len(gens)
            w = weights or [1.0] * len(gens)
            while any(alive):
                for i, g in enumerate(gens):
                    if not alive[i]:
                        continue
                    acc[i] += w[i]
                    while acc[i] >= 1.0 and alive[i]:
                        acc[i] -= 1.0
                        try:
                            k = next(g)
                            if k:
                                acc[i] -= float(k)
                        except StopIteration:
                            alive[i] = False

        self.load_gains(0)
        self.load_x(0, 0, *xt(0))
        if self.do_ffn:
            run(self.ffn(0, 1, *xt(0)))
        for s, (l, I) in enumerate(slots):
            if I == 0 and self.do_mix:
                self.load_params(l)
            nxt = convs[l + 1] if l + 1 < L else []
            per = (len(nxt) + NT - 1) // NT if nxt else 0
            for fn in nxt[I * per:(I + 1) * per]:
                fn()
            x, xk = xt(s)
            n_f = 2 * (1 + NJ + 2 * NJJ + 2) + 2
            if self.do_mix:
                n_m = 24 + 2 * (8 * (NC * I + NC) + 8) + 30
                run(self.mixer(l, I, x, xk), fstream(s), weights=[1.0, min(1.0, self.FW * n_f / n_m)])
            else:
                run(fstream(s))
        run(fstream(nslot))
        self.P.emit(finals)
        self.es.close()
        return self.nc


_CACHE = {}


def kernel(**inputs):
    x = np.asarray(inputs["x"])
    B, S, _ = x.shape
    L = int(np.asarray(inputs["ffn1_norm"]).shape[0])
    key = (S, L)
    if key not in _CACHE:
        _CACHE[key] = Builder(S, L).build()
    nc = _CACHE[key]
    shared = {k: np.ascontiguousarray(np.asarray(v)) for k, v in inputs.items() if k not in ("x", "positions")}
    pos = np.asarray(inputs["positions"]).astype(np.int32)
    in_maps = []
    for b in range(B):
        m = dict(shared)
        m["x"] = np.ascontiguousarray(x[b])
        m["positions"] = np.ascontiguousarray(pos[b])
        in_maps.append(m)
    res = run_bass_kernel_spmd(nc, in_maps, core_ids=list(range(B)))
    return np.stack([np.asarray(r["out"]) for r in res.results], axis=0).astype(np.float32)
```
